# Optimizing a Trainium2 kernel written in Bass

```python
import jax, jax.numpy as jnp
from jax import lax
import numpy as np

D_MODEL = 1024
BATCH = 4
SEQ = 8192
DEPTH = 4

GRID_W = 64
CTX_LEN = 256
HG_HEADS = 4
HG_DK = 128
HG_DV = 128
GLA_HEADS = 4
GLA_DK = 64
GLA_DV = 128
GLA_GATE_RANK = 16
GLA_GATE_NORM = 16.0
HG_KW = HG_HEADS * HG_DK
HG_VW = HG_HEADS * HG_DV
GLA_KW = GLA_HEADS * GLA_DK
GLA_VW = GLA_HEADS * GLA_DV
D_MIX = HG_VW + GLA_VW
CHUNK = 64
PEER_HEADS = 8
PEER_NKEYS = 128
PEER_EXPERTS = PEER_NKEYS * PEER_NKEYS
PEER_DQ = 256
PEER_TOPK = 16
PEER_BLOCK = 128
ALPHA = (2.0 * DEPTH) ** 0.25
BETA = (8.0 * DEPTH) ** -0.25
EPS = 1e-6
LB_FLOOR = 1e-30
IN_SPLITS = (HG_KW, HG_KW, HG_KW, HG_VW, HG_VW, GLA_KW, GLA_KW, GLA_VW, GLA_VW, GLA_GATE_RANK, GLA_GATE_RANK)
D_IN = sum(IN_SPLITS)
IN_SPLIT_POINTS = [int(p) for p in np.cumsum(IN_SPLITS)[:-1]]

kernel_name = 'hybrid_hgrn2_gla_peer_dit'


def layer_norm(x, gamma=None, beta=None):
    xf = x.astype(jnp.float32)
    mu = jnp.mean(xf, axis=-1, keepdims=True)
    var = jnp.mean(jnp.square(xf - mu), axis=-1, keepdims=True)
    y = (xf - mu) * lax.rsqrt(var + EPS)
    if gamma is not None:
        y = y * gamma.astype(jnp.float32) + beta.astype(jnp.float32)
    return y.astype(x.dtype)


def modulate(x, shift, scale):
    return layer_norm(x) * (1.0 + scale) + shift


def heads(t, n):
    B, L, _ = t.shape
    return t.reshape(B, L, n, -1).transpose(0, 2, 1, 3).astype(jnp.float32)


def head_rms_merge(o, gain):
    o = o * lax.rsqrt(jnp.mean(o * o, axis=-1, keepdims=True) + EPS) * gain.astype(jnp.float32)
    B, n, L, d = o.shape
    return o.transpose(0, 2, 1, 3).reshape(B, L, n * d)


def to_col_major(t, rows):
    B, L, C = t.shape
    return t.reshape(B, rows, GRID_W, C).transpose(0, 2, 1, 3).reshape(B, L, C)


def from_col_major(t, rows):
    B, L, C = t.shape
    return t.reshape(B, GRID_W, rows, C).transpose(0, 2, 1, 3).reshape(B, L, C)


def chunk_scan(q, k, v, log_a, s0):
    B, H, L, dk = q.shape
    dv = v.shape[-1]
    n = L // CHUNK

    def split(t):
        return jnp.moveaxis(t.reshape(B, H, n, CHUNK, t.shape[-1]), 2, 0)

    incl = jnp.tril(jnp.ones((CHUNK, CHUNK), dtype=bool))[:, :, None]

    def step(s, inp):
        qc, kc, vc, ac = inp
        b = jnp.cumsum(ac, axis=-2)
        rel = b[..., :, None, :] - b[..., None, :, :]
        decay = jnp.where(incl, jnp.exp(jnp.minimum(rel, 0.0)), 0.0)
        scores = jnp.einsum('bhtk,bhsk,bhtsk->bhts', qc, kc, decay)
        o = jnp.einsum('bhts,bhsv->bhtv', scores, vc) + jnp.einsum('bhtk,bhkv->bhtv', qc * jnp.exp(b), s)
        b_end = b[..., -1:, :]
        s_new = jnp.exp(b_end[..., 0, :])[..., None] * s + jnp.einsum('bhsk,bhsv->bhkv', kc * jnp.exp(b_end - b), vc)
        return s_new, o

    s_end, o = lax.scan(step, s0, (split(q), split(k), split(v), split(log_a)))
    return jnp.moveaxis(o, 0, 2).reshape(B, H, L, dv), s_end


def scan_two_way(ctx_parts, lat_parts):
    qc, vc, kcf, acf, kcb, acb = ctx_parts
    ql, vl, klf, alf, klb, alb = lat_parts
    B, H, _, dk = qc.shape
    s0 = jnp.zeros((B, H, dk, vc.shape[-1]), jnp.float32)
    flip = lambda t: jnp.flip(t, axis=2)
    oc_f, sc_f = chunk_scan(qc, kcf, vc, acf, s0)
    ol_f, _ = chunk_scan(ql, klf, vl, alf, sc_f)
    oc_b, sc_b = chunk_scan(flip(qc), flip(kcb), flip(vc), flip(acb), s0)
    ol_b, _ = chunk_scan(flip(ql), flip(klb), flip(vl), flip(alb), sc_b)
    return oc_f + flip(oc_b), ol_f + flip(ol_b)


def hgrn2_inputs(hq, hff, hfb, hi, lb_f, lb_b):
    def gate(z, lb):
        z = z.astype(jnp.float32)
        log_lb = jnp.log(jnp.maximum(lb, LB_FLOOR))
        log_f = jnp.logaddexp(log_lb, jnp.log1p(-lb) + jax.nn.log_sigmoid(z))
        k = (1.0 - lb) * jax.nn.sigmoid(-z)
        return heads(k, HG_HEADS), heads(log_f, HG_HEADS)
    kf, af = gate(hff, lb_f)
    kb, ab = gate(hfb, lb_b)
    return (heads(hq, HG_HEADS), heads(hi, HG_HEADS), kf, af, kb, ab)


def gla_inputs(gq, gk, gv, glf, glb, w_gk2, b_gk):
    def gate(r, w, b):
        la = jax.nn.log_sigmoid((r @ w + b).astype(jnp.float32)) / GLA_GATE_NORM
        return heads(la, GLA_HEADS)
    k = heads(gk, GLA_HEADS)
    q = heads(gq, GLA_HEADS) * (GLA_DK ** -0.5)
    return (q, heads(gv, GLA_HEADS), k, gate(glf, w_gk2[0], b_gk[0]), k, gate(glb, w_gk2[1], b_gk[1]))


def token_mixer(u_c, u_l, rows, lb_f, lb_b, w_gk2, b_gk, hg_gain, gla_gain):
    pc = jnp.split(u_c, IN_SPLIT_POINTS, axis=-1)
    pl = jnp.split(u_l, IN_SPLIT_POINTS, axis=-1)
    a_c, a_l = scan_two_way(hgrn2_inputs(pc[0], pc[1], pc[2], pc[3], lb_f, lb_b),
                            hgrn2_inputs(pl[0], pl[1], pl[2], pl[3], lb_f, lb_b))
    out_a_c = head_rms_merge(a_c, hg_gain).astype(u_c.dtype) * jax.nn.silu(pc[4])
    out_a_l = head_rms_merge(a_l, hg_gain).astype(u_l.dtype) * jax.nn.silu(pl[4])
    gl = [to_col_major(pl[j], rows) for j in (5, 6, 7, 9, 10)]
    b_c, b_l = scan_two_way(gla_inputs(pc[5], pc[6], pc[7], pc[9], pc[10], w_gk2, b_gk),
                            gla_inputs(gl[0], gl[1], gl[2], gl[3], gl[4], w_gk2, b_gk))
    out_b_c = head_rms_merge(b_c, gla_gain).astype(u_c.dtype) * jax.nn.silu(pc[8])
    out_b_l = from_col_major(head_rms_merge(b_l, gla_gain), rows).astype(u_l.dtype) * jax.nn.silu(pl[8])
    return (jnp.concatenate([out_a_c, out_b_c], axis=-1), jnp.concatenate([out_a_l, out_b_l], axis=-1))


def peer(h, w_query, sub_keys, expert_u, expert_v):
    B, L, D = h.shape
    tok = h.reshape(-1, PEER_BLOCK, D)
    k1 = sub_keys[0].astype(jnp.float32)
    k2 = sub_keys[1].astype(jnp.float32)

    def block(t):
        q = (t @ w_query).reshape(PEER_BLOCK, PEER_HEADS, 2, PEER_DQ // 2).astype(jnp.float32)
        s1 = jnp.einsum('thd,nd->thn', q[:, :, 0], k1)
        s2 = jnp.einsum('thd,nd->thn', q[:, :, 1], k2)
        v1, i1 = lax.top_k(s1, PEER_TOPK)
        v2, i2 = lax.top_k(s2, PEER_TOPK)
        cand_s = (v1[..., :, None] + v2[..., None, :]).reshape(PEER_BLOCK, PEER_HEADS, PEER_TOPK * PEER_TOPK)
        cand_i = (i1[..., :, None] * PEER_NKEYS + i2[..., None, :]).reshape(PEER_BLOCK, PEER_HEADS, PEER_TOPK * PEER_TOPK)
        top_s, pos = lax.top_k(cand_s, PEER_TOPK)
        idx = jnp.take_along_axis(cand_i, pos, axis=-1)
        g = jax.nn.softmax(top_s, axis=-1)
        act = jax.nn.gelu(jnp.einsum('thkd,td->thk', expert_u[idx], t).astype(jnp.float32))
        return jnp.einsum('thk,thkd->td', (g * act).astype(t.dtype), expert_v[idx])

    return lax.map(block, tok).reshape(B, L, D)


def setup_inputs(seed: int = 0) -> dict:
    key = jax.random.key(seed)
    ks = jax.random.split(key, 20)
    f32 = jnp.float32
    nrm = lambda k, shape, s: jax.random.normal(k, shape, f32) * s
    return {
        'x': nrm(ks[0], (BATCH, SEQ, D_MODEL), 1.0),
        'c': nrm(ks[1], (BATCH, D_MODEL), 1.0),
        'ctx': nrm(ks[2], (BATCH, CTX_LEN, D_MODEL), 1.0),
        'c_ctx': nrm(ks[3], (D_MODEL,), 1.0),
        'w_ada': nrm(ks[4], (DEPTH, D_MODEL, 6 * D_MODEL), 0.5 * D_MODEL ** -0.5),
        'b_ada': nrm(ks[5], (DEPTH, 6 * D_MODEL), 0.01),
        'w_in': nrm(ks[6], (DEPTH, D_MODEL, D_IN), D_MODEL ** -0.5),
        'w_gk2': nrm(ks[7], (DEPTH, 2, GLA_GATE_RANK, GLA_KW), GLA_GATE_RANK ** -0.5),
        'b_gk': nrm(ks[8], (DEPTH, 2, GLA_KW), 0.01),
        'hg_lower_bounds': nrm(ks[9], (2, DEPTH, HG_KW), 0.1),
        'hg_norm': 1.0 + nrm(ks[10], (DEPTH, HG_DV), 0.01),
        'gla_norm': 1.0 + nrm(ks[11], (DEPTH, GLA_DV), 0.01),
        'w_out': nrm(ks[12], (DEPTH, D_MIX, D_MODEL), BETA * D_MIX ** -0.5),
        'ln_gamma': 1.0 + nrm(ks[13], (DEPTH, 2, D_MODEL), 0.01),
        'ln_beta': nrm(ks[14], (DEPTH, 2, D_MODEL), 0.01),
        'peer_w_query': nrm(ks[15], (DEPTH, D_MODEL, PEER_HEADS * PEER_DQ), D_MODEL ** -0.5),
        'peer_sub_keys': nrm(ks[16], (DEPTH, 2, PEER_NKEYS, PEER_DQ // 2), (PEER_DQ // 2) ** -0.5),
        'peer_u': nrm(ks[17], (DEPTH, PEER_EXPERTS, D_MODEL), D_MODEL ** -0.5),
        'peer_v': nrm(ks[18], (DEPTH, PEER_EXPERTS, D_MODEL), BETA),
    }


def reference(x, c, ctx, c_ctx, w_ada, b_ada, w_in, w_gk2, b_gk, hg_lower_bounds, hg_norm, gla_norm,
              w_out, ln_gamma, ln_beta, peer_w_query, peer_sub_keys, peer_u, peer_v):
    rows = x.shape[1] // GRID_W
    sm = jax.nn.softmax(hg_lower_bounds.astype(jnp.float32), axis=1)
    lb_all = jnp.clip(jnp.cumsum(sm, axis=1) - sm[:, :1], 0.0, 1.0 - 1e-6)
    xc = ctx
    for i in range(DEPTH):
        last = i == DEPTH - 1
        m_l = jax.nn.silu(c) @ w_ada[i] + b_ada[i]
        m_c = jax.nn.silu(c_ctx) @ w_ada[i] + b_ada[i]
        sh1_l, sc1_l, g1_l, sh2_l, sc2_l, g2_l = [t[:, None, :] for t in jnp.split(m_l, 6, axis=-1)]
        sh1_c, sc1_c, g1_c, sh2_c, sc2_c, g2_c = jnp.split(m_c, 6, axis=-1)
        u_l = modulate(x, sh1_l, sc1_l) @ w_in[i]
        u_c = modulate(xc, sh1_c, sc1_c) @ w_in[i]
        mix_c, mix_l = token_mixer(u_c, u_l, rows, lb_all[0, i], lb_all[1, i], w_gk2[i], b_gk[i],
                                   hg_norm[i], gla_norm[i])
        x = layer_norm(ALPHA * x + g1_l * (mix_l @ w_out[i]), ln_gamma[i, 0], ln_beta[i, 0])
        if not last:
            xc = layer_norm(ALPHA * xc + g1_c * (mix_c @ w_out[i]), ln_gamma[i, 0], ln_beta[i, 0])
        y_l = peer(modulate(x, sh2_l, sc2_l), peer_w_query[i], peer_sub_keys[i], peer_u[i], peer_v[i])
        x = layer_norm(ALPHA * x + g2_l * y_l, ln_gamma[i, 1], ln_beta[i, 1])
        if not last:
            y_c = peer(modulate(xc, sh2_c, sc2_c), peer_w_query[i], peer_sub_keys[i], peer_u[i], peer_v[i])
            xc = layer_norm(ALPHA * xc + g2_c * y_c, ln_gamma[i, 1], ln_beta[i, 1])
    return x
```

```python
import numpy as np
from contextlib import ExitStack
import concourse.bass as bass
import concourse.mybir as mybir
from concourse.bass_utils import run_bass_kernel_spmd

F32 = mybir.dt.float32
BF16 = mybir.dt.bfloat16
U32 = mybir.dt.uint32
I32 = mybir.dt.int32
U8 = mybir.dt.uint8
AF = mybir.ActivationFunctionType
ALU = mybir.AluOpType
AX = mybir.AxisListType

D = 1024
SEQ = 8192
CTX = 256
T = SEQ + CTX
NT = T // 128
DEPTH = 4
GRID_W = 64
ROWS = SEQ // GRID_W
DIN = 4128
NEXP = 16384
EPS = 1e-6
ALPHA = (2.0 * DEPTH) ** 0.25
C = 64

O_HQ, O_HFF, O_HFB, O_HI, O_HG = 0, 512, 1024, 1536, 2048
O_GQ, O_GK, O_GV, O_GG, O_GLF, O_GLB = 2560, 2816, 3072, 3584, 4096, 4112


class Buf:
    def __init__(self, t=None, name="", excl=False):
        self.t = t
        self.name = name
        self.w = None
        self.r = {}
        self.excl = excl

    def __getitem__(self, k):
        return self.t[k]


class Eng:
    def __init__(self, h, sem, name):
        self.h = h
        self.sem = sem
        self.name = name
        self.cnt = 0
        self.known = {}
        self.dsems = []
        self.dpos = 0


class Prog:
    def __init__(self, nc, st):
        self.nc = nc
        self.st = st
        mk = lambda h, n: Eng(h, st.enter_context(nc.semaphore("p_" + n)), n)
        self.pe = mk(nc.tensor, "pe")
        self.dve = mk(nc.vector, "dve")
        self.act = mk(nc.scalar, "act")
        self.pool = mk(nc.gpsimd, "pool")
        self.sp = mk(nc.sync, "sp")
        self.engs = [self.pe, self.dve, self.act, self.pool, self.sp]
        self.semcnt = {}
        for e, n in ((self.sp, 20), (self.pool, 20), (self.act, 4)):
            for i in range(n):
                s = st.enter_context(nc.semaphore(f"d_{e.name}{i}"))
                e.dsems.append(s)
                self.semcnt[s] = 0
        self.nbuf = 0

    def sb(self, shape, dtype, name=None, st=None):
        self.nbuf += 1
        name = f"{name or 'sb'}_{self.nbuf}"
        t = (st or self.st).enter_context(self.nc.sbuf_tensor(name, list(shape), dtype))
        return Buf(t, name)

    def ps(self, shape, dtype, name=None, st=None):
        self.nbuf += 1
        name = f"{name or 'ps'}_{self.nbuf}"
        t = (st or self.st).enter_context(self.nc.psum_tensor(name, list(shape), dtype))
        return Buf(t, name, excl=True)

    def dram(self, name, shape, dtype, kind="Internal"):
        t = self.nc.dram_tensor(name, list(shape), dtype, kind=kind)
        return Buf(t.ap(), name)

    def _deps(self, eng, reads, writes):
        deps = []
        for b in reads:
            if b.w is not None:
                deps.append(b.w)
            if b.excl:
                for s, v in b.r.items():
                    if s is not eng.sem:
                        deps.append((s, v))
        for b in writes:
            if b.w is not None and b.w[0] is not eng.sem:
                deps.append(b.w)
            for s, v in b.r.items():
                if s is not eng.sem:
                    deps.append((s, v))
        return deps

    def _wait(self, eng, deps):
        for s, v in deps:
            if eng.known.get(s, 0) < v:
                eng.h.wait_ge(s, v)
                eng.known[s] = v

    def _mark(self, tok, reads, writes):
        for b in reads:
            if b.r.get(tok[0], 0) < tok[1]:
                b.r[tok[0]] = tok[1]
        for b in writes:
            b.w = tok
            b.r = {}

    def op(self, eng, fn, reads=(), writes=()):
        self._wait(eng, self._deps(eng, reads, writes))
        inst = fn(eng.h)
        eng.cnt += 1
        inst.then_inc(eng.sem, 1)
        self._mark((eng.sem, eng.cnt), reads, writes)

    def dma(self, q, out, in_, reads=(), writes=(), indirect=None, **kw):
        s = q.dsems[q.dpos % len(q.dsems)]
        q.dpos += 1
        deps = self._deps(q, reads, writes)
        if self.semcnt[s] > 0:
            deps.append((s, self.semcnt[s]))
        self._wait(q, deps)
        if indirect is not None:
            inst = q.h.indirect_dma_start(out=out, out_offset=None, in_=in_, in_offset=indirect, **kw)
        else:
            inst = q.h.dma_start(out=out, in_=in_, **kw)
        self.semcnt[s] += 16
        inst.then_inc(s, 16)
        self._mark((s, self.semcnt[s]), reads, writes)

    def barrier(self):
        toks = [(e.sem, e.cnt) for e in self.engs if e.cnt > 0]
        toks += [(s, v) for s, v in self.semcnt.items() if v > 0]
        for e in self.engs:
            self._wait(e, [t for t in toks if t[0] is not e.sem or True])

    def V(self, fn, reads=(), writes=()):
        self.op(self.dve, fn, reads, writes)

    def A(self, fn, reads=(), writes=()):
        self.op(self.act, fn, reads, writes)

    def G(self, fn, reads=(), writes=()):
        self.op(self.pool, fn, reads, writes)

    def P(self, fn, reads=(), writes=()):
        self.op(self.pe, fn, reads, writes)


class NS:
    pass


def build_nc(mode="full", nlayers=DEPTH):
    nc = bass.Bass("TRN2", target_bir_lowering=False)
    with ExitStack() as st:
        _emit(nc, st, mode, nlayers)
    return nc


def _emit(nc, st, mode, nlayers):
    pg = Prog(nc, st)
    S = NS()
    S.nc, S.pg, S.mode = nc, pg, mode
    din = lambda n, s: Buf(nc.dram_tensor(n, list(s), F32, kind="ExternalInput").ap(), n)
    S.x_in = din("x", (SEQ, D))
    S.ctx_in = din("ctx", (CTX, D))
    S.c_in = din("c", (1, D))
    S.cctx_in = din("c_ctx", (1, D))
    S.w_ada = din("w_ada", (DEPTH, D, 6 * D))
    S.b_ada = din("b_ada", (DEPTH, 6 * D))
    S.w_in = din("w_in", (DEPTH, D, DIN))
    S.w_gk2 = din("w_gk2", (DEPTH, 2, 16, 256))
    S.b_gk = din("b_gk", (DEPTH, 2, 256))
    S.hg_lb = din("hg_lower_bounds", (2, DEPTH, 512))
    S.hg_norm = din("hg_norm", (DEPTH, 128))
    S.gla_norm = din("gla_norm", (DEPTH, 128))
    S.w_out = din("w_out", (DEPTH, D, D))
    S.ln_gamma = din("ln_gamma", (DEPTH, 2, D))
    S.ln_beta = din("ln_beta", (DEPTH, 2, D))
    S.pwq = din("peer_w_query", (DEPTH, D, 2048))
    S.psk = din("peer_sub_keys", (DEPTH, 2, 128, 128))
    if mode in ("full", "C", "chain"):
        S.peer_u = din("peer_u", (DEPTH, NEXP, D))
        S.peer_v = din("peer_v", (DEPTH, NEXP, D))

    S.MOD = pg.dram("s_mod", (DEPTH, 2, 6 * D), F32)
    S.OML = pg.dram("s_oml", (2, DEPTH, 512), F32)
    ukind = {"A": "ExternalOutput", "B": "ExternalInput", "C": "ExternalInput"}.get(mode, "Internal")
    S.U = pg.dram("s_u", (T, DIN), F32, kind=ukind)
    okind = {"B": "ExternalOutput", "C": "ExternalInput"}.get(mode, "Internal")
    S.OF = pg.dram("s_of", (T, D), F32, kind=okind)
    S.OB = pg.dram("s_ob", (T, D), F32, kind=okind)
    S.X = pg.dram("s_x", (T, D), F32, kind=("ExternalOutput" if mode in ("C", "chain") else "Internal"))
    if mode == "full":
        S.OUT = pg.dram("out", (SEQ, D), F32, kind="ExternalOutput")

    S.ident = pg.sb((128, 128), F32, "ident")
    S.ones = pg.sb((128, 128), F32, "ones")
    pg.G(lambda e: e.memset(S.ones[:], 1.0), writes=[S.ones])
    pg.G(lambda e: e.affine_select(out=S.ident[:], in_=S.ones[:], pattern=[[1, 128]], compare_op=ALU.is_equal,
                                   fill=0.0, base=0, channel_multiplier=-1), reads=[S.ones], writes=[S.ident])

    if mode in ("full", "A", "C", "chain"):
        prologue_mod(S)
    if mode in ("full", "B", "chain"):
        prologue_lb(S)
        scan_consts(S)
    pg.barrier()
    for li in range(nlayers):
        if mode in ("full", "A", "chain"):
            phase_a(S, li)
            pg.barrier()
        if mode in ("full", "B", "chain"):
            phase_b(S, li)
            pg.barrier()
        if mode in ("full", "C", "chain"):
            phase_c(S, li)
            pg.barrier()
    pg.barrier()


def prologue_mod(S):
    pg = S.pg
    with ExitStack() as ps_:
        cc = pg.sb((128, 8, 2), F32, "cc", ps_)
        scc = pg.sb((128, 8, 2), F32, "scc", ps_)
        sig = pg.sb((128, 8, 2), F32, "sig", ps_)
        pg.dma(pg.sp, cc[:, :, 0], S.c_in[0, :].rearrange("(c p) -> p c", p=128), reads=[S.c_in], writes=[cc],
               allow_slow_non_contiguous=True)
        pg.dma(pg.sp, cc[:, :, 1], S.cctx_in[0, :].rearrange("(c p) -> p c", p=128), reads=[S.cctx_in], writes=[cc],
               allow_slow_non_contiguous=True)
        pg.A(lambda e: e.activation(out=sig[:], in_=cc[:], func=AF.Sigmoid), reads=[cc], writes=[sig])
        pg.V(lambda e: e.tensor_tensor(out=scc[:], in0=cc[:], in1=sig[:], op=ALU.mult), reads=[cc, sig], writes=[scc])
        wblk = [pg.sb((128, 8, 512), F32, f"wblk{i}", ps_) for i in range(2)]
        msb = pg.sb((2, 6 * D), F32, "msb", ps_)
        bsb = pg.sb((2, 6 * D), F32, "bsb", ps_)
        pm = [pg.ps((128, 512), F32, f"pm{i}", ps_) for i in range(2)]
        k = 0
        for li in range(DEPTH):
            pg.dma(pg.sp, bsb[:], S.b_ada[li, :].partition_broadcast(2), reads=[S.b_ada], writes=[bsb])
            for j in range(12):
                wb = wblk[k % 2]
                pmj = pm[k % 2]
                k += 1
                pg.dma(pg.sp, wb[:], S.w_ada[li, :, j * 512:(j + 1) * 512].rearrange("(c p) n -> p c n", p=128),
                       reads=[S.w_ada], writes=[wb])
                for ch in range(8):
                    pg.P(lambda e, ch=ch, wb=wb, pmj=pmj: e.matmul(pmj[0:2, :], lhsT=scc[:, ch, :], rhs=wb[:, ch, :],
                                                                  start=(ch == 0), stop=(ch == 7)),
                         reads=[scc, wb], writes=[pmj])
                pg.V(lambda e, j=j, pmj=pmj: e.tensor_tensor(out=msb[:, j * 512:(j + 1) * 512], in0=pmj[0:2, :],
                                                             in1=bsb[:, j * 512:(j + 1) * 512], op=ALU.add),
                     reads=[pmj, bsb], writes=[msb])
            pg.dma(pg.sp, S.MOD[li], msb[:], reads=[msb], writes=[S.MOD])
        pg.barrier()


def prologue_lb(S):
    pg = S.pg
    with ExitStack() as ps_:
        x = pg.sb((128, 2, 4, 4), F32, "lb_x", ps_)
        mx = pg.sb((128, 2, 4), F32, "lb_mx", ps_)
        sm = pg.sb((128, 2, 4), F32, "lb_sm", ps_)
        oml = pg.sb((128, 2, 4, 4), F32, "lb_oml", ps_)
        for d in range(2):
            for l in range(4):
                pg.dma(pg.sp, x[:, d, l, :], S.hg_lb[d, l, :].rearrange("(h p) -> p h", p=128), reads=[S.hg_lb], writes=[x],
                       allow_slow_non_contiguous=True)
        V = pg.V
        V(lambda e: e.tensor_tensor(out=mx[:], in0=x[:, :, 0, :], in1=x[:, :, 1, :], op=ALU.max), reads=[x], writes=[mx])
        for l in (2, 3):
            V(lambda e, l=l: e.tensor_tensor(out=mx[:], in0=mx[:], in1=x[:, :, l, :], op=ALU.max), reads=[x, mx], writes=[mx])
        for l in range(4):
            V(lambda e, l=l: e.tensor_tensor(out=x[:, :, l, :], in0=x[:, :, l, :], in1=mx[:], op=ALU.subtract),
              reads=[x, mx], writes=[x])
        pg.A(lambda e: e.activation(out=x[:], in_=x[:], func=AF.Exp), reads=[x], writes=[x])
        V(lambda e: e.tensor_tensor(out=sm[:], in0=x[:, :, 0, :], in1=x[:, :, 1, :], op=ALU.add), reads=[x], writes=[sm])
        for l in (2, 3):
            V(lambda e, l=l: e.tensor_tensor(out=sm[:], in0=sm[:], in1=x[:, :, l, :], op=ALU.add), reads=[x, sm], writes=[sm])
        V(lambda e: e.reciprocal(out=sm[:], in_=sm[:]), reads=[sm], writes=[sm])
        for l in range(4):
            V(lambda e, l=l: e.tensor_tensor(out=x[:, :, l, :], in0=x[:, :, l, :], in1=sm[:], op=ALU.mult),
              reads=[x, sm], writes=[x])
        for l in (2, 3):
            V(lambda e, l=l: e.tensor_tensor(out=x[:, :, l, :], in0=x[:, :, l, :], in1=x[:, :, l - 1, :], op=ALU.add),
              reads=[x], writes=[x])
        pg.V(lambda e: e.memset(oml[:], 1.0), writes=[oml])
        for l in (1, 2, 3):
            V(lambda e, l=l: e.tensor_scalar(out=x[:, :, l, :], in0=x[:, :, l, :], scalar1=1.0 - 1e-6, scalar2=0.0,
                                             op0=ALU.min, op1=ALU.max), reads=[x], writes=[x])
            V(lambda e, l=l: e.tensor_scalar(out=oml[:, :, l, :], in0=x[:, :, l, :], scalar1=-1.0, scalar2=1.0,
                                             op0=ALU.mult, op1=ALU.add), reads=[x], writes=[oml])
        for d in range(2):
            for l in range(4):
                pg.dma(pg.sp, S.OML[d, l, :].rearrange("(h p) -> p h", p=128), oml[:, d, l, :], reads=[oml], writes=[S.OML],
                       allow_slow_non_contiguous=True)
        pg.barrier()


def scan_consts(S):
    pg = S.pg
    G = pg.G
    ones = S.ones
    le = pg.sb((C, C), F32, "c_le")
    ge = pg.sb((C, C), F32, "c_ge")
    bmf = pg.sb((C, C + 2), F32, "c_bmf")
    bmb = pg.sb((C, C + 2), F32, "c_bmb")
    G(lambda e: e.affine_select(out=le[:], in_=ones[0:C, 0:C], pattern=[[1, C]], compare_op=ALU.is_ge, fill=0.0,
                                base=0, channel_multiplier=-1), reads=[ones], writes=[le])
    G(lambda e: e.affine_select(out=ge[:], in_=ones[0:C, 0:C], pattern=[[-1, C]], compare_op=ALU.is_ge, fill=0.0,
                                base=0, channel_multiplier=1), reads=[ones], writes=[ge])
    G(lambda e: e.affine_select(out=bmf[:], in_=ones[0:C, 0:C + 2], pattern=[[0, C + 2]], compare_op=ALU.is_ge, fill=0.0,
                                base=C // 2 - 1, channel_multiplier=-1), reads=[ones], writes=[bmf])
    G(lambda e: e.affine_select(out=bmb[:], in_=ones[0:C, 0:C + 2], pattern=[[0, C + 2]], compare_op=ALU.is_ge, fill=0.0,
                                base=-(C // 2), channel_multiplier=1), reads=[ones], writes=[bmb])
    S.tm = {}
    for key, bm, tri in (("f", bmf, le), ("b", bmb, ge)):
        for scale, tag in ((1.0, "h"), (-1.0 / 16.0, "g")):
            tm = pg.sb((C, C + 2), F32, f"c_tm{key}{tag}")
            pg.V(lambda e, tm=tm, bm=bm: e.tensor_copy(out=tm[:], in_=bm[:]), reads=[bm], writes=[tm])
            pg.V(lambda e, tm=tm, tri=tri: e.tensor_tensor(out=tm[:, 0:C], in0=tm[:, 0:C], in1=tri[:], op=ALU.subtract),
                 reads=[tm, tri], writes=[tm])
            if scale != 1.0:
                pg.V(lambda e, tm=tm, scale=scale: e.tensor_scalar(out=tm[:], in0=tm[:], scalar1=scale, scalar2=None, op0=ALU.mult),
                     reads=[tm], writes=[tm])
            S.tm[key + tag] = tm
    S.mask = {}
    for key, tri in (("f", le), ("b", ge)):
        m = pg.sb((C, 2, C), F32, f"c_mask{key}")
        for g in range(2):
            pg.V(lambda e, m=m, g=g, tri=tri: e.tensor_copy(out=m[:, g, :], in_=tri[:]), reads=[tri], writes=[m])
        S.mask[key] = m


def phase_a(S, li):
    pg = S.pg
    ident = S.ident

    def tile_src(tt):
        if li == 0:
            if tt < 2:
                return S.ctx_in, S.ctx_in[tt * 128:(tt + 1) * 128, :]
            return S.x_in, S.x_in[(tt - 2) * 128:(tt - 1) * 128, :]
        return S.X, S.X[tt * 128:(tt + 1) * 128, :]

    with ExitStack() as pa:
        wsb = pg.sb((128, 8, DIN), BF16, "w_in_sb", pa)
        for q4 in range(4):
            pg.dma(pg.pool, wsb[:, :, q4 * 1032:(q4 + 1) * 1032],
                   S.w_in[li, :, q4 * 1032:(q4 + 1) * 1032].rearrange("(c p) n -> p c n", p=128),
                   reads=[S.w_in], writes=[wsb])
        shv = pg.sb((128, 2, 8), F32, "shv", pa)
        scv = pg.sb((128, 2, 8), F32, "scv", pa)
        for r in range(2):
            pg.dma(pg.sp, shv[:, r, :], S.MOD[li, r, 0:D].rearrange("(c p) -> p c", p=128), reads=[S.MOD], writes=[shv],
                   allow_slow_non_contiguous=True)
            pg.dma(pg.sp, scv[:, r, :], S.MOD[li, r, D:2 * D].rearrange("(c p) -> p c", p=128), reads=[S.MOD], writes=[scv],
                   allow_slow_non_contiguous=True)
        pg.V(lambda e: e.tensor_scalar(out=scv[:], in0=scv[:], scalar1=1.0, scalar2=None, op0=ALU.add), reads=[scv], writes=[scv])
        xts = [pg.sb((128, D), F32, f"a_x{i}", pa) for i in range(2)]
        xns = [pg.sb((128, D), F32, f"a_xn{i}", pa) for i in range(2)]
        xmT = [pg.sb((128, 8, 128), BF16, f"a_xmT{i}", pa) for i in range(2)]
        uts = [pg.sb((128, DIN), F32, f"a_u{i}", pa) for i in range(2)]
        lnb = [ln_bufs(pg, pa, f"a{i}") for i in range(2)]
        psT = [pg.ps((128, 512), F32, f"a_psT{i}", pa) for i in range(2)]
        psU = [pg.ps((128, 512), F32, f"a_psU{i}", pa) for i in range(4)]
        nu = 0
        for tt in range(NT):
            i2 = tt % 2
            r = 1 if tt < 2 else 0
            xt, xn, xm, ut = xts[i2], xns[i2], xmT[i2], uts[i2]
            srcb, srcap = tile_src(tt)
            pg.dma(pg.sp, xt[:], srcap, reads=[srcb], writes=[xt])
            rstd, nmr = ln_stats(pg, xt, lnb[i2])
            pg.A(lambda e, xt=xt, xn=xn, rstd=rstd, nmr=nmr: e.activation(out=xn[:], in_=xt[:], func=AF.Identity,
                                                                         bias=nmr[:], scale=rstd[:]),
                 reads=[xt, rstd, nmr], writes=[xn])
            for hb in range(2):
                pT = psT[hb]
                for c4 in range(4):
                    ch = hb * 4 + c4
                    pg.P(lambda e, pT=pT, c4=c4, ch=ch, xn=xn: e.transpose(out=pT[:, c4 * 128:(c4 + 1) * 128],
                                                                          in_=xn[:, ch * 128:(ch + 1) * 128], identity=ident[:]),
                         reads=[xn, ident], writes=[pT])
                for c4 in range(4):
                    ch = hb * 4 + c4
                    pg.A(lambda e, pT=pT, c4=c4, ch=ch, xm=xm, r=r: e.activation(
                        out=xm[:, ch, :], in_=pT[:, c4 * 128:(c4 + 1) * 128], func=AF.Identity,
                        bias=shv[:, r, ch:ch + 1], scale=scv[:, r, ch:ch + 1]),
                         reads=[pT, shv, scv], writes=[xm])
            for j in range(9):
                c0 = j * 512
                cw = min(512, DIN - c0)
                pu = psU[nu % 4]
                nu += 1
                for ch in range(8):
                    pg.P(lambda e, pu=pu, ch=ch, c0=c0, cw=cw, xm=xm: e.matmul(pu[:, 0:cw], lhsT=xm[:, ch, :],
                                                                              rhs=wsb[:, ch, c0:c0 + cw],
                                                                              start=(ch == 0), stop=(ch == 7)),
                         reads=[xm, wsb], writes=[pu])
                pg.V(lambda e, pu=pu, c0=c0, cw=cw, ut=ut: e.tensor_copy(out=ut[:, c0:c0 + cw], in_=pu[:, 0:cw]),
                     reads=[pu], writes=[ut])
            pg.dma(pg.pool, S.U[tt * 128:(tt + 1) * 128, :], ut[:], reads=[ut], writes=[S.U])
        pg.barrier()


def ln_bufs(pg, st, tag):
    b = NS()
    b.stp = pg.sb((128, 2, 6), F32, f"ln_st{tag}", st)
    b.mv = pg.sb((128, 2), F32, f"ln_mv{tag}", st)
    b.rstd = pg.sb((128, 1), F32, f"ln_rs{tag}", st)
    b.nmr = pg.sb((128, 1), F32, f"ln_nm{tag}", st)
    return b


def ln_stats(pg, xt, b):
    for h in range(2):
        pg.V(lambda e, h=h: e.bn_stats(out=b.stp[:, h, :], in_=xt[:, h * 512:(h + 1) * 512]), reads=[xt], writes=[b.stp])
    pg.V(lambda e: e.bn_aggr(out=b.mv[:], in_=b.stp[:].rearrange("p a b -> p (a b)")), reads=[b.stp], writes=[b.mv])
    pg.A(lambda e: e.activation(out=b.rstd[:], in_=b.mv[:, 1:2], func=AF.Sqrt, bias=EPS, scale=1.0), reads=[b.mv], writes=[b.rstd])
    pg.V(lambda e: e.reciprocal(out=b.rstd[:], in_=b.rstd[:]), reads=[b.rstd], writes=[b.rstd])
    pg.V(lambda e: e.scalar_tensor_tensor(out=b.nmr[:], in0=b.mv[:, 0:1], scalar=-1.0, in1=b.rstd[:], op0=ALU.mult, op1=ALU.mult),
         reads=[b.mv, b.rstd], writes=[b.nmr])
    return b.rstd, b.nmr


def group_rows(ap2d, unit_is_gla, j):
    if j < 2:
        return ap2d[j * 128:(j + 1) * 128, :].rearrange("(g t) w -> t g w", t=C)
    jj = j - 2
    if not unit_is_gla:
        return ap2d[CTX + jj * 128:CTX + (jj + 1) * 128, :].rearrange("(g t) w -> t g w", t=C)
    return ap2d[CTX:, :].rearrange("(r c) w -> c r w", c=GRID_W)[jj].rearrange("(g t) w -> t g w", t=C)


def phase_b(S, li):
    pg = S.pg
    V, A, G, P = pg.V, pg.A, pg.G, pg.P
    ident = S.ident
    fwd_groups = list(range(NT))
    bwd_groups = [1, 0] + list(range(NT - 1, 1, -1))
    with ExitStack() as pb:
        pre = [pg.ps((128, 512), F32, f"b_pre{i}", pb) for i in range(2)]
        tok = [pg.ps((128, 512), F32, f"b_tok{i}", pb) for i in range(2)]
        obk = [pg.ps((128, 512), F32, f"b_o{i}", pb) for i in range(2)]
        dsb = [pg.ps((128, 512), F32, f"b_ds{i}", pb) for i in range(2)]
        NB = 2
        mk = lambda shape, dt, nm: [pg.sb(shape, dt, f"b_{nm}{i}", pb) for i in range(NB)]
        qg = mk((C, 2, 128), F32, "qg")
        zg = mk((C, 2, 128), F32, "zg")
        vg = mk((C, 2, 128), F32, "vg")
        glg = mk((C, 2, 17), F32, "glg")
        kg = mk((C, 2, 128), F32, "kg")
        la = mk((C, 2, 128), F32, "la")
        vb = mk((C, 2, 128), BF16, "vb")
        ET = mk((128, 2, C), F32, "ET")
        EiT = mk((128, 2, C), F32, "EiT")
        gm = mk((128, 2, 2), F32, "gm")
        Etk = mk((C, 2, 128), F32, "Etk")
        KdT = mk((128, 2, C), BF16, "KdT")
        QdT = mk((128, 2, C), BF16, "QdT")
        Kd = mk((C, 2, 128), BF16, "Kd")
        scTd = {"f": mk((C, 2, C), BF16, "scTf"), "b": mk((C, 2, C), BF16, "scTb")}
        glT = mk((17, 2, C), F32, "glT")
        osb = mk((C, 2, 128), F32, "osb")
        for b_ in scTd["f"] + scTd["b"]:
            G(lambda e, b_=b_: e.memset(b_[:], 0.0), writes=[b_])
        for b_ in glg:
            G(lambda e, b_=b_: e.memset(b_[:], 1.0), writes=[b_])
        Sst = pg.sb((128, 128), F32, "b_S", pb)
        Stmp = pg.sb((128, 128), F32, "b_Stmp", pb)
        Ssc = [pg.sb((128, 128), BF16, f"b_Ssc{i}", pb) for i in range(2)]
        omlr = pg.sb((C, 2, 128), F32, "b_omlr", pb)
        wgk = pg.sb((17, 256), F32, "b_wgk", pb)
        it = 0
        for unit in getattr(S, 'units', range(8)):
            gla = unit >= 4
            h = unit % 4
            dk = 64 if gla else 128
            ocol = unit * 128
            for di, dkey in enumerate("fb"):
                fwd = di == 0
                tm = S.tm[dkey + ("g" if gla else "h")]
                mask = S.mask[dkey]
                endc = C - 1 if fwd else 0
                OD = S.OF if fwd else S.OB
                if gla:
                    pg.dma(pg.sp, wgk[0:16, :], S.w_gk2[li, di], reads=[S.w_gk2], writes=[wgk])
                    pg.dma(pg.sp, wgk[16:17, :], S.b_gk[li, di:di + 1, :], reads=[S.b_gk], writes=[wgk])
                else:
                    for g in range(2):
                        pg.dma(pg.sp, omlr[:, g, :], S.OML[di, li, h * 128:(h + 1) * 128].partition_broadcast(C),
                               reads=[S.OML], writes=[omlr])
                G(lambda e: e.memset(Sst[:], 0.0), writes=[Sst])
                nds = 0
                for j in (fwd_groups if fwd else bwd_groups):
                    i2 = it % NB
                    it += 1
                    q_, z_, v_, gl_, k_, la_, vb_ = qg[i2], zg[i2], vg[i2], glg[i2], kg[i2], la[i2], vb[i2]
                    ET_, EiT_, gm_, Etk_, KdT_, QdT_, Kd_, scT_, glT_, osb_ = (ET[i2], EiT[i2], gm[i2], Etk[i2], KdT[i2],
                                                                             QdT[i2], Kd[i2], scTd[dkey][i2], glT[i2], osb[i2])
                    pre_, tok_, o_ = pre[i2], tok[i2], obk[i2]
                    uview = lambda c0, w: group_rows(S.U[:, c0:c0 + w], gla, j)
                    if gla:
                        pg.dma(pg.sp, q_[:, :, 0:64], uview(O_GQ + h * 64, 64), reads=[S.U], writes=[q_])
                        pg.dma(pg.sp, k_[:, :, 0:64], uview(O_GK + h * 64, 64), reads=[S.U], writes=[k_])
                        pg.dma(pg.sp, v_[:], uview(O_GV + h * 128, 128), reads=[S.U], writes=[v_])
                        pg.dma(pg.sp, gl_[:, :, 0:16], uview(O_GLF if fwd else O_GLB, 16), reads=[S.U], writes=[gl_])
                    else:
                        pg.dma(pg.sp, q_[:], uview(O_HQ + h * 128, 128), reads=[S.U], writes=[q_])
                        pg.dma(pg.sp, z_[:], uview((O_HFF if fwd else O_HFB) + h * 128, 128), reads=[S.U], writes=[z_])
                        pg.dma(pg.sp, v_[:], uview(O_HI + h * 128, 128), reads=[S.U], writes=[v_])
                    if gla:
                        for g in range(2):
                            P(lambda e, g=g: e.transpose(out=tok_[0:17, 384 + g * C:384 + (g + 1) * C], in_=gl_[:, g, :],
                                                         identity=ident[0:C, 0:C]), reads=[gl_, ident], writes=[tok_])
                        V(lambda e: e.tensor_copy(out=glT_[:].rearrange("p g t -> p (g t)"), in_=tok_[0:17, 384:512]),
                          reads=[tok_], writes=[glT_])
                        for g in range(2):
                            P(lambda e, g=g: e.matmul(pre_[0:C, 260 + g * 64:260 + (g + 1) * 64], lhsT=glT_[:, g, :],
                                                      rhs=wgk[:, h * 64:(h + 1) * 64], start=True, stop=True),
                              reads=[glT_, wgk], writes=[pre_])
                        A(lambda e: e.activation(out=la_[:, :, 0:64], in_=pre_[0:C, 260:388].rearrange("p (g k) -> p g k", g=2),
                                                 func=AF.Exp, scale=-1.0), reads=[pre_], writes=[la_])
                        A(lambda e: e.activation(out=la_[:, :, 0:64], in_=la_[:, :, 0:64], func=AF.Ln, bias=1.0, scale=1.0),
                          reads=[la_], writes=[la_])
                    else:
                        A(lambda e: e.activation(out=k_[:], in_=z_[:], func=AF.Sigmoid, scale=-1.0), reads=[z_], writes=[k_])
                        V(lambda e: e.tensor_tensor(out=k_[:], in0=k_[:], in1=omlr[:], op=ALU.mult), reads=[k_, omlr], writes=[k_])
                        A(lambda e: e.activation(out=la_[:], in_=k_[:], func=AF.Ln, bias=1.0, scale=-1.0), reads=[k_], writes=[la_])
                    G(lambda e: e.tensor_copy(out=vb_[:], in_=v_[:]), reads=[v_], writes=[vb_])
                    for g in range(2):
                        P(lambda e, g=g: e.matmul(pre_[0:dk, g * 66:(g + 1) * 66], lhsT=la_[:, g, 0:dk], rhs=tm[:],
                                                  start=True, stop=True), reads=[la_, tm], writes=[pre_])
                        P(lambda e, g=g: e.matmul(pre_[0:C, 132 + g * dk:132 + (g + 1) * dk], lhsT=tm[:, 0:C], rhs=la_[:, g, 0:dk],
                                                  start=True, stop=True), reads=[la_, tm], writes=[pre_])
                    for g in range(2):
                        P(lambda e, g=g: e.transpose(out=tok_[0:dk, g * C:(g + 1) * C], in_=k_[:, g, 0:dk],
                                                     identity=ident[0:C, 0:C]), reads=[k_, ident], writes=[tok_])
                        P(lambda e, g=g: e.transpose(out=tok_[0:dk, 128 + g * C:128 + (g + 1) * C], in_=q_[:, g, 0:dk],
                                                     identity=ident[0:C, 0:C]), reads=[q_, ident], writes=[tok_])
                    dtv = pre_[0:dk, 0:132].rearrange("p (g c) -> p g c", g=2)
                    A(lambda e: e.activation(out=ET_[0:dk], in_=dtv[:, :, 0:C], func=AF.Exp), reads=[pre_], writes=[ET_])
                    A(lambda e: e.activation(out=EiT_[0:dk], in_=dtv[:, :, 0:C], func=AF.Exp, scale=-1.0), reads=[pre_], writes=[EiT_])
                    A(lambda e: e.activation(out=gm_[0:dk, :, 0:1], in_=dtv[:, :, C:C + 1], func=AF.Exp), reads=[pre_], writes=[gm_])
                    V(lambda e: e.tensor_tensor(out=gm_[0:dk, :, 1:2], in0=gm_[0:dk, :, 0:1], in1=EiT_[0:dk, :, endc:endc + 1],
                                                op=ALU.mult), reads=[gm_, EiT_], writes=[gm_])
                    A(lambda e: e.activation(out=Etk_[:, :, 0:dk], in_=pre_[0:C, 132:132 + 2 * dk].rearrange("p (g k) -> p g k", g=2),
                                             func=AF.Exp), reads=[pre_], writes=[Etk_])
                    V(lambda e: e.tensor_tensor(out=KdT_[0:dk], in0=tok_[0:dk, 0:128].rearrange("p (g c) -> p g c", g=2),
                                                in1=ET_[0:dk], op=ALU.mult), reads=[tok_, ET_], writes=[KdT_])
                    qsc = 0.125 if gla else 1.0
                    V(lambda e: e.scalar_tensor_tensor(out=QdT_[0:dk], in0=tok_[0:dk, 128:256].rearrange("p (g c) -> p g c", g=2),
                                                       scalar=qsc, in1=EiT_[0:dk], op0=ALU.mult, op1=ALU.mult),
                      reads=[tok_, EiT_], writes=[QdT_])
                    V(lambda e: e.tensor_tensor(out=Kd_[:, :, 0:dk], in0=k_[:, :, 0:dk], in1=Etk_[:, :, 0:dk], op=ALU.mult),
                      reads=[k_, Etk_], writes=[Kd_])
                    sco = 256
                    for g in range(2):
                        P(lambda e, g=g: e.matmul(tok_[0:C, sco + g * C:sco + (g + 1) * C], lhsT=KdT_[0:dk, g, :],
                                                  rhs=QdT_[0:dk, g, :], start=True, stop=True),
                          reads=[KdT_, QdT_], writes=[tok_])
                    V(lambda e: e.copy_predicated(out=scT_[:], mask=mask[:].bitcast(U32),
                                                  data=tok_[0:C, sco:sco + 2 * C].rearrange("p (g c) -> p g c", g=2)),
                      reads=[tok_, mask], writes=[scT_])
                    for g in ((0, 1) if fwd else (1, 0)):
                        ssc = Ssc[nds % 2]
                        dsp = dsb[nds % 2]
                        nds += 1
                        A(lambda e, g=g, ssc=ssc: e.activation(out=ssc[0:dk, :], in_=Sst[0:dk, :], func=AF.Identity,
                                                               scale=gm_[0:dk, g, 0:1]), reads=[Sst, gm_], writes=[ssc])
                        P(lambda e, g=g: e.matmul(o_[0:C, g * 128:(g + 1) * 128], lhsT=scT_[:, g, :], rhs=vb_[:, g, :],
                                                  start=True, stop=False), reads=[scT_, vb_], writes=[o_])
                        P(lambda e, g=g, ssc=ssc: e.matmul(o_[0:C, g * 128:(g + 1) * 128], lhsT=QdT_[0:dk, g, :], rhs=ssc[0:dk, :],
                                                           start=False, stop=True), reads=[QdT_, ssc], writes=[o_])
                        P(lambda e, g=g, dsp=dsp: e.matmul(dsp[0:dk, 0:128], lhsT=Kd_[:, g, 0:dk], rhs=vb_[:, g, :],
                                                           start=True, stop=True), reads=[Kd_, vb_], writes=[dsp])
                        V(lambda e, g=g: e.tensor_scalar(out=Stmp[0:dk, :], in0=Sst[0:dk, :], scalar1=gm_[0:dk, g, 1:2], scalar2=None,
                                                         op0=ALU.mult), reads=[Sst, gm_], writes=[Stmp])
                        V(lambda e, g=g, dsp=dsp: e.scalar_tensor_tensor(out=Sst[0:dk, :], in0=dsp[0:dk, 0:128],
                                                                         scalar=EiT_[0:dk, g, endc:endc + 1], in1=Stmp[0:dk, :],
                                                                         op0=ALU.mult, op1=ALU.add),
                          reads=[dsp, EiT_, Stmp], writes=[Sst])
                    A(lambda e: e.activation(out=osb_[:], in_=o_[0:C, 0:256].rearrange("p (g v) -> p g v", g=2), func=AF.Copy),
                      reads=[o_], writes=[osb_])
                    pg.dma(pg.pool, group_rows(OD[:, ocol:ocol + 128], gla, j), osb_[:], reads=[osb_], writes=[OD])
                    if getattr(S, "dbgfn", None) is not None:
                        S.dbgfn(dict(k=k_, la=la_, ET=ET_, EiT=EiT_, gm=gm_, Etk=Etk_, KdT=KdT_, QdT=QdT_, Kd=Kd_, scT=scT_,
                                     osb=osb_, S=Sst, vb=vb_, q=q_, v=v_, tm=tm, pre=pre_, tok=tok_), pb)
                        pg.barrier()
                        return
        pg.barrier()


def phase_c(S, li):
    pg = S.pg
    V, A, G, P = pg.V, pg.A, pg.G, pg.P
    ident = S.ident
    last = li == DEPTH - 1
    tiles = getattr(S, "ctiles", None) or list(range(2 if last else 0, NT))

    def x_src(tt):
        if li == 0:
            if tt < 2:
                return S.ctx_in, S.ctx_in[tt * 128:(tt + 1) * 128, :]
            return S.x_in, S.x_in[(tt - 2) * 128:(tt - 1) * 128, :]
        return S.X, S.X[tt * 128:(tt + 1) * 128, :]

    with ExitStack() as pc:
        pT = [pg.ps((128, 512), F32, f"c_pT{i}", pc) for i in range(2)]
        py = [pg.ps((128, 512), F32, f"c_py{i}", pc) for i in range(2)]
        pss = [pg.ps((128, 512), F32, f"c_ps{i}", pc) for i in range(4)]
        qbanks = pT + py
        wo = pg.sb((128, 8, D), BF16, "c_wo", pc)
        for hh in range(2):
            pg.dma(pg.pool, wo[:, :, hh * 512:(hh + 1) * 512],
                   S.w_out[li, :, hh * 512:(hh + 1) * 512].rearrange("(c p) n -> p c n", p=128), reads=[S.w_out], writes=[wo])
        wq = pg.sb((128, 8, 2048), BF16, "c_wq", pc)
        for hh in range(2):
            pg.dma(pg.pool, wq[:, :, hh * 1024:(hh + 1) * 1024],
                   S.pwq[li, :, hh * 1024:(hh + 1) * 1024].rearrange("(c p) n -> p c n", p=128), reads=[S.pwq], writes=[wq])
        ktmp = pg.sb((128, 2, 128), F32, "c_ktmp", pc)
        keysT = pg.sb((128, 2, 128), F32, "c_keysT", pc)
        for hf in range(2):
            pg.dma(pg.sp, ktmp[:, hf, :], S.psk[li, hf], reads=[S.psk], writes=[ktmp])
        for hf in range(2):
            P(lambda e, hf=hf: e.transpose(out=pT[0][:, hf * 128:(hf + 1) * 128], in_=ktmp[:, hf, :], identity=ident[:]),
              reads=[ktmp, ident], writes=[pT[0]])
        V(lambda e: e.tensor_copy(out=keysT[:].rearrange("p a b -> p (a b)"), in_=pT[0][:, 0:256]), reads=[pT[0]], writes=[keysT])
        bt = {n: pg.sb((128, D), F32, f"c_bt_{n}", pc) for n in ("gain", "g1", "gam1", "bet1", "sh2", "sc2", "g2", "gam2", "bet2")}
        for hh in range(4):
            pg.dma(pg.sp, bt["gain"][:, hh * 128:(hh + 1) * 128], S.hg_norm[li, :].partition_broadcast(128),
                   reads=[S.hg_norm], writes=[bt["gain"]])
            pg.dma(pg.sp, bt["gain"][:, 512 + hh * 128:512 + (hh + 1) * 128], S.gla_norm[li, :].partition_broadcast(128),
                   reads=[S.gla_norm], writes=[bt["gain"]])
        pg.dma(pg.sp, bt["gam1"][:], S.ln_gamma[li, 0, :].partition_broadcast(128), reads=[S.ln_gamma], writes=[bt["gam1"]])
        pg.dma(pg.sp, bt["bet1"][:], S.ln_beta[li, 0, :].partition_broadcast(128), reads=[S.ln_beta], writes=[bt["bet1"]])
        pg.dma(pg.sp, bt["gam2"][:], S.ln_gamma[li, 1, :].partition_broadcast(128), reads=[S.ln_gamma], writes=[bt["gam2"]])
        pg.dma(pg.sp, bt["bet2"][:], S.ln_beta[li, 1, :].partition_broadcast(128), reads=[S.ln_beta], writes=[bt["bet2"]])

        def load_mod(r):
            for n, k in (("g1", 2), ("sh2", 3), ("sc2", 4), ("g2", 5)):
                pg.dma(pg.sp, bt[n][:], S.MOD[li, r, k * D:(k + 1) * D].partition_broadcast(128), reads=[S.MOD], writes=[bt[n]])
            V(lambda e: e.tensor_scalar(out=bt["sc2"][:], in0=bt["sc2"][:], scalar1=1.0, scalar2=None, op0=ALU.add),
              reads=[bt["sc2"]], writes=[bt["sc2"]])

        iot = pg.sb((128, 8, 16, 16), F32, "c_iot", pc)
        with ExitStack() as tmpst:
            ioti = pg.sb((128, 2048), I32, "c_ioti", tmpst)
            G(lambda e: e.iota(out=ioti[:], pattern=[[0, 128], [1, 16]], base=0, channel_multiplier=0), writes=[ioti])
            V(lambda e: e.tensor_copy(out=iot[:].rearrange("p a b c -> p (a b c)"), in_=ioti[:]), reads=[ioti], writes=[iot])
            pg.barrier()
        sbt = lambda n, shape=(128, D), dt=F32: pg.sb(shape, dt, "c_" + n, pc)
        of_t, ob_t, ug, x_t, x1, tb, acc, junk = (sbt(n) for n in ("of", "ob", "ug", "x", "x1", "t", "acc", "junk"))
        sgb = junk
        mixT = sbt("mixT", (128, 8, 128), BF16)
        tT = sbt("tT", (128, 8, 128), BF16)
        qT = sbt("qT", (128, 16, 128), F32)
        s_sb = sbt("s", (128, 16, 128), F32)
        cand = sbt("cand", (128, 8, 256), F32)
        s2 = [sbt(f"s2{i}", (128, 256), F32) for i in range(2)]
        vv = sbt("vv", (128, 16, 16), F32)
        ii = sbt("ii", (128, 16, 16), U32)
        iif = sbt("iif", (128, 16, 16), F32)
        ts = sbt("ts", (128, 8, 16), F32)
        pos = sbt("pos", (128, 8, 16), U32)
        pa = sbt("pa", (128, 8, 16), U32)
        pbb = sbt("pb", (128, 8, 16), U32)
        paf = sbt("paf", (128, 8, 16), F32)
        pbf = sbt("pbf", (128, 8, 16), F32)
        i1g = sbt("i1g", (128, 8, 16), F32)
        i2g = sbt("i2g", (128, 8, 16), F32)
        idxf = sbt("idxf", (128, 128), F32)
        idxu = sbt("idxu", (128, 128), U32)
        ex = sbt("ex", (128, 8, 16), F32)
        zz = sbt("zz", (128, 8), F32)
        gw = sbt("gw", (128, 128), F32)
        actb = sbt("act", (128, 128), F32)
        g1b = sbt("gl1", (128, 128), F32)
        g2b = sbt("gl2", (128, 128), F32)
        wgt = sbt("wgt", (128, 128), F32)
        ssq = sbt("ssq", (128, 8), F32)
        lnb = ln_bufs(pg, pc, "c")
        NG = 8
        gbuf = [sbt(f"gb{i}") for i in range(NG)]
        ng = 0
        cur_r = None
        U_tab = S.peer_u[:].rearrange("l e d -> (l e) d") if hasattr(S, "peer_u") else None
        V_tab = S.peer_v[:].rearrange("l e d -> (l e) d") if hasattr(S, "peer_v") else None

        def top16(src3, n, w, vals, idxs, k0):
            sc = s2[k0 % 2]
            V(lambda e: e.max(out=vals[:, n, 0:8], in_=src3[:, n, :]), reads=[src3], writes=[vals])
            V(lambda e: e.max_index(out=idxs[:, n, 0:8], in_max=vals[:, n, 0:8], in_values=src3[:, n, :]),
              reads=[src3, vals], writes=[idxs])
            V(lambda e: e.match_replace(out=sc[:, 0:w], in_to_replace=vals[:, n, 0:8], in_values=src3[:, n, :], imm_value=-1e30),
              reads=[src3, vals], writes=[sc])
            V(lambda e: e.max(out=vals[:, n, 8:16], in_=sc[:, 0:w]), reads=[sc], writes=[vals])
            V(lambda e: e.max_index(out=idxs[:, n, 8:16], in_max=vals[:, n, 8:16], in_values=sc[:, 0:w]),
              reads=[sc, vals], writes=[idxs])

        for tt in tiles:
            r = 1 if tt < 2 else 0
            if r != cur_r:
                load_mod(r)
                cur_r = r
            rows = slice(tt * 128, (tt + 1) * 128)
            xsb, xsap = x_src(tt)
            pg.dma(pg.sp, of_t[:], S.OF[rows, :], reads=[S.OF], writes=[of_t])
            pg.dma(pg.sp, ob_t[:], S.OB[rows, :], reads=[S.OB], writes=[ob_t])
            pg.dma(pg.sp, x_t[:], xsap, reads=[xsb], writes=[x_t])
            pg.dma(pg.sp, ug[:, 0:512], S.U[rows, O_HG:O_HG + 512], reads=[S.U], writes=[ug])
            pg.dma(pg.sp, ug[:, 512:1024], S.U[rows, O_GG:O_GG + 512], reads=[S.U], writes=[ug])
            V(lambda e: e.tensor_tensor(out=of_t[:], in0=of_t[:], in1=ob_t[:], op=ALU.add), reads=[of_t, ob_t], writes=[of_t])
            A(lambda e: e.activation(out=junk[:], in_=of_t[:], func=AF.Square), reads=[of_t], writes=[junk])
            V(lambda e: e.tensor_reduce(out=ssq[:], in_=junk[:].rearrange("p (h d) -> p h d", h=8), axis=AX.X, op=ALU.add),
              reads=[junk], writes=[ssq])
            A(lambda e: e.activation(out=ssq[:], in_=ssq[:], func=AF.Sqrt, bias=EPS, scale=1.0 / 128.0), reads=[ssq], writes=[ssq])
            V(lambda e: e.reciprocal(out=ssq[:], in_=ssq[:]), reads=[ssq], writes=[ssq])
            V(lambda e: e.tensor_tensor(out=of_t[:].rearrange("p (h d) -> p h d", h=8), in0=of_t[:].rearrange("p (h d) -> p h d", h=8),
                                        in1=ssq[:].unsqueeze(2).to_broadcast([128, 8, 128]), op=ALU.mult),
              reads=[of_t, ssq], writes=[of_t])
            V(lambda e: e.tensor_tensor(out=of_t[:], in0=of_t[:], in1=bt["gain"][:], op=ALU.mult), reads=[of_t, bt["gain"]], writes=[of_t])
            A(lambda e: e.activation(out=sgb[:], in_=ug[:], func=AF.Sigmoid), reads=[ug], writes=[sgb])
            V(lambda e: e.tensor_tensor(out=ug[:], in0=ug[:], in1=sgb[:], op=ALU.mult), reads=[ug, sgb], writes=[ug])
            V(lambda e: e.tensor_tensor(out=of_t[:], in0=of_t[:], in1=ug[:], op=ALU.mult), reads=[of_t, ug], writes=[of_t])
            if getattr(S, "dbgc", None) is not None and tt == S.dbgc_tile:
                S.dbgc(dict(mix=of_t, ssq=ssq, gain=bt["gain"], ug=ug), pc)
            for hb in range(2):
                for c4 in range(4):
                    ch = hb * 4 + c4
                    P(lambda e, hb=hb, c4=c4, ch=ch: e.transpose(out=pT[hb][:, c4 * 128:(c4 + 1) * 128],
                                                                in_=of_t[:, ch * 128:(ch + 1) * 128], identity=ident[:]),
                      reads=[of_t, ident], writes=[pT[hb]])
                A(lambda e, hb=hb: e.activation(out=mixT[:, hb * 4:(hb + 1) * 4, :].rearrange("p a b -> p (a b)"), in_=pT[hb][:],
                                               func=AF.Copy), reads=[pT[hb]], writes=[mixT])
            for nb in range(2):
                for ch in range(8):
                    P(lambda e, nb=nb, ch=ch: e.matmul(py[nb][:], lhsT=mixT[:, ch, :], rhs=wo[:, ch, nb * 512:(nb + 1) * 512],
                                                      start=(ch == 0), stop=(ch == 7)), reads=[mixT, wo], writes=[py[nb]])
                V(lambda e, nb=nb: e.tensor_tensor(out=x1[:, nb * 512:(nb + 1) * 512], in0=py[nb][:],
                                                   in1=bt["g1"][:, nb * 512:(nb + 1) * 512], op=ALU.mult),
                  reads=[py[nb], bt["g1"]], writes=[x1])
            if getattr(S, "dbgc", None) is not None and tt == S.dbgc_tile:
                S.dbgc(dict(y1=x1, g1=bt["g1"], xt=x_t), pc)
            V(lambda e: e.scalar_tensor_tensor(out=x1[:], in0=x_t[:], scalar=ALPHA, in1=x1[:], op0=ALU.mult, op1=ALU.add),
              reads=[x_t, x1], writes=[x1])
            rstd, nmr = ln_stats(pg, x1, lnb)
            A(lambda e: e.activation(out=x1[:], in_=x1[:], func=AF.Identity, bias=nmr[:], scale=rstd[:]), reads=[x1, rstd, nmr], writes=[x1])
            V(lambda e: e.tensor_tensor(out=x1[:], in0=x1[:], in1=bt["gam1"][:], op=ALU.mult), reads=[x1, bt["gam1"]], writes=[x1])
            V(lambda e: e.tensor_tensor(out=x1[:], in0=x1[:], in1=bt["bet1"][:], op=ALU.add), reads=[x1, bt["bet1"]], writes=[x1])
            rstd, nmr = ln_stats(pg, x1, lnb)
            A(lambda e: e.activation(out=tb[:], in_=x1[:], func=AF.Identity, bias=nmr[:], scale=rstd[:]), reads=[x1, rstd, nmr], writes=[tb])
            V(lambda e: e.tensor_tensor(out=tb[:], in0=tb[:], in1=bt["sc2"][:], op=ALU.mult), reads=[tb, bt["sc2"]], writes=[tb])
            V(lambda e: e.tensor_tensor(out=tb[:], in0=tb[:], in1=bt["sh2"][:], op=ALU.add), reads=[tb, bt["sh2"]], writes=[tb])
            for hb in range(2):
                for c4 in range(4):
                    ch = hb * 4 + c4
                    P(lambda e, hb=hb, c4=c4, ch=ch: e.transpose(out=pT[hb][:, c4 * 128:(c4 + 1) * 128],
                                                                in_=tb[:, ch * 128:(ch + 1) * 128], identity=ident[:]),
                      reads=[tb, ident], writes=[pT[hb]])
                A(lambda e, hb=hb: e.activation(out=tT[:, hb * 4:(hb + 1) * 4, :].rearrange("p a b -> p (a b)"), in_=pT[hb][:],
                                               func=AF.Copy), reads=[pT[hb]], writes=[tT])
            for qb in range(4):
                bank = qbanks[qb]
                for p4 in range(4):
                    pq = qb * 4 + p4
                    for ch in range(8):
                        P(lambda e, bank=bank, p4=p4, pq=pq, ch=ch: e.matmul(bank[:, p4 * 128:(p4 + 1) * 128],
                                                                            lhsT=wq[:, ch, pq * 128:(pq + 1) * 128], rhs=tT[:, ch, :],
                                                                            start=(ch == 0), stop=(ch == 7)),
                          reads=[wq, tT], writes=[bank])
                eng = A if qb % 2 == 0 else V
                if qb % 2 == 0:
                    A(lambda e, bank=bank, qb=qb: e.activation(out=qT[:, qb * 4:(qb + 1) * 4, :].rearrange("p a b -> p (a b)"),
                                                               in_=bank[:], func=AF.Copy), reads=[bank], writes=[qT])
                else:
                    V(lambda e, bank=bank, qb=qb: e.tensor_copy(out=qT[:, qb * 4:(qb + 1) * 4, :].rearrange("p a b -> p (a b)"),
                                                                in_=bank[:]), reads=[bank], writes=[qT])
            for qb in range(4):
                bank = pss[qb]
                for p4 in range(4):
                    pq = qb * 4 + p4
                    P(lambda e, bank=bank, p4=p4, pq=pq: e.matmul(bank[:, p4 * 128:(p4 + 1) * 128], lhsT=qT[:, pq, :],
                                                                 rhs=keysT[:, pq % 2, :], start=True, stop=True),
                      reads=[qT, keysT], writes=[bank])
                if qb % 2 == 0:
                    A(lambda e, bank=bank, qb=qb: e.activation(out=s_sb[:, qb * 4:(qb + 1) * 4, :].rearrange("p a b -> p (a b)"),
                                                               in_=bank[:], func=AF.Copy), reads=[bank], writes=[s_sb])
                else:
                    V(lambda e, bank=bank, qb=qb: e.tensor_copy(out=s_sb[:, qb * 4:(qb + 1) * 4, :].rearrange("p a b -> p (a b)"),
                                                                in_=bank[:]), reads=[bank], writes=[s_sb])
            for pq in range(16):
                top16(s_sb, pq, 128, vv, ii, pq)
            for h in range(8):
                V(lambda e, h=h: e.tensor_tensor(out=cand[:, h, :].rearrange("p (a b) -> p a b", a=16),
                                                 in0=vv[:, 2 * h, :].unsqueeze(2).to_broadcast([128, 16, 16]),
                                                 in1=vv[:, 2 * h + 1, :].unsqueeze(1).to_broadcast([128, 16, 16]), op=ALU.add),
                  reads=[vv], writes=[cand])
            for h in range(8):
                top16(cand, h, 256, ts, pos, h)
            V(lambda e: e.tensor_single_scalar(out=pa[:], in_=pos[:], scalar=4, op=ALU.logical_shift_right), reads=[pos], writes=[pa])
            V(lambda e: e.tensor_single_scalar(out=pbb[:], in_=pos[:], scalar=15, op=ALU.bitwise_and), reads=[pos], writes=[pbb])
            V(lambda e: e.tensor_copy(out=paf[:], in_=pa[:]), reads=[pa], writes=[paf])
            V(lambda e: e.tensor_copy(out=pbf[:], in_=pbb[:]), reads=[pbb], writes=[pbf])
            V(lambda e: e.tensor_copy(out=iif[:], in_=ii[:]), reads=[ii], writes=[iif])
            eq = s_sb[:].rearrange("p a (b c) -> p (a b c)", b=8)[:, 0:2048].rearrange("p (h k a) -> p h k a", h=8, k=16)
            prod = cand[:].rearrange("p h (k a) -> p h k a", k=16)
            iif4 = iif[:].rearrange("p (h two) a -> p h two a", two=2)
            for half, (pxf, outg) in enumerate(((paf, i1g), (pbf, i2g))):
                V(lambda e, pxf=pxf: e.tensor_tensor(out=eq, in0=pxf[:].unsqueeze(3).to_broadcast([128, 8, 16, 16]), in1=iot[:],
                                                     op=ALU.is_equal), reads=[pxf, iot], writes=[s_sb])
                V(lambda e, half=half: e.tensor_tensor(out=prod, in0=eq,
                                                       in1=iif4[:, :, half, :].unsqueeze(2).to_broadcast([128, 8, 16, 16]), op=ALU.mult),
                  reads=[s_sb, iif], writes=[cand])
                V(lambda e, outg=outg: e.tensor_reduce(out=outg[:], in_=prod, axis=AX.X, op=ALU.add), reads=[cand], writes=[outg])
            V(lambda e: e.scalar_tensor_tensor(out=idxf[:].rearrange("p (h k) -> p h k", h=8), in0=i1g[:], scalar=128.0, in1=i2g[:],
                                               op0=ALU.mult, op1=ALU.add), reads=[i1g, i2g], writes=[idxf])
            if li > 0:
                V(lambda e: e.tensor_scalar(out=idxf[:], in0=idxf[:], scalar1=float(li * NEXP), scalar2=None, op0=ALU.add),
                  reads=[idxf], writes=[idxf])
            V(lambda e: e.tensor_copy(out=idxu[:], in_=idxf[:]), reads=[idxf], writes=[idxu])
            V(lambda e: e.tensor_tensor(out=ex[:], in0=ts[:], in1=ts[:, :, 0:1].to_broadcast([128, 8, 16]), op=ALU.subtract),
              reads=[ts], writes=[ex])
            A(lambda e: e.activation(out=ex[:], in_=ex[:], func=AF.Exp), reads=[ex], writes=[ex])
            V(lambda e: e.tensor_reduce(out=zz[:], in_=ex[:], axis=AX.X, op=ALU.add), reads=[ex], writes=[zz])
            V(lambda e: e.reciprocal(out=zz[:], in_=zz[:]), reads=[zz], writes=[zz])
            V(lambda e: e.tensor_tensor(out=gw[:].rearrange("p (h k) -> p h k", h=8), in0=ex[:],
                                        in1=zz[:].unsqueeze(2).to_broadcast([128, 8, 16]), op=ALU.mult), reads=[ex, zz], writes=[gw])
            for j in range(128):
                gb = gbuf[ng % NG]
                ng += 1
                pg.dma(pg.pool, gb[:], U_tab, reads=[S.peer_u, idxu], writes=[gb],
                       indirect=bass.IndirectOffsetOnAxis(ap=idxu[:, j:j + 1], axis=0))
                V(lambda e, gb=gb, j=j: e.scalar_tensor_tensor(out=junk[:], in0=gb[:], scalar=1.0, in1=tb[:], op0=ALU.mult,
                                                              op1=ALU.mult, accum_out=actb[:, j:j + 1]),
                  reads=[gb, tb], writes=[junk, actb])
            V(lambda e: e.tensor_tensor(out=g1b[:], in0=actb[:], in1=actb[:], op=ALU.mult), reads=[actb], writes=[g1b])
            V(lambda e: e.tensor_scalar(out=g1b[:], in0=g1b[:], scalar1=0.044715, scalar2=1.0, op0=ALU.mult, op1=ALU.add),
              reads=[g1b], writes=[g1b])
            V(lambda e: e.tensor_tensor(out=g1b[:], in0=g1b[:], in1=actb[:], op=ALU.mult), reads=[g1b, actb], writes=[g1b])
            A(lambda e: e.activation(out=g2b[:], in_=g1b[:], func=AF.Sigmoid, scale=1.5957691216057308), reads=[g1b], writes=[g2b])
            V(lambda e: e.tensor_tensor(out=g2b[:], in0=g2b[:], in1=actb[:], op=ALU.mult), reads=[g2b, actb], writes=[g2b])
            V(lambda e: e.tensor_tensor(out=wgt[:], in0=g2b[:], in1=gw[:], op=ALU.mult), reads=[g2b, gw], writes=[wgt])
            for j in range(128):
                gb = gbuf[ng % NG]
                ng += 1
                pg.dma(pg.pool, gb[:], V_tab, reads=[S.peer_v, idxu], writes=[gb],
                       indirect=bass.IndirectOffsetOnAxis(ap=idxu[:, j:j + 1], axis=0))
                if j == 0:
                    V(lambda e, gb=gb: e.tensor_scalar(out=acc[:], in0=gb[:], scalar1=wgt[:, 0:1], scalar2=None, op0=ALU.mult),
                      reads=[gb, wgt], writes=[acc])
                else:
                    V(lambda e, gb=gb, j=j: e.scalar_tensor_tensor(out=acc[:], in0=gb[:], scalar=wgt[:, j:j + 1], in1=acc[:],
                                                                  op0=ALU.mult, op1=ALU.add), reads=[gb, wgt, acc], writes=[acc])
            V(lambda e: e.tensor_tensor(out=acc[:], in0=acc[:], in1=bt["g2"][:], op=ALU.mult), reads=[acc, bt["g2"]], writes=[acc])
            V(lambda e: e.scalar_tensor_tensor(out=acc[:], in0=x1[:], scalar=ALPHA, in1=acc[:], op0=ALU.mult, op1=ALU.add),
              reads=[x1, acc], writes=[acc])
            rstd, nmr = ln_stats(pg, acc, lnb)
            A(lambda e: e.activation(out=acc[:], in_=acc[:], func=AF.Identity, bias=nmr[:], scale=rstd[:]), reads=[acc, rstd, nmr], writes=[acc])
            V(lambda e: e.tensor_tensor(out=acc[:], in0=acc[:], in1=bt["gam2"][:], op=ALU.mult), reads=[acc, bt["gam2"]], writes=[acc])
            V(lambda e: e.tensor_tensor(out=acc[:], in0=acc[:], in1=bt["bet2"][:], op=ALU.add), reads=[acc, bt["bet2"]], writes=[acc])
            if last and S.mode == "full":
                pg.dma(pg.act, S.OUT[(tt - 2) * 128:(tt - 1) * 128, :], acc[:], reads=[acc], writes=[S.OUT])
            else:
                pg.dma(pg.act, S.X[rows, :], acc[:], reads=[acc], writes=[S.X])
            if getattr(S, "dbgc", None) is not None and tt == S.dbgc_tile:
                S.dbgc(dict(x1=x1, t=tb, s=None, vv=vv, ii=ii, ts=ts, pos=pos, idxf=idxf, gw=gw, act=actb, wgt=wgt), pc)
        pg.barrier()

def _run(nc, in_maps, ncores):
    return run_bass_kernel_spmd(nc, in_maps, core_ids=list(range(ncores)))


def make_in_maps(inputs, batches, big=True):
    f = lambda a: np.ascontiguousarray(np.asarray(a, dtype=np.float32))
    names = ["w_ada", "b_ada", "w_in", "w_gk2", "b_gk", "hg_lower_bounds", "hg_norm", "gla_norm",
             "w_out", "ln_gamma", "ln_beta", "peer_w_query", "peer_sub_keys"] + (["peer_u", "peer_v"] if big else [])
    shared = {k: f(inputs[k]) for k in names}
    maps = []
    for b in batches:
        m = dict(shared)
        m["x"] = f(inputs["x"][b])
        m["ctx"] = f(inputs["ctx"][b])
        m["c"] = f(inputs["c"][b:b + 1])
        m["c_ctx"] = f(np.asarray(inputs["c_ctx"]).reshape(1, D))
        maps.append(m)
    return maps


def kernel(**inputs):
    nc = build_nc("full")
    maps = make_in_maps(inputs, range(4))
    res = _run(nc, maps, 4)
    return np.stack([r["out"] for r in res.results], axis=0)
```

```python
import numpy as np
from contextlib import ExitStack
import concourse.bass as bass
import concourse.mybir as mybir
from concourse.bass_utils import run_bass_kernel_spmd

F32 = mybir.dt.float32
BF16 = mybir.dt.bfloat16
U32 = mybir.dt.uint32
I32 = mybir.dt.int32
U8 = mybir.dt.uint8
AF = mybir.ActivationFunctionType
ALU = mybir.AluOpType
AX = mybir.AxisListType

D = 1024
SEQ = 8192
CTX = 256
T = SEQ + CTX
NT = T // 128
DEPTH = 4
GRID_W = 64
ROWS = SEQ // GRID_W
DIN = 4128
NEXP = 16384
EPS = 1e-6
ALPHA = (2.0 * DEPTH) ** 0.25
C = 64

O_HQ, O_HFF, O_HFB, O_HI, O_HG = 0, 512, 1024, 1536, 2048
O_GQ, O_GK, O_GV, O_GG, O_GLF, O_GLB = 2560, 2816, 3072, 3584, 4096, 4112


class Buf:
    def __init__(self, t=None, name="", excl=False):
        self.t = t
        self.name = name
        self.w = None
        self.r = {}
        self.excl = excl

    def __getitem__(self, k):
        return self.t[k]


class Eng:
    def __init__(self, h, sem, name):
        self.h = h
        self.sem = sem
        self.name = name
        self.cnt = 0
        self.known = {}
        self.dsems = []
        self.dpos = 0


class Prog:
    def __init__(self, nc, st):
        self.nc = nc
        self.st = st
        mk = lambda h, n: Eng(h, st.enter_context(nc.semaphore("p_" + n)), n)
        self.pe = mk(nc.tensor, "pe")
        self.dve = mk(nc.vector, "dve")
        self.act = mk(nc.scalar, "act")
        self.pool = mk(nc.gpsimd, "pool")
        self.sp = mk(nc.sync, "sp")
        self.engs = [self.pe, self.dve, self.act, self.pool, self.sp]
        self.semcnt = {}
        for e, n in ((self.sp, 20), (self.pool, 20), (self.act, 4)):
            for i in range(n):
                s = st.enter_context(nc.semaphore(f"d_{e.name}{i}"))
                e.dsems.append(s)
                self.semcnt[s] = 0
        self.nbuf = 0

    def sb(self, shape, dtype, name=None, st=None):
        self.nbuf += 1
        name = f"{name or 'sb'}_{self.nbuf}"
        t = (st or self.st).enter_context(self.nc.sbuf_tensor(name, list(shape), dtype))
        return Buf(t, name)

    def ps(self, shape, dtype, name=None, st=None):
        self.nbuf += 1
        name = f"{name or 'ps'}_{self.nbuf}"
        t = (st or self.st).enter_context(self.nc.psum_tensor(name, list(shape), dtype))
        return Buf(t, name, excl=True)

    def dram(self, name, shape, dtype, kind="Internal"):
        t = self.nc.dram_tensor(name, list(shape), dtype, kind=kind)
        return Buf(t.ap(), name)

    def _deps(self, eng, reads, writes):
        deps = []
        for b in reads:
            if b.w is not None:
                deps.append(b.w)
            if b.excl:
                for s, v in b.r.items():
                    if s is not eng.sem:
                        deps.append((s, v))
        for b in writes:
            if b.w is not None and b.w[0] is not eng.sem:
                deps.append(b.w)
            for s, v in b.r.items():
                if s is not eng.sem:
                    deps.append((s, v))
        return deps

    def _wait(self, eng, deps):
        for s, v in deps:
            if eng.known.get(s, 0) < v:
                eng.h.wait_ge(s, v)
                eng.known[s] = v

    def _mark(self, tok, reads, writes):
        for b in reads:
            if b.r.get(tok[0], 0) < tok[1]:
                b.r[tok[0]] = tok[1]
        for b in writes:
            b.w = tok
            b.r = {}

    def op(self, eng, fn, reads=(), writes=()):
        self._wait(eng, self._deps(eng, reads, writes))
        inst = fn(eng.h)
        eng.cnt += 1
        inst.then_inc(eng.sem, 1)
        self._mark((eng.sem, eng.cnt), reads, writes)

    def dma(self, q, out, in_, reads=(), writes=(), indirect=None, **kw):
        s = q.dsems[q.dpos % len(q.dsems)]
        q.dpos += 1
        deps = self._deps(q, reads, writes)
        if self.semcnt[s] > 0:
            deps.append((s, self.semcnt[s]))
        self._wait(q, deps)
        if indirect is not None:
            inst = q.h.indirect_dma_start(out=out, out_offset=None, in_=in_, in_offset=indirect, **kw)
        else:
            inst = q.h.dma_start(out=out, in_=in_, **kw)
        self.semcnt[s] += 16
        inst.then_inc(s, 16)
        self._mark((s, self.semcnt[s]), reads, writes)

    def barrier(self):
        toks = [(e.sem, e.cnt) for e in self.engs if e.cnt > 0]
        toks += [(s, v) for s, v in self.semcnt.items() if v > 0]
        for e in self.engs:
            self._wait(e, [t for t in toks if t[0] is not e.sem or True])

    def V(self, fn, reads=(), writes=()):
        self.op(self.dve, fn, reads, writes)

    def A(self, fn, reads=(), writes=()):
        self.op(self.act, fn, reads, writes)

    def G(self, fn, reads=(), writes=()):
        self.op(self.pool, fn, reads, writes)

    def P(self, fn, reads=(), writes=()):
        self.op(self.pe, fn, reads, writes)


class NS:
    pass


def build_nc(mode="full", nlayers=DEPTH):
    nc = bass.Bass("TRN2", target_bir_lowering=False)
    with ExitStack() as st:
        _emit(nc, st, mode, nlayers)
    return nc


def _emit(nc, st, mode, nlayers):
    pg = Prog(nc, st)
    S = NS()
    S.nc, S.pg, S.mode = nc, pg, mode
    din = lambda n, s: Buf(nc.dram_tensor(n, list(s), F32, kind="ExternalInput").ap(), n)
    S.x_in = din("x", (SEQ, D))
    S.ctx_in = din("ctx", (CTX, D))
    S.c_in = din("c", (1, D))
    S.cctx_in = din("c_ctx", (1, D))
    S.w_ada = din("w_ada", (DEPTH, D, 6 * D))
    S.b_ada = din("b_ada", (DEPTH, 6 * D))
    S.w_in = din("w_in", (DEPTH, D, DIN))
    S.w_gk2 = din("w_gk2", (DEPTH, 2, 16, 256))
    S.b_gk = din("b_gk", (DEPTH, 2, 256))
    S.hg_lb = din("hg_lower_bounds", (2, DEPTH, 512))
    S.hg_norm = din("hg_norm", (DEPTH, 128))
    S.gla_norm = din("gla_norm", (DEPTH, 128))
    S.w_out = din("w_out", (DEPTH, D, D))
    S.ln_gamma = din("ln_gamma", (DEPTH, 2, D))
    S.ln_beta = din("ln_beta", (DEPTH, 2, D))
    S.pwq = din("peer_w_query", (DEPTH, D, 2048))
    S.psk = din("peer_sub_keys", (DEPTH, 2, 128, 128))
    if mode in ("full", "C", "chain"):
        S.peer_u = din("peer_u", (DEPTH, NEXP, D))
        S.peer_v = din("peer_v", (DEPTH, NEXP, D))

    S.MOD = pg.dram("s_mod", (DEPTH, 2, 6 * D), F32)
    S.OML = pg.dram("s_oml", (2, DEPTH, 512), F32)
    ukind = {"A": "ExternalOutput", "B": "ExternalInput", "C": "ExternalInput"}.get(mode, "Internal")
    S.U = pg.dram("s_u", (T, DIN), F32, kind=ukind)
    okind = {"B": "ExternalOutput", "C": "ExternalInput"}.get(mode, "Internal")
    S.OF = pg.dram("s_of", (T, D), F32, kind=okind)
    S.OB = pg.dram("s_ob", (T, D), F32, kind=okind)
    S.X = pg.dram("s_x", (T, D), F32, kind=("ExternalOutput" if mode in ("C", "chain") else "Internal"))
    if mode == "full":
        S.OUT = pg.dram("out", (SEQ, D), F32, kind="ExternalOutput")
    if hasattr(S, "peer_u"):
        S.UB = pg.dram("s_ub", (DEPTH * NEXP, D), BF16)
        S.VB = pg.dram("s_vb", (DEPTH * NEXP, D), BF16)

    S.ident = pg.sb((128, 128), F32, "ident")
    S.ones = pg.sb((128, 128), F32, "ones")
    pg.G(lambda e: e.memset(S.ones[:], 1.0), writes=[S.ones])
    pg.G(lambda e: e.affine_select(out=S.ident[:], in_=S.ones[:], pattern=[[1, 128]], compare_op=ALU.is_equal,
                                   fill=0.0, base=0, channel_multiplier=-1), reads=[S.ones], writes=[S.ident])

    if mode in ("full", "A", "C", "chain"):
        prologue_mod(S)
    if hasattr(S, "peer_u"):
        convert_tables(S, nlayers)
    if mode in ("full", "B", "chain"):
        prologue_lb(S)
        scan_consts(S)
    pg.barrier()
    for li in range(nlayers):
        if mode in ("full", "A", "chain"):
            phase_a(S, li)
            pg.barrier()
        if mode in ("full", "B", "chain"):
            phase_b(S, li)
            pg.barrier()
        if mode in ("full", "C", "chain"):
            phase_c(S, li)
            pg.barrier()
    pg.barrier()


def prologue_mod(S):
    pg = S.pg
    with ExitStack() as ps_:
        cc = pg.sb((128, 8, 2), F32, "cc", ps_)
        scc = pg.sb((128, 8, 2), F32, "scc", ps_)
        sig = pg.sb((128, 8, 2), F32, "sig", ps_)
        pg.dma(pg.sp, cc[:, :, 0], S.c_in[0, :].rearrange("(c p) -> p c", p=128), reads=[S.c_in], writes=[cc],
               allow_slow_non_contiguous=True)
        pg.dma(pg.sp, cc[:, :, 1], S.cctx_in[0, :].rearrange("(c p) -> p c", p=128), reads=[S.cctx_in], writes=[cc],
               allow_slow_non_contiguous=True)
        pg.A(lambda e: e.activation(out=sig[:], in_=cc[:], func=AF.Sigmoid), reads=[cc], writes=[sig])
        pg.V(lambda e: e.tensor_tensor(out=scc[:], in0=cc[:], in1=sig[:], op=ALU.mult), reads=[cc, sig], writes=[scc])
        wblk = [pg.sb((128, 8, 512), F32, f"wblk{i}", ps_) for i in range(2)]
        msb = pg.sb((2, 6 * D), F32, "msb", ps_)
        bsb = pg.sb((2, 6 * D), F32, "bsb", ps_)
        pm = [pg.ps((128, 512), F32, f"pm{i}", ps_) for i in range(2)]
        k = 0
        for li in range(DEPTH):
            pg.dma(pg.sp, bsb[:], S.b_ada[li, :].partition_broadcast(2), reads=[S.b_ada], writes=[bsb])
            for j in range(12):
                wb = wblk[k % 2]
                pmj = pm[k % 2]
                k += 1
                pg.dma(pg.sp, wb[:], S.w_ada[li, :, j * 512:(j + 1) * 512].rearrange("(c p) n -> p c n", p=128),
                       reads=[S.w_ada], writes=[wb])
                for ch in range(8):
                    pg.P(lambda e, ch=ch, wb=wb, pmj=pmj: e.matmul(pmj[0:2, :], lhsT=scc[:, ch, :], rhs=wb[:, ch, :],
                                                                  start=(ch == 0), stop=(ch == 7)),
                         reads=[scc, wb], writes=[pmj])
                pg.V(lambda e, j=j, pmj=pmj: e.tensor_tensor(out=msb[:, j * 512:(j + 1) * 512], in0=pmj[0:2, :],
                                                             in1=bsb[:, j * 512:(j + 1) * 512], op=ALU.add),
                     reads=[pmj, bsb], writes=[msb])
            pg.dma(pg.sp, S.MOD[li], msb[:], reads=[msb], writes=[S.MOD])
        pg.barrier()


def convert_tables(S, nlayers):
    pg = S.pg
    with ExitStack() as cs:
        fin = [pg.sb((128, 4096), F32, f"cv_in{i}", cs) for i in range(3)]
        fout = [pg.sb((128, 4096), BF16, f"cv_out{i}", cs) for i in range(3)]
        k = 0
        for src, dst in ((S.peer_u, S.UB), (S.peer_v, S.VB)):
            for li in range(nlayers):
                for c in range(NEXP // 512):
                    fi, fo = fin[k % 3], fout[k % 3]
                    pg.dma(pg.sp, fi[:].rearrange("p (a d) -> p a d", a=4),
                           src[li, c * 512:(c + 1) * 512, :].rearrange("(p a) d -> p a d", a=4), reads=[src], writes=[fi])
                    if k % 2 == 0:
                        pg.A(lambda e, fi=fi, fo=fo: e.activation(out=fo[:], in_=fi[:], func=AF.Copy), reads=[fi], writes=[fo])
                    else:
                        pg.V(lambda e, fi=fi, fo=fo: e.tensor_copy(out=fo[:], in_=fi[:]), reads=[fi], writes=[fo])
                    r0 = li * NEXP + c * 512
                    pg.dma(pg.pool, dst[r0:r0 + 512, :].rearrange("(p a) d -> p a d", a=4),
                           fo[:].rearrange("p (a d) -> p a d", a=4), reads=[fo], writes=[])
                    k += 1
        pg.barrier()


def prologue_lb(S):
    pg = S.pg
    with ExitStack() as ps_:
        x = pg.sb((128, 2, 4, 4), F32, "lb_x", ps_)
        mx = pg.sb((128, 2, 4), F32, "lb_mx", ps_)
        sm = pg.sb((128, 2, 4), F32, "lb_sm", ps_)
        oml = pg.sb((128, 2, 4, 4), F32, "lb_oml", ps_)
        for d in range(2):
            for l in range(4):
                pg.dma(pg.sp, x[:, d, l, :], S.hg_lb[d, l, :].rearrange("(h p) -> p h", p=128), reads=[S.hg_lb], writes=[x],
                       allow_slow_non_contiguous=True)
        V = pg.V
        V(lambda e: e.tensor_tensor(out=mx[:], in0=x[:, :, 0, :], in1=x[:, :, 1, :], op=ALU.max), reads=[x], writes=[mx])
        for l in (2, 3):
            V(lambda e, l=l: e.tensor_tensor(out=mx[:], in0=mx[:], in1=x[:, :, l, :], op=ALU.max), reads=[x, mx], writes=[mx])
        for l in range(4):
            V(lambda e, l=l: e.tensor_tensor(out=x[:, :, l, :], in0=x[:, :, l, :], in1=mx[:], op=ALU.subtract),
              reads=[x, mx], writes=[x])
        pg.A(lambda e: e.activation(out=x[:], in_=x[:], func=AF.Exp), reads=[x], writes=[x])
        V(lambda e: e.tensor_tensor(out=sm[:], in0=x[:, :, 0, :], in1=x[:, :, 1, :], op=ALU.add), reads=[x], writes=[sm])
        for l in (2, 3):
            V(lambda e, l=l: e.tensor_tensor(out=sm[:], in0=sm[:], in1=x[:, :, l, :], op=ALU.add), reads=[x, sm], writes=[sm])
        V(lambda e: e.reciprocal(out=sm[:], in_=sm[:]), reads=[sm], writes=[sm])
        for l in range(4):
            V(lambda e, l=l: e.tensor_tensor(out=x[:, :, l, :], in0=x[:, :, l, :], in1=sm[:], op=ALU.mult),
              reads=[x, sm], writes=[x])
        for l in (2, 3):
            V(lambda e, l=l: e.tensor_tensor(out=x[:, :, l, :], in0=x[:, :, l, :], in1=x[:, :, l - 1, :], op=ALU.add),
              reads=[x], writes=[x])
        pg.V(lambda e: e.memset(oml[:], 1.0), writes=[oml])
        for l in (1, 2, 3):
            V(lambda e, l=l: e.tensor_scalar(out=x[:, :, l, :], in0=x[:, :, l, :], scalar1=1.0 - 1e-6, scalar2=0.0,
                                             op0=ALU.min, op1=ALU.max), reads=[x], writes=[x])
            V(lambda e, l=l: e.tensor_scalar(out=oml[:, :, l, :], in0=x[:, :, l, :], scalar1=-1.0, scalar2=1.0,
                                             op0=ALU.mult, op1=ALU.add), reads=[x], writes=[oml])
        for d in range(2):
            for l in range(4):
                pg.dma(pg.sp, S.OML[d, l, :].rearrange("(h p) -> p h", p=128), oml[:, d, l, :], reads=[oml], writes=[S.OML],
                       allow_slow_non_contiguous=True)
        pg.barrier()


def scan_consts(S):
    pg = S.pg
    G = pg.G
    ones = S.ones
    le = pg.sb((C, C), F32, "c_le")
    ge = pg.sb((C, C), F32, "c_ge")
    bmf = pg.sb((C, C + 2), F32, "c_bmf")
    bmb = pg.sb((C, C + 2), F32, "c_bmb")
    G(lambda e: e.affine_select(out=le[:], in_=ones[0:C, 0:C], pattern=[[1, C]], compare_op=ALU.is_ge, fill=0.0,
                                base=0, channel_multiplier=-1), reads=[ones], writes=[le])
    G(lambda e: e.affine_select(out=ge[:], in_=ones[0:C, 0:C], pattern=[[-1, C]], compare_op=ALU.is_ge, fill=0.0,
                                base=0, channel_multiplier=1), reads=[ones], writes=[ge])
    G(lambda e: e.affine_select(out=bmf[:], in_=ones[0:C, 0:C + 2], pattern=[[0, C + 2]], compare_op=ALU.is_ge, fill=0.0,
                                base=C // 2 - 1, channel_multiplier=-1), reads=[ones], writes=[bmf])
    G(lambda e: e.affine_select(out=bmb[:], in_=ones[0:C, 0:C + 2], pattern=[[0, C + 2]], compare_op=ALU.is_ge, fill=0.0,
                                base=-(C // 2), channel_multiplier=1), reads=[ones], writes=[bmb])
    S.tm = {}
    for key, bm, tri in (("f", bmf, le), ("b", bmb, ge)):
        for scale, tag in ((1.0, "h"), (-1.0 / 16.0, "g")):
            tm = pg.sb((C, C + 2), F32, f"c_tm{key}{tag}")
            pg.V(lambda e, tm=tm, bm=bm: e.tensor_copy(out=tm[:], in_=bm[:]), reads=[bm], writes=[tm])
            pg.V(lambda e, tm=tm, tri=tri: e.tensor_tensor(out=tm[:, 0:C], in0=tm[:, 0:C], in1=tri[:], op=ALU.subtract),
                 reads=[tm, tri], writes=[tm])
            if scale != 1.0:
                pg.V(lambda e, tm=tm, scale=scale: e.tensor_scalar(out=tm[:], in0=tm[:], scalar1=scale, scalar2=None, op0=ALU.mult),
                     reads=[tm], writes=[tm])
            S.tm[key + tag] = tm
    S.mask = {}
    for key, tri in (("f", le), ("b", ge)):
        m = pg.sb((C, 2, C), F32, f"c_mask{key}")
        for g in range(2):
            pg.V(lambda e, m=m, g=g, tri=tri: e.tensor_copy(out=m[:, g, :], in_=tri[:]), reads=[tri], writes=[m])
        S.mask[key] = m


def phase_a(S, li):
    pg = S.pg
    ident = S.ident

    def tile_src(tt):
        if li == 0:
            if tt < 2:
                return S.ctx_in, S.ctx_in[tt * 128:(tt + 1) * 128, :]
            return S.x_in, S.x_in[(tt - 2) * 128:(tt - 1) * 128, :]
        return S.X, S.X[tt * 128:(tt + 1) * 128, :]

    with ExitStack() as pa:
        wsb = pg.sb((128, 8, DIN), BF16, "w_in_sb", pa)
        for q4 in range(4):
            pg.dma(pg.pool, wsb[:, :, q4 * 1032:(q4 + 1) * 1032],
                   S.w_in[li, :, q4 * 1032:(q4 + 1) * 1032].rearrange("(c p) n -> p c n", p=128),
                   reads=[S.w_in], writes=[wsb])
        shv = pg.sb((128, 2, 8), F32, "shv", pa)
        scv = pg.sb((128, 2, 8), F32, "scv", pa)
        for r in range(2):
            pg.dma(pg.sp, shv[:, r, :], S.MOD[li, r, 0:D].rearrange("(c p) -> p c", p=128), reads=[S.MOD], writes=[shv],
                   allow_slow_non_contiguous=True)
            pg.dma(pg.sp, scv[:, r, :], S.MOD[li, r, D:2 * D].rearrange("(c p) -> p c", p=128), reads=[S.MOD], writes=[scv],
                   allow_slow_non_contiguous=True)
        pg.V(lambda e: e.tensor_scalar(out=scv[:], in0=scv[:], scalar1=1.0, scalar2=None, op0=ALU.add), reads=[scv], writes=[scv])
        xts = [pg.sb((128, D), F32, f"a_x{i}", pa) for i in range(2)]
        xns = [pg.sb((128, D), F32, f"a_xn{i}", pa) for i in range(2)]
        xmT = [pg.sb((128, 8, 128), BF16, f"a_xmT{i}", pa) for i in range(2)]
        uts = [pg.sb((128, DIN), F32, f"a_u{i}", pa) for i in range(2)]
        lnb = [ln_bufs(pg, pa, f"a{i}") for i in range(2)]
        psT = [pg.ps((128, 512), F32, f"a_psT{i}", pa) for i in range(2)]
        psU = [pg.ps((128, 512), F32, f"a_psU{i}", pa) for i in range(4)]
        nu = 0
        for tt in range(NT):
            i2 = tt % 2
            r = 1 if tt < 2 else 0
            xt, xn, xm, ut = xts[i2], xns[i2], xmT[i2], uts[i2]
            srcb, srcap = tile_src(tt)
            pg.dma(pg.sp, xt[:], srcap, reads=[srcb], writes=[xt])
            rstd, nmr = ln_stats(pg, xt, lnb[i2])
            pg.A(lambda e, xt=xt, xn=xn, rstd=rstd, nmr=nmr: e.activation(out=xn[:], in_=xt[:], func=AF.Identity,
                                                                         bias=nmr[:], scale=rstd[:]),
                 reads=[xt, rstd, nmr], writes=[xn])
            for hb in range(2):
                pT = psT[hb]
                for c4 in range(4):
                    ch = hb * 4 + c4
                    pg.P(lambda e, pT=pT, c4=c4, ch=ch, xn=xn: e.transpose(out=pT[:, c4 * 128:(c4 + 1) * 128],
                                                                          in_=xn[:, ch * 128:(ch + 1) * 128], identity=ident[:]),
                         reads=[xn, ident], writes=[pT])
                for c4 in range(4):
                    ch = hb * 4 + c4
                    pg.A(lambda e, pT=pT, c4=c4, ch=ch, xm=xm, r=r: e.activation(
                        out=xm[:, ch, :], in_=pT[:, c4 * 128:(c4 + 1) * 128], func=AF.Identity,
                        bias=shv[:, r, ch:ch + 1], scale=scv[:, r, ch:ch + 1]),
                         reads=[pT, shv, scv], writes=[xm])
            for j in range(9):
                c0 = j * 512
                cw = min(512, DIN - c0)
                pu = psU[nu % 4]
                nu += 1
                for ch in range(8):
                    pg.P(lambda e, pu=pu, ch=ch, c0=c0, cw=cw, xm=xm: e.matmul(pu[:, 0:cw], lhsT=xm[:, ch, :],
                                                                              rhs=wsb[:, ch, c0:c0 + cw],
                                                                              start=(ch == 0), stop=(ch == 7)),
                         reads=[xm, wsb], writes=[pu])
                pg.V(lambda e, pu=pu, c0=c0, cw=cw, ut=ut: e.tensor_copy(out=ut[:, c0:c0 + cw], in_=pu[:, 0:cw]),
                     reads=[pu], writes=[ut])
            pg.dma(pg.pool, S.U[tt * 128:(tt + 1) * 128, :], ut[:], reads=[ut], writes=[S.U])
        pg.barrier()


def ln_bufs(pg, st, tag):
    b = NS()
    b.stp = pg.sb((128, 2, 6), F32, f"ln_st{tag}", st)
    b.mv = pg.sb((128, 2), F32, f"ln_mv{tag}", st)
    b.rstd = pg.sb((128, 1), F32, f"ln_rs{tag}", st)
    b.nmr = pg.sb((128, 1), F32, f"ln_nm{tag}", st)
    return b


def ln_stats(pg, xt, b):
    for h in range(2):
        pg.V(lambda e, h=h: e.bn_stats(out=b.stp[:, h, :], in_=xt[:, h * 512:(h + 1) * 512]), reads=[xt], writes=[b.stp])
    pg.V(lambda e: e.bn_aggr(out=b.mv[:], in_=b.stp[:].rearrange("p a b -> p (a b)")), reads=[b.stp], writes=[b.mv])
    pg.A(lambda e: e.activation(out=b.rstd[:], in_=b.mv[:, 1:2], func=AF.Sqrt, bias=EPS, scale=1.0), reads=[b.mv], writes=[b.rstd])
    pg.V(lambda e: e.reciprocal(out=b.rstd[:], in_=b.rstd[:]), reads=[b.rstd], writes=[b.rstd])
    pg.V(lambda e: e.scalar_tensor_tensor(out=b.nmr[:], in0=b.mv[:, 0:1], scalar=-1.0, in1=b.rstd[:], op0=ALU.mult, op1=ALU.mult),
         reads=[b.mv, b.rstd], writes=[b.nmr])
    return b.rstd, b.nmr


def group_rows(ap2d, unit_is_gla, j):
    if j < 2:
        return ap2d[j * 128:(j + 1) * 128, :].rearrange("(g t) w -> t g w", t=C)
    jj = j - 2
    if not unit_is_gla:
        return ap2d[CTX + jj * 128:CTX + (jj + 1) * 128, :].rearrange("(g t) w -> t g w", t=C)
    return ap2d[CTX:, :].rearrange("(r c) w -> c r w", c=GRID_W)[jj].rearrange("(g t) w -> t g w", t=C)


def phase_b(S, li):
    pg = S.pg
    V, A, G, P = pg.V, pg.A, pg.G, pg.P
    ident = S.ident
    fwd_groups = list(range(NT))
    bwd_groups = [1, 0] + list(range(NT - 1, 1, -1))
    with ExitStack() as pb:
        pre = [pg.ps((128, 512), F32, f"b_pre{i}", pb) for i in range(2)]
        tok = [pg.ps((128, 512), F32, f"b_tok{i}", pb) for i in range(2)]
        obk = [pg.ps((128, 512), F32, f"b_o{i}", pb) for i in range(2)]
        dsb = [pg.ps((128, 512), F32, f"b_ds{i}", pb) for i in range(2)]
        NB = 2
        mk = lambda shape, dt, nm: [pg.sb(shape, dt, f"b_{nm}{i}", pb) for i in range(NB)]
        qg = mk((C, 2, 128), F32, "qg")
        zg = mk((C, 2, 128), F32, "zg")
        vg = mk((C, 2, 128), F32, "vg")
        glg = mk((C, 2, 17), F32, "glg")
        kg = mk((C, 2, 128), F32, "kg")
        la = mk((C, 2, 128), F32, "la")
        vb = mk((C, 2, 128), BF16, "vb")
        ET = mk((128, 2, C), F32, "ET")
        EiT = mk((128, 2, C), F32, "EiT")
        gm = mk((128, 2, 2), F32, "gm")
        Etk = mk((C, 2, 128), F32, "Etk")
        KdT = mk((128, 2, C), BF16, "KdT")
        QdT = mk((128, 2, C), BF16, "QdT")
        Kd = mk((C, 2, 128), BF16, "Kd")
        scTd = {"f": mk((C, 2, C), BF16, "scTf"), "b": mk((C, 2, C), BF16, "scTb")}
        glT = mk((17, 2, C), F32, "glT")
        osb = mk((C, 2, 128), F32, "osb")
        for b_ in scTd["f"] + scTd["b"]:
            G(lambda e, b_=b_: e.memset(b_[:], 0.0), writes=[b_])
        for b_ in glg:
            G(lambda e, b_=b_: e.memset(b_[:], 1.0), writes=[b_])
        Sst = pg.sb((128, 128), F32, "b_S", pb)
        Stmp = pg.sb((128, 128), F32, "b_Stmp", pb)
        Ssc = [pg.sb((128, 128), BF16, f"b_Ssc{i}", pb) for i in range(2)]
        omlr = pg.sb((C, 2, 128), F32, "b_omlr", pb)
        wgk = pg.sb((17, 256), F32, "b_wgk", pb)
        it = 0
        for unit in getattr(S, 'units', range(8)):
            gla = unit >= 4
            h = unit % 4
            dk = 64 if gla else 128
            ocol = unit * 128
            for di, dkey in enumerate("fb"):
                fwd = di == 0
                tm = S.tm[dkey + ("g" if gla else "h")]
                mask = S.mask[dkey]
                endc = C - 1 if fwd else 0
                OD = S.OF if fwd else S.OB
                if gla:
                    pg.dma(pg.sp, wgk[0:16, :], S.w_gk2[li, di], reads=[S.w_gk2], writes=[wgk])
                    pg.dma(pg.sp, wgk[16:17, :], S.b_gk[li, di:di + 1, :], reads=[S.b_gk], writes=[wgk])
                else:
                    for g in range(2):
                        pg.dma(pg.sp, omlr[:, g, :], S.OML[di, li, h * 128:(h + 1) * 128].partition_broadcast(C),
                               reads=[S.OML], writes=[omlr])
                G(lambda e: e.memset(Sst[:], 0.0), writes=[Sst])
                nds = 0
                for j in (fwd_groups if fwd else bwd_groups):
                    i2 = it % NB
                    it += 1
                    q_, z_, v_, gl_, k_, la_, vb_ = qg[i2], zg[i2], vg[i2], glg[i2], kg[i2], la[i2], vb[i2]
                    ET_, EiT_, gm_, Etk_, KdT_, QdT_, Kd_, scT_, glT_, osb_ = (ET[i2], EiT[i2], gm[i2], Etk[i2], KdT[i2],
                                                                             QdT[i2], Kd[i2], scTd[dkey][i2], glT[i2], osb[i2])
                    pre_, tok_, o_ = pre[i2], tok[i2], obk[i2]
                    uview = lambda c0, w: group_rows(S.U[:, c0:c0 + w], gla, j)
                    if gla:
                        pg.dma(pg.sp, q_[:, :, 0:64], uview(O_GQ + h * 64, 64), reads=[S.U], writes=[q_])
                        pg.dma(pg.sp, k_[:, :, 0:64], uview(O_GK + h * 64, 64), reads=[S.U], writes=[k_])
                        pg.dma(pg.sp, v_[:], uview(O_GV + h * 128, 128), reads=[S.U], writes=[v_])
                        pg.dma(pg.sp, gl_[:, :, 0:16], uview(O_GLF if fwd else O_GLB, 16), reads=[S.U], writes=[gl_])
                    else:
                        pg.dma(pg.sp, q_[:], uview(O_HQ + h * 128, 128), reads=[S.U], writes=[q_])
                        pg.dma(pg.sp, z_[:], uview((O_HFF if fwd else O_HFB) + h * 128, 128), reads=[S.U], writes=[z_])
                        pg.dma(pg.sp, v_[:], uview(O_HI + h * 128, 128), reads=[S.U], writes=[v_])
                    if gla:
                        for g in range(2):
                            P(lambda e, g=g: e.transpose(out=tok_[0:17, 384 + g * C:384 + (g + 1) * C], in_=gl_[:, g, :],
                                                         identity=ident[0:C, 0:C]), reads=[gl_, ident], writes=[tok_])
                        V(lambda e: e.tensor_copy(out=glT_[:].rearrange("p g t -> p (g t)"), in_=tok_[0:17, 384:512]),
                          reads=[tok_], writes=[glT_])
                        for g in range(2):
                            P(lambda e, g=g: e.matmul(pre_[0:C, 260 + g * 64:260 + (g + 1) * 64], lhsT=glT_[:, g, :],
                                                      rhs=wgk[:, h * 64:(h + 1) * 64], start=True, stop=True),
                              reads=[glT_, wgk], writes=[pre_])
                        A(lambda e: e.activation(out=la_[:, :, 0:64], in_=pre_[0:C, 260:388].rearrange("p (g k) -> p g k", g=2),
                                                 func=AF.Exp, scale=-1.0), reads=[pre_], writes=[la_])
                        A(lambda e: e.activation(out=la_[:, :, 0:64], in_=la_[:, :, 0:64], func=AF.Ln, bias=1.0, scale=1.0),
                          reads=[la_], writes=[la_])
                    else:
                        A(lambda e: e.activation(out=k_[:], in_=z_[:], func=AF.Sigmoid, scale=-1.0), reads=[z_], writes=[k_])
                        V(lambda e: e.tensor_tensor(out=k_[:], in0=k_[:], in1=omlr[:], op=ALU.mult), reads=[k_, omlr], writes=[k_])
                        A(lambda e: e.activation(out=la_[:], in_=k_[:], func=AF.Ln, bias=1.0, scale=-1.0), reads=[k_], writes=[la_])
                    G(lambda e: e.tensor_copy(out=vb_[:], in_=v_[:]), reads=[v_], writes=[vb_])
                    for g in range(2):
                        P(lambda e, g=g: e.matmul(pre_[0:dk, g * 66:(g + 1) * 66], lhsT=la_[:, g, 0:dk], rhs=tm[:],
                                                  start=True, stop=True), reads=[la_, tm], writes=[pre_])
                        P(lambda e, g=g: e.matmul(pre_[0:C, 132 + g * dk:132 + (g + 1) * dk], lhsT=tm[:, 0:C], rhs=la_[:, g, 0:dk],
                                                  start=True, stop=True), reads=[la_, tm], writes=[pre_])
                    for g in range(2):
                        P(lambda e, g=g: e.transpose(out=tok_[0:dk, g * C:(g + 1) * C], in_=k_[:, g, 0:dk],
                                                     identity=ident[0:C, 0:C]), reads=[k_, ident], writes=[tok_])
                        P(lambda e, g=g: e.transpose(out=tok_[0:dk, 128 + g * C:128 + (g + 1) * C], in_=q_[:, g, 0:dk],
                                                     identity=ident[0:C, 0:C]), reads=[q_, ident], writes=[tok_])
                    dtv = pre_[0:dk, 0:132].rearrange("p (g c) -> p g c", g=2)
                    A(lambda e: e.activation(out=ET_[0:dk], in_=dtv[:, :, 0:C], func=AF.Exp), reads=[pre_], writes=[ET_])
                    A(lambda e: e.activation(out=EiT_[0:dk], in_=dtv[:, :, 0:C], func=AF.Exp, scale=-1.0), reads=[pre_], writes=[EiT_])
                    A(lambda e: e.activation(out=gm_[0:dk, :, 0:1], in_=dtv[:, :, C:C + 1], func=AF.Exp), reads=[pre_], writes=[gm_])
                    V(lambda e: e.tensor_tensor(out=gm_[0:dk, :, 1:2], in0=gm_[0:dk, :, 0:1], in1=EiT_[0:dk, :, endc:endc + 1],
                                                op=ALU.mult), reads=[gm_, EiT_], writes=[gm_])
                    A(lambda e: e.activation(out=Etk_[:, :, 0:dk], in_=pre_[0:C, 132:132 + 2 * dk].rearrange("p (g k) -> p g k", g=2),
                                             func=AF.Exp), reads=[pre_], writes=[Etk_])
                    V(lambda e: e.tensor_tensor(out=KdT_[0:dk], in0=tok_[0:dk, 0:128].rearrange("p (g c) -> p g c", g=2),
                                                in1=ET_[0:dk], op=ALU.mult), reads=[tok_, ET_], writes=[KdT_])
                    qsc = 0.125 if gla else 1.0
                    V(lambda e: e.scalar_tensor_tensor(out=QdT_[0:dk], in0=tok_[0:dk, 128:256].rearrange("p (g c) -> p g c", g=2),
                                                       scalar=qsc, in1=EiT_[0:dk], op0=ALU.mult, op1=ALU.mult),
                      reads=[tok_, EiT_], writes=[QdT_])
                    V(lambda e: e.tensor_tensor(out=Kd_[:, :, 0:dk], in0=k_[:, :, 0:dk], in1=Etk_[:, :, 0:dk], op=ALU.mult),
                      reads=[k_, Etk_], writes=[Kd_])
                    sco = 256
                    for g in range(2):
                        P(lambda e, g=g: e.matmul(tok_[0:C, sco + g * C:sco + (g + 1) * C], lhsT=KdT_[0:dk, g, :],
                                                  rhs=QdT_[0:dk, g, :], start=True, stop=True),
                          reads=[KdT_, QdT_], writes=[tok_])
                    V(lambda e: e.copy_predicated(out=scT_[:], mask=mask[:].bitcast(U32),
                                                  data=tok_[0:C, sco:sco + 2 * C].rearrange("p (g c) -> p g c", g=2)),
                      reads=[tok_, mask], writes=[scT_])
                    for g in ((0, 1) if fwd else (1, 0)):
                        ssc = Ssc[nds % 2]
                        dsp = dsb[nds % 2]
                        nds += 1
                        A(lambda e, g=g, ssc=ssc: e.activation(out=ssc[0:dk, :], in_=Sst[0:dk, :], func=AF.Identity,
                                                               scale=gm_[0:dk, g, 0:1]), reads=[Sst, gm_], writes=[ssc])
                        P(lambda e, g=g: e.matmul(o_[0:C, g * 128:(g + 1) * 128], lhsT=scT_[:, g, :], rhs=vb_[:, g, :],
                                                  start=True, stop=False), reads=[scT_, vb_], writes=[o_])
                        P(lambda e, g=g, ssc=ssc: e.matmul(o_[0:C, g * 128:(g + 1) * 128], lhsT=QdT_[0:dk, g, :], rhs=ssc[0:dk, :],
                                                           start=False, stop=True), reads=[QdT_, ssc], writes=[o_])
                        P(lambda e, g=g, dsp=dsp: e.matmul(dsp[0:dk, 0:128], lhsT=Kd_[:, g, 0:dk], rhs=vb_[:, g, :],
                                                           start=True, stop=True), reads=[Kd_, vb_], writes=[dsp])
                        V(lambda e, g=g: e.tensor_scalar(out=Stmp[0:dk, :], in0=Sst[0:dk, :], scalar1=gm_[0:dk, g, 1:2], scalar2=None,
                                                         op0=ALU.mult), reads=[Sst, gm_], writes=[Stmp])
                        V(lambda e, g=g, dsp=dsp: e.scalar_tensor_tensor(out=Sst[0:dk, :], in0=dsp[0:dk, 0:128],
                                                                         scalar=EiT_[0:dk, g, endc:endc + 1], in1=Stmp[0:dk, :],
                                                                         op0=ALU.mult, op1=ALU.add),
                          reads=[dsp, EiT_, Stmp], writes=[Sst])
                    A(lambda e: e.activation(out=osb_[:], in_=o_[0:C, 0:256].rearrange("p (g v) -> p g v", g=2), func=AF.Copy),
                      reads=[o_], writes=[osb_])
                    pg.dma(pg.pool, group_rows(OD[:, ocol:ocol + 128], gla, j), osb_[:], reads=[osb_], writes=[OD])
                    if getattr(S, "dbgfn", None) is not None:
                        S.dbgfn(dict(k=k_, la=la_, ET=ET_, EiT=EiT_, gm=gm_, Etk=Etk_, KdT=KdT_, QdT=QdT_, Kd=Kd_, scT=scT_,
                                     osb=osb_, S=Sst, vb=vb_, q=q_, v=v_, tm=tm, pre=pre_, tok=tok_), pb)
                        pg.barrier()
                        return
        pg.barrier()


def phase_c(S, li):
    pg = S.pg
    V, A, G, P = pg.V, pg.A, pg.G, pg.P
    ident = S.ident
    last = li == DEPTH - 1
    tiles = getattr(S, "ctiles", None) or list(range(2 if last else 0, NT))

    def x_src(tt):
        if li == 0:
            if tt < 2:
                return S.ctx_in, S.ctx_in[tt * 128:(tt + 1) * 128, :]
            return S.x_in, S.x_in[(tt - 2) * 128:(tt - 1) * 128, :]
        return S.X, S.X[tt * 128:(tt + 1) * 128, :]

    with ExitStack() as pc:
        pT = [pg.ps((128, 512), F32, f"c_pT{i}", pc) for i in range(2)]
        py = [pg.ps((128, 512), F32, f"c_py{i}", pc) for i in range(2)]
        pss = [pg.ps((128, 512), F32, f"c_ps{i}", pc) for i in range(4)]
        qbanks = pT + py
        wo = pg.sb((128, 8, D), BF16, "c_wo", pc)
        for hh in range(2):
            pg.dma(pg.pool, wo[:, :, hh * 512:(hh + 1) * 512],
                   S.w_out[li, :, hh * 512:(hh + 1) * 512].rearrange("(c p) n -> p c n", p=128), reads=[S.w_out], writes=[wo])
        wq = pg.sb((128, 8, 2048), BF16, "c_wq", pc)
        for hh in range(2):
            pg.dma(pg.pool, wq[:, :, hh * 1024:(hh + 1) * 1024],
                   S.pwq[li, :, hh * 1024:(hh + 1) * 1024].rearrange("(c p) n -> p c n", p=128), reads=[S.pwq], writes=[wq])
        ktmp = pg.sb((128, 2, 128), F32, "c_ktmp", pc)
        keysT = pg.sb((128, 2, 128), F32, "c_keysT", pc)
        for hf in range(2):
            pg.dma(pg.sp, ktmp[:, hf, :], S.psk[li, hf], reads=[S.psk], writes=[ktmp])
        for hf in range(2):
            P(lambda e, hf=hf: e.transpose(out=pT[0][:, hf * 128:(hf + 1) * 128], in_=ktmp[:, hf, :], identity=ident[:]),
              reads=[ktmp, ident], writes=[pT[0]])
        V(lambda e: e.tensor_copy(out=keysT[:].rearrange("p a b -> p (a b)"), in_=pT[0][:, 0:256]), reads=[pT[0]], writes=[keysT])
        bt = {n: pg.sb((128, D), F32, f"c_bt_{n}", pc) for n in ("gain", "g1", "gam1", "bet1", "sh2", "sc2", "g2", "gam2", "bet2")}
        for hh in range(4):
            pg.dma(pg.sp, bt["gain"][:, hh * 128:(hh + 1) * 128], S.hg_norm[li, :].partition_broadcast(128),
                   reads=[S.hg_norm], writes=[bt["gain"]])
            pg.dma(pg.sp, bt["gain"][:, 512 + hh * 128:512 + (hh + 1) * 128], S.gla_norm[li, :].partition_broadcast(128),
                   reads=[S.gla_norm], writes=[bt["gain"]])
        pg.dma(pg.sp, bt["gam1"][:], S.ln_gamma[li, 0, :].partition_broadcast(128), reads=[S.ln_gamma], writes=[bt["gam1"]])
        pg.dma(pg.sp, bt["bet1"][:], S.ln_beta[li, 0, :].partition_broadcast(128), reads=[S.ln_beta], writes=[bt["bet1"]])
        pg.dma(pg.sp, bt["gam2"][:], S.ln_gamma[li, 1, :].partition_broadcast(128), reads=[S.ln_gamma], writes=[bt["gam2"]])
        pg.dma(pg.sp, bt["bet2"][:], S.ln_beta[li, 1, :].partition_broadcast(128), reads=[S.ln_beta], writes=[bt["bet2"]])

        def load_mod(r):
            for n, k in (("g1", 2), ("sh2", 3), ("sc2", 4), ("g2", 5)):
                pg.dma(pg.sp, bt[n][:], S.MOD[li, r, k * D:(k + 1) * D].partition_broadcast(128), reads=[S.MOD], writes=[bt[n]])
            V(lambda e: e.tensor_scalar(out=bt["sc2"][:], in0=bt["sc2"][:], scalar1=1.0, scalar2=None, op0=ALU.add),
              reads=[bt["sc2"]], writes=[bt["sc2"]])

        iot = pg.sb((128, 8, 16, 16), F32, "c_iot", pc)
        with ExitStack() as tmpst:
            ioti = pg.sb((128, 2048), I32, "c_ioti", tmpst)
            G(lambda e: e.iota(out=ioti[:], pattern=[[0, 128], [1, 16]], base=0, channel_multiplier=0), writes=[ioti])
            V(lambda e: e.tensor_copy(out=iot[:].rearrange("p a b c -> p (a b c)"), in_=ioti[:]), reads=[ioti], writes=[iot])
            pg.barrier()
        sbt = lambda n, shape=(128, D), dt=F32: pg.sb(shape, dt, "c_" + n, pc)
        of_t, ob_t, ug, x_t, x1, tb, acc, junk = (sbt(n) for n in ("of", "ob", "ug", "x", "x1", "t", "acc", "junk"))
        sgb = junk
        mixT = sbt("mixT", (128, 8, 128), BF16)
        tT = sbt("tT", (128, 8, 128), BF16)
        qT = sbt("qT", (128, 16, 128), F32)
        s_sb = sbt("s", (128, 16, 128), F32)
        cand = sbt("cand", (128, 8, 256), F32)
        s2 = [sbt(f"s2{i}", (128, 256), F32) for i in range(2)]
        vv = sbt("vv", (128, 16, 16), F32)
        ii = sbt("ii", (128, 16, 16), U32)
        iif = sbt("iif", (128, 16, 16), F32)
        ts = sbt("ts", (128, 8, 16), F32)
        pos = sbt("pos", (128, 8, 16), U32)
        pa = sbt("pa", (128, 8, 16), U32)
        pbb = sbt("pb", (128, 8, 16), U32)
        paf = sbt("paf", (128, 8, 16), F32)
        pbf = sbt("pbf", (128, 8, 16), F32)
        i1g = sbt("i1g", (128, 8, 16), F32)
        i2g = sbt("i2g", (128, 8, 16), F32)
        idxf = sbt("idxf", (128, 128), F32)
        idxu = sbt("idxu", (128, 128), U32)
        ex = sbt("ex", (128, 8, 16), F32)
        zz = sbt("zz", (128, 8), F32)
        gw = sbt("gw", (128, 128), F32)
        actb = sbt("act", (128, 128), F32)
        g1b = sbt("gl1", (128, 128), F32)
        g2b = sbt("gl2", (128, 128), F32)
        wgt = sbt("wgt", (128, 128), F32)
        ssq = sbt("ssq", (128, 8), F32)
        lnb = ln_bufs(pg, pc, "c")
        NG = 10
        gbuf = [sbt(f"gb{i}", (128, D), BF16) for i in range(NG)]
        dg = [sbt(f"dg{i}", (128, 16, 128), BF16) for i in range(2)]
        act2 = sbt("act2", (128, 2, 128), F32)
        ng = 0
        cur_r = None
        U_tab = S.UB[:, :]
        V_tab = S.VB[:, :]

        def top16(src3, n, w, vals, idxs, k0):
            sc = s2[k0 % 2]
            V(lambda e: e.max(out=vals[:, n, 0:8], in_=src3[:, n, :]), reads=[src3], writes=[vals])
            V(lambda e: e.max_index(out=idxs[:, n, 0:8], in_max=vals[:, n, 0:8], in_values=src3[:, n, :]),
              reads=[src3, vals], writes=[idxs])
            V(lambda e: e.match_replace(out=sc[:, 0:w], in_to_replace=vals[:, n, 0:8], in_values=src3[:, n, :], imm_value=-1e30),
              reads=[src3, vals], writes=[sc])
            V(lambda e: e.max(out=vals[:, n, 8:16], in_=sc[:, 0:w]), reads=[sc], writes=[vals])
            V(lambda e: e.max_index(out=idxs[:, n, 8:16], in_max=vals[:, n, 8:16], in_values=sc[:, 0:w]),
              reads=[sc, vals], writes=[idxs])

        for tt in tiles:
            r = 1 if tt < 2 else 0
            if r != cur_r:
                load_mod(r)
                cur_r = r
            rows = slice(tt * 128, (tt + 1) * 128)
            xsb, xsap = x_src(tt)
            pg.dma(pg.sp, of_t[:], S.OF[rows, :], reads=[S.OF], writes=[of_t])
            pg.dma(pg.sp, ob_t[:], S.OB[rows, :], reads=[S.OB], writes=[ob_t])
            pg.dma(pg.sp, x_t[:], xsap, reads=[xsb], writes=[x_t])
            pg.dma(pg.sp, ug[:, 0:512], S.U[rows, O_HG:O_HG + 512], reads=[S.U], writes=[ug])
            pg.dma(pg.sp, ug[:, 512:1024], S.U[rows, O_GG:O_GG + 512], reads=[S.U], writes=[ug])
            V(lambda e: e.tensor_tensor(out=of_t[:], in0=of_t[:], in1=ob_t[:], op=ALU.add), reads=[of_t, ob_t], writes=[of_t])
            A(lambda e: e.activation(out=junk[:], in_=of_t[:], func=AF.Square), reads=[of_t], writes=[junk])
            V(lambda e: e.tensor_reduce(out=ssq[:], in_=junk[:].rearrange("p (h d) -> p h d", h=8), axis=AX.X, op=ALU.add),
              reads=[junk], writes=[ssq])
            A(lambda e: e.activation(out=ssq[:], in_=ssq[:], func=AF.Sqrt, bias=EPS, scale=1.0 / 128.0), reads=[ssq], writes=[ssq])
            V(lambda e: e.reciprocal(out=ssq[:], in_=ssq[:]), reads=[ssq], writes=[ssq])
            V(lambda e: e.tensor_tensor(out=of_t[:].rearrange("p (h d) -> p h d", h=8), in0=of_t[:].rearrange("p (h d) -> p h d", h=8),
                                        in1=ssq[:].unsqueeze(2).to_broadcast([128, 8, 128]), op=ALU.mult),
              reads=[of_t, ssq], writes=[of_t])
            V(lambda e: e.tensor_tensor(out=of_t[:], in0=of_t[:], in1=bt["gain"][:], op=ALU.mult), reads=[of_t, bt["gain"]], writes=[of_t])
            A(lambda e: e.activation(out=sgb[:], in_=ug[:], func=AF.Sigmoid), reads=[ug], writes=[sgb])
            V(lambda e: e.tensor_tensor(out=ug[:], in0=ug[:], in1=sgb[:], op=ALU.mult), reads=[ug, sgb], writes=[ug])
            V(lambda e: e.tensor_tensor(out=of_t[:], in0=of_t[:], in1=ug[:], op=ALU.mult), reads=[of_t, ug], writes=[of_t])
            if getattr(S, "dbgc", None) is not None and tt == S.dbgc_tile:
                S.dbgc(dict(mix=of_t, ssq=ssq, gain=bt["gain"], ug=ug), pc)
            for hb in range(2):
                for c4 in range(4):
                    ch = hb * 4 + c4
                    P(lambda e, hb=hb, c4=c4, ch=ch: e.transpose(out=pT[hb][:, c4 * 128:(c4 + 1) * 128],
                                                                in_=of_t[:, ch * 128:(ch + 1) * 128], identity=ident[:]),
                      reads=[of_t, ident], writes=[pT[hb]])
                A(lambda e, hb=hb: e.activation(out=mixT[:, hb * 4:(hb + 1) * 4, :].rearrange("p a b -> p (a b)"), in_=pT[hb][:],
                                               func=AF.Copy), reads=[pT[hb]], writes=[mixT])
            for nb in range(2):
                for ch in range(8):
                    P(lambda e, nb=nb, ch=ch: e.matmul(py[nb][:], lhsT=mixT[:, ch, :], rhs=wo[:, ch, nb * 512:(nb + 1) * 512],
                                                      start=(ch == 0), stop=(ch == 7)), reads=[mixT, wo], writes=[py[nb]])
                V(lambda e, nb=nb: e.tensor_tensor(out=x1[:, nb * 512:(nb + 1) * 512], in0=py[nb][:],
                                                   in1=bt["g1"][:, nb * 512:(nb + 1) * 512], op=ALU.mult),
                  reads=[py[nb], bt["g1"]], writes=[x1])
            if getattr(S, "dbgc", None) is not None and tt == S.dbgc_tile:
                S.dbgc(dict(y1=x1, g1=bt["g1"], xt=x_t), pc)
            V(lambda e: e.scalar_tensor_tensor(out=x1[:], in0=x_t[:], scalar=ALPHA, in1=x1[:], op0=ALU.mult, op1=ALU.add),
              reads=[x_t, x1], writes=[x1])
            rstd, nmr = ln_stats(pg, x1, lnb)
            A(lambda e: e.activation(out=x1[:], in_=x1[:], func=AF.Identity, bias=nmr[:], scale=rstd[:]), reads=[x1, rstd, nmr], writes=[x1])
            V(lambda e: e.tensor_tensor(out=x1[:], in0=x1[:], in1=bt["gam1"][:], op=ALU.mult), reads=[x1, bt["gam1"]], writes=[x1])
            V(lambda e: e.tensor_tensor(out=x1[:], in0=x1[:], in1=bt["bet1"][:], op=ALU.add), reads=[x1, bt["bet1"]], writes=[x1])
            rstd, nmr = ln_stats(pg, x1, lnb)
            A(lambda e: e.activation(out=tb[:], in_=x1[:], func=AF.Identity, bias=nmr[:], scale=rstd[:]), reads=[x1, rstd, nmr], writes=[tb])
            V(lambda e: e.tensor_tensor(out=tb[:], in0=tb[:], in1=bt["sc2"][:], op=ALU.mult), reads=[tb, bt["sc2"]], writes=[tb])
            V(lambda e: e.tensor_tensor(out=tb[:], in0=tb[:], in1=bt["sh2"][:], op=ALU.add), reads=[tb, bt["sh2"]], writes=[tb])
            for hb in range(2):
                for c4 in range(4):
                    ch = hb * 4 + c4
                    P(lambda e, hb=hb, c4=c4, ch=ch: e.transpose(out=pT[hb][:, c4 * 128:(c4 + 1) * 128],
                                                                in_=tb[:, ch * 128:(ch + 1) * 128], identity=ident[:]),
                      reads=[tb, ident], writes=[pT[hb]])
                A(lambda e, hb=hb: e.activation(out=tT[:, hb * 4:(hb + 1) * 4, :].rearrange("p a b -> p (a b)"), in_=pT[hb][:],
                                               func=AF.Copy), reads=[pT[hb]], writes=[tT])
            for qb in range(4):
                bank = qbanks[qb]
                for p4 in range(4):
                    pq = qb * 4 + p4
                    for ch in range(8):
                        P(lambda e, bank=bank, p4=p4, pq=pq, ch=ch: e.matmul(bank[:, p4 * 128:(p4 + 1) * 128],
                                                                            lhsT=wq[:, ch, pq * 128:(pq + 1) * 128], rhs=tT[:, ch, :],
                                                                            start=(ch == 0), stop=(ch == 7)),
                          reads=[wq, tT], writes=[bank])
                eng = A if qb % 2 == 0 else V
                if qb % 2 == 0:
                    A(lambda e, bank=bank, qb=qb: e.activation(out=qT[:, qb * 4:(qb + 1) * 4, :].rearrange("p a b -> p (a b)"),
                                                               in_=bank[:], func=AF.Copy), reads=[bank], writes=[qT])
                else:
                    V(lambda e, bank=bank, qb=qb: e.tensor_copy(out=qT[:, qb * 4:(qb + 1) * 4, :].rearrange("p a b -> p (a b)"),
                                                                in_=bank[:]), reads=[bank], writes=[qT])
            for qb in range(4):
                bank = pss[qb]
                for p4 in range(4):
                    pq = qb * 4 + p4
                    P(lambda e, bank=bank, p4=p4, pq=pq: e.matmul(bank[:, p4 * 128:(p4 + 1) * 128], lhsT=qT[:, pq, :],
                                                                 rhs=keysT[:, pq % 2, :], start=True, stop=True),
                      reads=[qT, keysT], writes=[bank])
                if qb % 2 == 0:
                    A(lambda e, bank=bank, qb=qb: e.activation(out=s_sb[:, qb * 4:(qb + 1) * 4, :].rearrange("p a b -> p (a b)"),
                                                               in_=bank[:], func=AF.Copy), reads=[bank], writes=[s_sb])
                else:
                    V(lambda e, bank=bank, qb=qb: e.tensor_copy(out=s_sb[:, qb * 4:(qb + 1) * 4, :].rearrange("p a b -> p (a b)"),
                                                                in_=bank[:]), reads=[bank], writes=[s_sb])
            for pq in range(16):
                top16(s_sb, pq, 128, vv, ii, pq)
            for h in range(8):
                V(lambda e, h=h: e.tensor_tensor(out=cand[:, h, :].rearrange("p (a b) -> p a b", a=16),
                                                 in0=vv[:, 2 * h, :].unsqueeze(2).to_broadcast([128, 16, 16]),
                                                 in1=vv[:, 2 * h + 1, :].unsqueeze(1).to_broadcast([128, 16, 16]), op=ALU.add),
                  reads=[vv], writes=[cand])
            for h in range(8):
                top16(cand, h, 256, ts, pos, h)
            V(lambda e: e.tensor_single_scalar(out=pa[:], in_=pos[:], scalar=4, op=ALU.logical_shift_right), reads=[pos], writes=[pa])
            V(lambda e: e.tensor_single_scalar(out=pbb[:], in_=pos[:], scalar=15, op=ALU.bitwise_and), reads=[pos], writes=[pbb])
            V(lambda e: e.tensor_copy(out=paf[:], in_=pa[:]), reads=[pa], writes=[paf])
            V(lambda e: e.tensor_copy(out=pbf[:], in_=pbb[:]), reads=[pbb], writes=[pbf])
            V(lambda e: e.tensor_copy(out=iif[:], in_=ii[:]), reads=[ii], writes=[iif])
            eq = s_sb[:].rearrange("p a (b c) -> p (a b c)", b=8)[:, 0:2048].rearrange("p (h k a) -> p h k a", h=8, k=16)
            prod = cand[:].rearrange("p h (k a) -> p h k a", k=16)
            iif4 = iif[:].rearrange("p (h two) a -> p h two a", two=2)
            for half, (pxf, outg) in enumerate(((paf, i1g), (pbf, i2g))):
                V(lambda e, pxf=pxf: e.tensor_tensor(out=eq, in0=pxf[:].unsqueeze(3).to_broadcast([128, 8, 16, 16]), in1=iot[:],
                                                     op=ALU.is_equal), reads=[pxf, iot], writes=[s_sb])
                V(lambda e, half=half: e.tensor_tensor(out=prod, in0=eq,
                                                       in1=iif4[:, :, half, :].unsqueeze(2).to_broadcast([128, 8, 16, 16]), op=ALU.mult),
                  reads=[s_sb, iif], writes=[cand])
                V(lambda e, outg=outg: e.tensor_reduce(out=outg[:], in_=prod, axis=AX.X, op=ALU.add), reads=[cand], writes=[outg])
            V(lambda e: e.scalar_tensor_tensor(out=idxf[:].rearrange("p (h k) -> p h k", h=8), in0=i1g[:], scalar=128.0, in1=i2g[:],
                                               op0=ALU.mult, op1=ALU.add), reads=[i1g, i2g], writes=[idxf])
            if li > 0:
                V(lambda e: e.tensor_scalar(out=idxf[:], in0=idxf[:], scalar1=float(li * NEXP), scalar2=None, op0=ALU.add),
                  reads=[idxf], writes=[idxf])
            V(lambda e: e.tensor_copy(out=idxu[:], in_=idxf[:]), reads=[idxf], writes=[idxu])
            V(lambda e: e.tensor_tensor(out=ex[:], in0=ts[:], in1=ts[:, :, 0:1].to_broadcast([128, 8, 16]), op=ALU.subtract),
              reads=[ts], writes=[ex])
            A(lambda e: e.activation(out=ex[:], in_=ex[:], func=AF.Exp), reads=[ex], writes=[ex])
            V(lambda e: e.tensor_reduce(out=zz[:], in_=ex[:], axis=AX.X, op=ALU.add), reads=[ex], writes=[zz])
            V(lambda e: e.reciprocal(out=zz[:], in_=zz[:]), reads=[zz], writes=[zz])
            V(lambda e: e.tensor_tensor(out=gw[:].rearrange("p (h k) -> p h k", h=8), in0=ex[:],
                                        in1=zz[:].unsqueeze(2).to_broadcast([128, 8, 16]), op=ALU.mult), reads=[ex, zz], writes=[gw])
            for hb in range(2):
                A(lambda e, hb=hb: e.activation(out=pT[hb][:], in_=tb[:, hb * 512:(hb + 1) * 512], func=AF.Copy),
                  reads=[tb], writes=[pT[hb]])
            for j in range(128):
                gb = gbuf[ng % NG]
                ng += 1
                pg.dma(pg.pool, gb[:], U_tab, reads=[S.UB, idxu], writes=[gb],
                       indirect=bass.IndirectOffsetOnAxis(ap=idxu[:, j:j + 1], axis=0))
                for hb in range(2):
                    V(lambda e, gb=gb, j=j, hb=hb: e.scalar_tensor_tensor(out=junk[:, hb * 512:(hb + 1) * 512],
                                                                         in0=gb[:, hb * 512:(hb + 1) * 512], scalar=1.0,
                                                                         in1=pT[hb][:], op0=ALU.mult, op1=ALU.mult,
                                                                         accum_out=act2[:, hb, j:j + 1]),
                      reads=[gb, pT[hb]], writes=[junk, act2])
            V(lambda e: e.tensor_tensor(out=actb[:], in0=act2[:, 0, :], in1=act2[:, 1, :], op=ALU.add), reads=[act2], writes=[actb])
            V(lambda e: e.tensor_tensor(out=g1b[:], in0=actb[:], in1=actb[:], op=ALU.mult), reads=[actb], writes=[g1b])
            V(lambda e: e.tensor_scalar(out=g1b[:], in0=g1b[:], scalar1=0.044715, scalar2=1.0, op0=ALU.mult, op1=ALU.add),
              reads=[g1b], writes=[g1b])
            V(lambda e: e.tensor_tensor(out=g1b[:], in0=g1b[:], in1=actb[:], op=ALU.mult), reads=[g1b, actb], writes=[g1b])
            A(lambda e: e.activation(out=g2b[:], in_=g1b[:], func=AF.Sigmoid, scale=1.5957691216057308), reads=[g1b], writes=[g2b])
            V(lambda e: e.tensor_tensor(out=g2b[:], in0=g2b[:], in1=actb[:], op=ALU.mult), reads=[g2b, actb], writes=[g2b])
            V(lambda e: e.tensor_tensor(out=wgt[:], in0=g2b[:], in1=gw[:], op=ALU.mult), reads=[g2b, gw], writes=[wgt])
            for j in range(128):
                jj = j % 16
                dgb = dg[(j // 16) % 2]
                if jj == 0:
                    g16 = j // 16
                    V(lambda e, dgb=dgb, g16=g16: e.tensor_tensor(out=dgb[:],
                                                                  in0=wgt[:, g16 * 16:(g16 + 1) * 16].unsqueeze(2).to_broadcast([128, 16, 128]),
                                                                  in1=ident[:].unsqueeze(1).to_broadcast([128, 16, 128]), op=ALU.mult),
                      reads=[wgt, ident], writes=[dgb])
                gb = gbuf[ng % NG]
                ng += 1
                pg.dma(pg.pool, gb[:], V_tab, reads=[S.VB, idxu], writes=[gb],
                       indirect=bass.IndirectOffsetOnAxis(ap=idxu[:, j:j + 1], axis=0))
                for hb in range(2):
                    P(lambda e, gb=gb, dgb=dgb, jj=jj, hb=hb, j=j: e.matmul(py[hb][:], lhsT=dgb[:, jj, :], rhs=gb[:, hb * 512:(hb + 1) * 512],
                                                                           start=(j == 0), stop=(j == 127)),
                      reads=[gb, dgb], writes=[py[hb]])
            for hb in range(2):
                V(lambda e, hb=hb: e.tensor_tensor(out=acc[:, hb * 512:(hb + 1) * 512], in0=py[hb][:],
                                                   in1=bt["g2"][:, hb * 512:(hb + 1) * 512], op=ALU.mult),
                  reads=[py[hb], bt["g2"]], writes=[acc])
            V(lambda e: e.scalar_tensor_tensor(out=acc[:], in0=x1[:], scalar=ALPHA, in1=acc[:], op0=ALU.mult, op1=ALU.add),
              reads=[x1, acc], writes=[acc])
            rstd, nmr = ln_stats(pg, acc, lnb)
            A(lambda e: e.activation(out=acc[:], in_=acc[:], func=AF.Identity, bias=nmr[:], scale=rstd[:]), reads=[acc, rstd, nmr], writes=[acc])
            V(lambda e: e.tensor_tensor(out=acc[:], in0=acc[:], in1=bt["gam2"][:], op=ALU.mult), reads=[acc, bt["gam2"]], writes=[acc])
            V(lambda e: e.tensor_tensor(out=acc[:], in0=acc[:], in1=bt["bet2"][:], op=ALU.add), reads=[acc, bt["bet2"]], writes=[acc])
            if last and S.mode == "full":
                pg.dma(pg.act, S.OUT[(tt - 2) * 128:(tt - 1) * 128, :], acc[:], reads=[acc], writes=[S.OUT])
            else:
                pg.dma(pg.act, S.X[rows, :], acc[:], reads=[acc], writes=[S.X])
            if getattr(S, "dbgc", None) is not None and tt == S.dbgc_tile:
                S.dbgc(dict(x1=x1, t=tb, s=None, vv=vv, ii=ii, ts=ts, pos=pos, idxf=idxf, gw=gw, act=actb, wgt=wgt), pc)
        pg.barrier()

def _run(nc, in_maps, ncores):
    return run_bass_kernel_spmd(nc, in_maps, core_ids=list(range(ncores)))


def make_in_maps(inputs, batches, big=True):
    f = lambda a: np.ascontiguousarray(np.asarray(a, dtype=np.float32))
    names = ["w_ada", "b_ada", "w_in", "w_gk2", "b_gk", "hg_lower_bounds", "hg_norm", "gla_norm",
             "w_out", "ln_gamma", "ln_beta", "peer_w_query", "peer_sub_keys"] + (["peer_u", "peer_v"] if big else [])
    shared = {k: f(inputs[k]) for k in names}
    maps = []
    for b in batches:
        m = dict(shared)
        m["x"] = f(inputs["x"][b])
        m["ctx"] = f(inputs["ctx"][b])
        m["c"] = f(inputs["c"][b:b + 1])
        m["c_ctx"] = f(np.asarray(inputs["c_ctx"]).reshape(1, D))
        maps.append(m)
    return maps


def kernel(**inputs):
    nc = build_nc("full")
    maps = make_in_maps(inputs, range(4))
    res = _run(nc, maps, 4)
    return np.stack([r["out"] for r in res.results], axis=0)
```

```python
import numpy as np
from contextlib import ExitStack
import concourse.bass as bass
import concourse.mybir as mybir
from concourse.bass_utils import run_bass_kernel_spmd

F32 = mybir.dt.float32
BF16 = mybir.dt.bfloat16
U32 = mybir.dt.uint32
I32 = mybir.dt.int32
U8 = mybir.dt.uint8
AF = mybir.ActivationFunctionType
ALU = mybir.AluOpType
AX = mybir.AxisListType

D = 1024
SEQ = 8192
CTX = 256
T = SEQ + CTX
NT = T // 128
DEPTH = 4
GRID_W = 64
ROWS = SEQ // GRID_W
DIN = 4128
NEXP = 16384
EPS = 1e-6
ALPHA = (2.0 * DEPTH) ** 0.25
C = 64

O_HQ, O_HFF, O_HFB, O_HI, O_HG = 0, 512, 1024, 1536, 2048
O_GQ, O_GK, O_GV, O_GG, O_GLF, O_GLB = 2560, 2816, 3072, 3584, 4096, 4112


class Buf:
    def __init__(self, t=None, name="", excl=False):
        self.t = t
        self.name = name
        self.w = None
        self.r = {}
        self.excl = excl

    def __getitem__(self, k):
        return self.t[k]


class Eng:
    def __init__(self, h, sem, name):
        self.h = h
        self.sem = sem
        self.name = name
        self.cnt = 0
        self.known = {}
        self.dsems = []
        self.dpos = 0


class Prog:
    def __init__(self, nc, st):
        self.nc = nc
        self.st = st
        mk = lambda h, n: Eng(h, st.enter_context(nc.semaphore("p_" + n)), n)
        self.pe = mk(nc.tensor, "pe")
        self.dve = mk(nc.vector, "dve")
        self.act = mk(nc.scalar, "act")
        self.pool = mk(nc.gpsimd, "pool")
        self.sp = mk(nc.sync, "sp")
        self.engs = [self.pe, self.dve, self.act, self.pool, self.sp]
        self.semcnt = {}
        for e, n in ((self.sp, 20), (self.pool, 20), (self.act, 4)):
            for i in range(n):
                s = st.enter_context(nc.semaphore(f"d_{e.name}{i}"))
                e.dsems.append(s)
                self.semcnt[s] = 0
        self.nbuf = 0

    def sb(self, shape, dtype, name=None, st=None):
        self.nbuf += 1
        name = f"{name or 'sb'}_{self.nbuf}"
        t = (st or self.st).enter_context(self.nc.sbuf_tensor(name, list(shape), dtype))
        return Buf(t, name)

    def ps(self, shape, dtype, name=None, st=None):
        self.nbuf += 1
        name = f"{name or 'ps'}_{self.nbuf}"
        t = (st or self.st).enter_context(self.nc.psum_tensor(name, list(shape), dtype))
        return Buf(t, name, excl=True)

    def dram(self, name, shape, dtype, kind="Internal"):
        t = self.nc.dram_tensor(name, list(shape), dtype, kind=kind)
        return Buf(t.ap(), name)

    def _deps(self, eng, reads, writes):
        deps = []
        for b in reads:
            if b.w is not None:
                deps.append(b.w)
            if b.excl:
                for s, v in b.r.items():
                    if s is not eng.sem:
                        deps.append((s, v))
        for b in writes:
            if b.w is not None and b.w[0] is not eng.sem:
                deps.append(b.w)
            for s, v in b.r.items():
                if s is not eng.sem:
                    deps.append((s, v))
        return deps

    def _wait(self, eng, deps):
        for s, v in deps:
            if eng.known.get(s, 0) < v:
                eng.h.wait_ge(s, v)
                eng.known[s] = v

    def _mark(self, tok, reads, writes):
        for b in reads:
            if b.r.get(tok[0], 0) < tok[1]:
                b.r[tok[0]] = tok[1]
        for b in writes:
            b.w = tok
            b.r = {}

    def op(self, eng, fn, reads=(), writes=()):
        self._wait(eng, self._deps(eng, reads, writes))
        inst = fn(eng.h)
        eng.cnt += 1
        inst.then_inc(eng.sem, 1)
        self._mark((eng.sem, eng.cnt), reads, writes)

    def dma(self, q, out, in_, reads=(), writes=(), indirect=None, **kw):
        s = q.dsems[q.dpos % len(q.dsems)]
        q.dpos += 1
        deps = self._deps(q, reads, writes)
        if self.semcnt[s] > 0:
            deps.append((s, self.semcnt[s]))
        self._wait(q, deps)
        if indirect is not None:
            inst = q.h.indirect_dma_start(out=out, out_offset=None, in_=in_, in_offset=indirect, **kw)
        else:
            inst = q.h.dma_start(out=out, in_=in_, **kw)
        self.semcnt[s] += 16
        inst.then_inc(s, 16)
        self._mark((s, self.semcnt[s]), reads, writes)

    def barrier(self):
        toks = [(e.sem, e.cnt) for e in self.engs if e.cnt > 0]
        toks += [(s, v) for s, v in self.semcnt.items() if v > 0]
        for e in self.engs:
            self._wait(e, [t for t in toks if t[0] is not e.sem or True])

    def V(self, fn, reads=(), writes=()):
        self.op(self.dve, fn, reads, writes)

    def A(self, fn, reads=(), writes=()):
        self.op(self.act, fn, reads, writes)

    def G(self, fn, reads=(), writes=()):
        self.op(self.pool, fn, reads, writes)

    def P(self, fn, reads=(), writes=()):
        self.op(self.pe, fn, reads, writes)


class NS:
    pass


class NSView:
    def __init__(self, buf, view):
        self.buf, self.view = buf, view

    def __getitem__(self, k):
        return self.view[k]


def build_nc(mode="full", nlayers=DEPTH):
    nc = bass.Bass("TRN2", target_bir_lowering=False)
    with ExitStack() as st:
        _emit(nc, st, mode, nlayers)
    return nc


def _emit(nc, st, mode, nlayers):
    pg = Prog(nc, st)
    S = NS()
    S.nc, S.pg, S.mode = nc, pg, mode
    din = lambda n, s: Buf(nc.dram_tensor(n, list(s), F32, kind="ExternalInput").ap(), n)
    S.x_in = din("x", (SEQ, D))
    S.ctx_in = din("ctx", (CTX, D))
    S.c_in = din("c", (1, D))
    S.cctx_in = din("c_ctx", (1, D))
    S.w_ada = din("w_ada", (DEPTH, D, 6 * D))
    S.b_ada = din("b_ada", (DEPTH, 6 * D))
    S.w_in = din("w_in", (DEPTH, D, DIN))
    S.w_gk2 = din("w_gk2", (DEPTH, 2, 16, 256))
    S.b_gk = din("b_gk", (DEPTH, 2, 256))
    S.hg_lb = din("hg_lower_bounds", (2, DEPTH, 512))
    S.hg_norm = din("hg_norm", (DEPTH, 128))
    S.gla_norm = din("gla_norm", (DEPTH, 128))
    S.w_out = din("w_out", (DEPTH, D, D))
    S.ln_gamma = din("ln_gamma", (DEPTH, 2, D))
    S.ln_beta = din("ln_beta", (DEPTH, 2, D))
    S.pwq = din("peer_w_query", (DEPTH, D, 2048))
    S.psk = din("peer_sub_keys", (DEPTH, 2, 128, 128))
    if mode in ("full", "C", "chain"):
        S.peer_u = din("peer_u", (DEPTH, NEXP, D))
        S.peer_v = din("peer_v", (DEPTH, NEXP, D))

    S.MOD = pg.dram("s_mod", (DEPTH, 2, 6 * D), F32)
    S.OML = pg.dram("s_oml", (2, DEPTH, 512), F32)
    ukind = {"A": "ExternalOutput", "B": "ExternalInput", "C": "ExternalInput"}.get(mode, "Internal")
    S.U = pg.dram("s_u", (T, DIN), F32, kind=ukind)
    okind = {"B": "ExternalOutput", "C": "ExternalInput"}.get(mode, "Internal")
    S.OF = pg.dram("s_of", (T, D), F32, kind=okind)
    S.OB = pg.dram("s_ob", (T, D), F32, kind=okind)
    S.X = pg.dram("s_x", (T, D), F32, kind=("ExternalOutput" if mode in ("C", "chain") else "Internal"))
    if mode == "full":
        S.OUT = pg.dram("out", (SEQ, D), F32, kind="ExternalOutput")
    if hasattr(S, "peer_u"):
        S.UVB = pg.dram("s_uvb", (DEPTH * NEXP, 2 * D), BF16)

    S.ident = pg.sb((128, 128), F32, "ident")
    S.ones = pg.sb((128, 128), F32, "ones")
    pg.G(lambda e: e.memset(S.ones[:], 1.0), writes=[S.ones])
    pg.G(lambda e: e.affine_select(out=S.ident[:], in_=S.ones[:], pattern=[[1, 128]], compare_op=ALU.is_equal,
                                   fill=0.0, base=0, channel_multiplier=-1), reads=[S.ones], writes=[S.ident])

    if mode in ("full", "A", "C", "chain"):
        prologue_mod(S)
    if hasattr(S, "peer_u"):
        convert_tables(S, nlayers)
    if mode in ("full", "B", "chain"):
        prologue_lb(S)
        scan_consts(S)
    pg.barrier()
    for li in range(nlayers):
        if mode in ("full", "A", "chain"):
            phase_a(S, li)
            pg.barrier()
        if mode in ("full", "B", "chain"):
            phase_b(S, li)
            pg.barrier()
        if mode in ("full", "C", "chain"):
            phase_c(S, li)
            pg.barrier()
    pg.barrier()


def prologue_mod(S):
    pg = S.pg
    with ExitStack() as ps_:
        cc = pg.sb((128, 8, 2), F32, "cc", ps_)
        scc = pg.sb((128, 8, 2), F32, "scc", ps_)
        sig = pg.sb((128, 8, 2), F32, "sig", ps_)
        pg.dma(pg.sp, cc[:, :, 0], S.c_in[0, :].rearrange("(c p) -> p c", p=128), reads=[S.c_in], writes=[cc],
               allow_slow_non_contiguous=True)
        pg.dma(pg.sp, cc[:, :, 1], S.cctx_in[0, :].rearrange("(c p) -> p c", p=128), reads=[S.cctx_in], writes=[cc],
               allow_slow_non_contiguous=True)
        pg.A(lambda e: e.activation(out=sig[:], in_=cc[:], func=AF.Sigmoid), reads=[cc], writes=[sig])
        pg.V(lambda e: e.tensor_tensor(out=scc[:], in0=cc[:], in1=sig[:], op=ALU.mult), reads=[cc, sig], writes=[scc])
        wblk = [pg.sb((128, 8, 512), F32, f"wblk{i}", ps_) for i in range(2)]
        msb = pg.sb((2, 6 * D), F32, "msb", ps_)
        bsb = pg.sb((2, 6 * D), F32, "bsb", ps_)
        pm = [pg.ps((128, 512), F32, f"pm{i}", ps_) for i in range(2)]
        k = 0
        for li in range(DEPTH):
            pg.dma(pg.sp, bsb[:], S.b_ada[li, :].partition_broadcast(2), reads=[S.b_ada], writes=[bsb])
            for j in range(12):
                wb = wblk[k % 2]
                pmj = pm[k % 2]
                k += 1
                pg.dma(pg.sp, wb[:], S.w_ada[li, :, j * 512:(j + 1) * 512].rearrange("(c p) n -> p c n", p=128),
                       reads=[S.w_ada], writes=[wb])
                for ch in range(8):
                    pg.P(lambda e, ch=ch, wb=wb, pmj=pmj: e.matmul(pmj[0:2, :], lhsT=scc[:, ch, :], rhs=wb[:, ch, :],
                                                                  start=(ch == 0), stop=(ch == 7)),
                         reads=[scc, wb], writes=[pmj])
                pg.V(lambda e, j=j, pmj=pmj: e.tensor_tensor(out=msb[:, j * 512:(j + 1) * 512], in0=pmj[0:2, :],
                                                             in1=bsb[:, j * 512:(j + 1) * 512], op=ALU.add),
                     reads=[pmj, bsb], writes=[msb])
            pg.dma(pg.sp, S.MOD[li], msb[:], reads=[msb], writes=[S.MOD])
        pg.barrier()


def convert_tables(S, nlayers):
    pg = S.pg
    with ExitStack() as cs:
        fin = [pg.sb((128, 4096), F32, f"cv_in{i}", cs) for i in range(3)]
        fout = [pg.sb((128, 4096), BF16, f"cv_out{i}", cs) for i in range(3)]
        k = 0
        for src, c0 in ((S.peer_u, 0), (S.peer_v, D)):
            for li in range(nlayers):
                for c in range(NEXP // 512):
                    fi, fo = fin[k % 3], fout[k % 3]
                    pg.dma(pg.sp, fi[:].rearrange("p (a d) -> p a d", a=4),
                           src[li, c * 512:(c + 1) * 512, :].rearrange("(p a) d -> p a d", a=4), reads=[src], writes=[fi])
                    if k % 2 == 0:
                        pg.A(lambda e, fi=fi, fo=fo: e.activation(out=fo[:], in_=fi[:], func=AF.Copy), reads=[fi], writes=[fo])
                    else:
                        pg.V(lambda e, fi=fi, fo=fo: e.tensor_copy(out=fo[:], in_=fi[:]), reads=[fi], writes=[fo])
                    r0 = li * NEXP + c * 512
                    pg.dma(pg.pool, S.UVB[r0:r0 + 512, c0:c0 + D].rearrange("(p a) d -> p a d", a=4),
                           fo[:].rearrange("p (a d) -> p a d", a=4), reads=[fo], writes=[])
                    k += 1
        pg.barrier()


def prologue_lb(S):
    pg = S.pg
    with ExitStack() as ps_:
        x = pg.sb((128, 2, 4, 4), F32, "lb_x", ps_)
        mx = pg.sb((128, 2, 4), F32, "lb_mx", ps_)
        sm = pg.sb((128, 2, 4), F32, "lb_sm", ps_)
        oml = pg.sb((128, 2, 4, 4), F32, "lb_oml", ps_)
        for d in range(2):
            for l in range(4):
                pg.dma(pg.sp, x[:, d, l, :], S.hg_lb[d, l, :].rearrange("(h p) -> p h", p=128), reads=[S.hg_lb], writes=[x],
                       allow_slow_non_contiguous=True)
        V = pg.V
        V(lambda e: e.tensor_tensor(out=mx[:], in0=x[:, :, 0, :], in1=x[:, :, 1, :], op=ALU.max), reads=[x], writes=[mx])
        for l in (2, 3):
            V(lambda e, l=l: e.tensor_tensor(out=mx[:], in0=mx[:], in1=x[:, :, l, :], op=ALU.max), reads=[x, mx], writes=[mx])
        for l in range(4):
            V(lambda e, l=l: e.tensor_tensor(out=x[:, :, l, :], in0=x[:, :, l, :], in1=mx[:], op=ALU.subtract),
              reads=[x, mx], writes=[x])
        pg.A(lambda e: e.activation(out=x[:], in_=x[:], func=AF.Exp), reads=[x], writes=[x])
        V(lambda e: e.tensor_tensor(out=sm[:], in0=x[:, :, 0, :], in1=x[:, :, 1, :], op=ALU.add), reads=[x], writes=[sm])
        for l in (2, 3):
            V(lambda e, l=l: e.tensor_tensor(out=sm[:], in0=sm[:], in1=x[:, :, l, :], op=ALU.add), reads=[x, sm], writes=[sm])
        V(lambda e: e.reciprocal(out=sm[:], in_=sm[:]), reads=[sm], writes=[sm])
        for l in range(4):
            V(lambda e, l=l: e.tensor_tensor(out=x[:, :, l, :], in0=x[:, :, l, :], in1=sm[:], op=ALU.mult),
              reads=[x, sm], writes=[x])
        for l in (2, 3):
            V(lambda e, l=l: e.tensor_tensor(out=x[:, :, l, :], in0=x[:, :, l, :], in1=x[:, :, l - 1, :], op=ALU.add),
              reads=[x], writes=[x])
        pg.V(lambda e: e.memset(oml[:], 1.0), writes=[oml])
        for l in (1, 2, 3):
            V(lambda e, l=l: e.tensor_scalar(out=x[:, :, l, :], in0=x[:, :, l, :], scalar1=1.0 - 1e-6, scalar2=0.0,
                                             op0=ALU.min, op1=ALU.max), reads=[x], writes=[x])
            V(lambda e, l=l: e.tensor_scalar(out=oml[:, :, l, :], in0=x[:, :, l, :], scalar1=-1.0, scalar2=1.0,
                                             op0=ALU.mult, op1=ALU.add), reads=[x], writes=[oml])
        for d in range(2):
            for l in range(4):
                pg.dma(pg.sp, S.OML[d, l, :].rearrange("(h p) -> p h", p=128), oml[:, d, l, :], reads=[oml], writes=[S.OML],
                       allow_slow_non_contiguous=True)
        pg.barrier()


def scan_consts(S):
    pg = S.pg
    G = pg.G
    ones = S.ones
    le = pg.sb((C, C), F32, "c_le")
    ge = pg.sb((C, C), F32, "c_ge")
    bmf = pg.sb((C, C + 2), F32, "c_bmf")
    bmb = pg.sb((C, C + 2), F32, "c_bmb")
    G(lambda e: e.affine_select(out=le[:], in_=ones[0:C, 0:C], pattern=[[1, C]], compare_op=ALU.is_ge, fill=0.0,
                                base=0, channel_multiplier=-1), reads=[ones], writes=[le])
    G(lambda e: e.affine_select(out=ge[:], in_=ones[0:C, 0:C], pattern=[[-1, C]], compare_op=ALU.is_ge, fill=0.0,
                                base=0, channel_multiplier=1), reads=[ones], writes=[ge])
    G(lambda e: e.affine_select(out=bmf[:], in_=ones[0:C, 0:C + 2], pattern=[[0, C + 2]], compare_op=ALU.is_ge, fill=0.0,
                                base=C // 2 - 1, channel_multiplier=-1), reads=[ones], writes=[bmf])
    G(lambda e: e.affine_select(out=bmb[:], in_=ones[0:C, 0:C + 2], pattern=[[0, C + 2]], compare_op=ALU.is_ge, fill=0.0,
                                base=-(C // 2), channel_multiplier=1), reads=[ones], writes=[bmb])
    S.tm = {}
    for key, bm, tri in (("f", bmf, le), ("b", bmb, ge)):
        for scale, tag in ((1.0, "h"), (-1.0 / 16.0, "g")):
            tm = pg.sb((C, C + 2), F32, f"c_tm{key}{tag}")
            pg.V(lambda e, tm=tm, bm=bm: e.tensor_copy(out=tm[:], in_=bm[:]), reads=[bm], writes=[tm])
            pg.V(lambda e, tm=tm, tri=tri: e.tensor_tensor(out=tm[:, 0:C], in0=tm[:, 0:C], in1=tri[:], op=ALU.subtract),
                 reads=[tm, tri], writes=[tm])
            if scale != 1.0:
                pg.V(lambda e, tm=tm, scale=scale: e.tensor_scalar(out=tm[:], in0=tm[:], scalar1=scale, scalar2=None, op0=ALU.mult),
                     reads=[tm], writes=[tm])
            S.tm[key + tag] = tm
    S.mask = {}
    for key, tri in (("f", le), ("b", ge)):
        m = pg.sb((C, 2, C), F32, f"c_mask{key}")
        for g in range(2):
            pg.V(lambda e, m=m, g=g, tri=tri: e.tensor_copy(out=m[:, g, :], in_=tri[:]), reads=[tri], writes=[m])
        S.mask[key] = m


def phase_a(S, li):
    pg = S.pg
    ident = S.ident

    def tile_src(tt):
        if li == 0:
            if tt < 2:
                return S.ctx_in, S.ctx_in[tt * 128:(tt + 1) * 128, :]
            return S.x_in, S.x_in[(tt - 2) * 128:(tt - 1) * 128, :]
        return S.X, S.X[tt * 128:(tt + 1) * 128, :]

    with ExitStack() as pa:
        wsb = pg.sb((128, 8, DIN), BF16, "w_in_sb", pa)
        for q4 in range(4):
            pg.dma(pg.pool, wsb[:, :, q4 * 1032:(q4 + 1) * 1032],
                   S.w_in[li, :, q4 * 1032:(q4 + 1) * 1032].rearrange("(c p) n -> p c n", p=128),
                   reads=[S.w_in], writes=[wsb])
        shv = pg.sb((128, 2, 8), F32, "shv", pa)
        scv = pg.sb((128, 2, 8), F32, "scv", pa)
        for r in range(2):
            pg.dma(pg.sp, shv[:, r, :], S.MOD[li, r, 0:D].rearrange("(c p) -> p c", p=128), reads=[S.MOD], writes=[shv],
                   allow_slow_non_contiguous=True)
            pg.dma(pg.sp, scv[:, r, :], S.MOD[li, r, D:2 * D].rearrange("(c p) -> p c", p=128), reads=[S.MOD], writes=[scv],
                   allow_slow_non_contiguous=True)
        pg.V(lambda e: e.tensor_scalar(out=scv[:], in0=scv[:], scalar1=1.0, scalar2=None, op0=ALU.add), reads=[scv], writes=[scv])
        xts = [pg.sb((128, D), F32, f"a_x{i}", pa) for i in range(2)]
        xns = [pg.sb((128, D), F32, f"a_xn{i}", pa) for i in range(2)]
        xmT = [pg.sb((128, 8, 128), BF16, f"a_xmT{i}", pa) for i in range(2)]
        uts = [pg.sb((128, DIN), F32, f"a_u{i}", pa) for i in range(2)]
        lnb = [ln_bufs(pg, pa, f"a{i}") for i in range(2)]
        psT = [pg.ps((128, 512), F32, f"a_psT{i}", pa) for i in range(2)]
        psU = [pg.ps((128, 512), F32, f"a_psU{i}", pa) for i in range(4)]
        nu = 0
        for tt in range(NT):
            i2 = tt % 2
            r = 1 if tt < 2 else 0
            xt, xn, xm, ut = xts[i2], xns[i2], xmT[i2], uts[i2]
            srcb, srcap = tile_src(tt)
            pg.dma(pg.sp, xt[:], srcap, reads=[srcb], writes=[xt])
            rstd, nmr = ln_stats(pg, xt, lnb[i2])
            pg.A(lambda e, xt=xt, xn=xn, rstd=rstd, nmr=nmr: e.activation(out=xn[:], in_=xt[:], func=AF.Identity,
                                                                         bias=nmr[:], scale=rstd[:]),
                 reads=[xt, rstd, nmr], writes=[xn])
            for hb in range(2):
                pT = psT[hb]
                for c4 in range(4):
                    ch = hb * 4 + c4
                    pg.P(lambda e, pT=pT, c4=c4, ch=ch, xn=xn: e.transpose(out=pT[:, c4 * 128:(c4 + 1) * 128],
                                                                          in_=xn[:, ch * 128:(ch + 1) * 128], identity=ident[:]),
                         reads=[xn, ident], writes=[pT])
                for c4 in range(4):
                    ch = hb * 4 + c4
                    pg.A(lambda e, pT=pT, c4=c4, ch=ch, xm=xm, r=r: e.activation(
                        out=xm[:, ch, :], in_=pT[:, c4 * 128:(c4 + 1) * 128], func=AF.Identity,
                        bias=shv[:, r, ch:ch + 1], scale=scv[:, r, ch:ch + 1]),
                         reads=[pT, shv, scv], writes=[xm])
            for j in range(9):
                c0 = j * 512
                cw = min(512, DIN - c0)
                pu = psU[nu % 4]
                nu += 1
                for ch in range(8):
                    pg.P(lambda e, pu=pu, ch=ch, c0=c0, cw=cw, xm=xm: e.matmul(pu[:, 0:cw], lhsT=xm[:, ch, :],
                                                                              rhs=wsb[:, ch, c0:c0 + cw],
                                                                              start=(ch == 0), stop=(ch == 7)),
                         reads=[xm, wsb], writes=[pu])
                pg.V(lambda e, pu=pu, c0=c0, cw=cw, ut=ut: e.tensor_copy(out=ut[:, c0:c0 + cw], in_=pu[:, 0:cw]),
                     reads=[pu], writes=[ut])
            pg.dma(pg.pool, S.U[tt * 128:(tt + 1) * 128, :], ut[:], reads=[ut], writes=[S.U])
        pg.barrier()


def ln_bufs(pg, st, tag):
    b = NS()
    b.stp = pg.sb((128, 2, 6), F32, f"ln_st{tag}", st)
    b.mv = pg.sb((128, 2), F32, f"ln_mv{tag}", st)
    b.rstd = pg.sb((128, 1), F32, f"ln_rs{tag}", st)
    b.nmr = pg.sb((128, 1), F32, f"ln_nm{tag}", st)
    return b


def ln_stats(pg, xt, b):
    for h in range(2):
        pg.V(lambda e, h=h: e.bn_stats(out=b.stp[:, h, :], in_=xt[:, h * 512:(h + 1) * 512]), reads=[xt], writes=[b.stp])
    pg.V(lambda e: e.bn_aggr(out=b.mv[:], in_=b.stp[:].rearrange("p a b -> p (a b)")), reads=[b.stp], writes=[b.mv])
    pg.A(lambda e: e.activation(out=b.rstd[:], in_=b.mv[:, 1:2], func=AF.Sqrt, bias=EPS, scale=1.0), reads=[b.mv], writes=[b.rstd])
    pg.V(lambda e: e.reciprocal(out=b.rstd[:], in_=b.rstd[:]), reads=[b.rstd], writes=[b.rstd])
    pg.V(lambda e: e.scalar_tensor_tensor(out=b.nmr[:], in0=b.mv[:, 0:1], scalar=-1.0, in1=b.rstd[:], op0=ALU.mult, op1=ALU.mult),
         reads=[b.mv, b.rstd], writes=[b.nmr])
    return b.rstd, b.nmr


def group_rows(ap2d, unit_is_gla, j):
    if j < 2:
        return ap2d[j * 128:(j + 1) * 128, :].rearrange("(g t) w -> t g w", t=C)
    jj = j - 2
    if not unit_is_gla:
        return ap2d[CTX + jj * 128:CTX + (jj + 1) * 128, :].rearrange("(g t) w -> t g w", t=C)
    return ap2d[CTX:, :].rearrange("(r c) w -> c r w", c=GRID_W)[jj].rearrange("(g t) w -> t g w", t=C)


def phase_b(S, li):
    pg = S.pg
    V, A, G, P = pg.V, pg.A, pg.G, pg.P
    ident = S.ident
    fwd_groups = list(range(NT))
    bwd_groups = [1, 0] + list(range(NT - 1, 1, -1))
    with ExitStack() as pb:
        pre = [pg.ps((128, 512), F32, f"b_pre{i}", pb) for i in range(2)]
        tok = [pg.ps((128, 512), F32, f"b_tok{i}", pb) for i in range(2)]
        obk = [pg.ps((128, 512), F32, f"b_o{i}", pb) for i in range(2)]
        dsb = [pg.ps((128, 512), F32, f"b_ds{i}", pb) for i in range(2)]
        NB = 3
        mk = lambda shape, dt, nm: [pg.sb(shape, dt, f"b_{nm}{i}", pb) for i in range(NB)]
        qg = mk((C, 2, 128), F32, "qg")
        zg = mk((C, 2, 128), F32, "zg")
        vg = mk((C, 2, 128), F32, "vg")
        glg = mk((C, 2, 17), F32, "glg")
        kg = mk((C, 2, 128), F32, "kg")
        la = mk((C, 2, 128), F32, "la")
        vb = mk((C, 2, 128), BF16, "vb")
        ET = mk((128, 2, C), F32, "ET")
        EiT = mk((128, 2, C), F32, "EiT")
        gm = mk((128, 2, 2), F32, "gm")
        Etk = mk((C, 2, 128), F32, "Etk")
        KdT = mk((128, 2, C), BF16, "KdT")
        QdT = mk((128, 2, C), BF16, "QdT")
        Kd = mk((C, 2, 128), BF16, "Kd")
        scTd = {"f": mk((C, 2, C), BF16, "scTf"), "b": mk((C, 2, C), BF16, "scTb")}
        glT = mk((17, 2, C), F32, "glT")
        osb = mk((C, 2, 128), F32, "osb")
        for b_ in scTd["f"] + scTd["b"]:
            G(lambda e, b_=b_: e.memset(b_[:], 0.0), writes=[b_])
        for b_ in glg:
            G(lambda e, b_=b_: e.memset(b_[:], 1.0), writes=[b_])
        Sst = pg.sb((128, 128), F32, "b_S", pb)
        Stmp = pg.sb((128, 128), F32, "b_Stmp", pb)
        Ssc = [pg.sb((128, 128), BF16, f"b_Ssc{i}", pb) for i in range(2)]
        omlr = pg.sb((C, 2, 128), F32, "b_omlr", pb)
        wgk = pg.sb((17, 256), F32, "b_wgk", pb)
        itc = [0]
        ndc = [0]
        for unit in getattr(S, 'units', range(8)):
            gla = unit >= 4
            h = unit % 4
            dk = 64 if gla else 128
            ocol = unit * 128
            for di, dkey in enumerate("fb"):
                fwd = di == 0
                tm = S.tm[dkey + ("g" if gla else "h")]
                mask = S.mask[dkey]
                endc = C - 1 if fwd else 0
                OD = S.OF if fwd else S.OB
                if gla:
                    pg.dma(pg.sp, wgk[0:16, :], S.w_gk2[li, di], reads=[S.w_gk2], writes=[wgk])
                    pg.dma(pg.sp, wgk[16:17, :], S.b_gk[li, di:di + 1, :], reads=[S.b_gk], writes=[wgk])
                else:
                    for g in range(2):
                        pg.dma(pg.sp, omlr[:, g, :], S.OML[di, li, h * 128:(h + 1) * 128].partition_broadcast(C),
                               reads=[S.OML], writes=[omlr])
                G(lambda e: e.memset(Sst[:], 0.0), writes=[Sst])
                def prep(j):
                    i2 = itc[0] % NB
                    ip = itc[0] % 2
                    itc[0] += 1
                    q_, z_, v_, gl_, k_, la_, vb_ = qg[i2], zg[i2], vg[i2], glg[i2], kg[i2], la[i2], vb[i2]
                    ET_, EiT_, gm_, Etk_, KdT_, QdT_, Kd_, scT_, glT_, osb_ = (ET[i2], EiT[i2], gm[i2], Etk[i2], KdT[i2],
                                                                             QdT[i2], Kd[i2], scTd[dkey][i2], glT[i2], osb[i2])
                    pre_, tok_ = pre[ip], tok[ip]
                    uview = lambda c0, w: group_rows(S.U[:, c0:c0 + w], gla, j)
                    if gla:
                        pg.dma(pg.sp, q_[:, :, 0:64], uview(O_GQ + h * 64, 64), reads=[S.U], writes=[q_])
                        pg.dma(pg.sp, k_[:, :, 0:64], uview(O_GK + h * 64, 64), reads=[S.U], writes=[k_])
                        pg.dma(pg.sp, v_[:], uview(O_GV + h * 128, 128), reads=[S.U], writes=[v_])
                        pg.dma(pg.sp, gl_[:, :, 0:16], uview(O_GLF if fwd else O_GLB, 16), reads=[S.U], writes=[gl_])
                    else:
                        pg.dma(pg.sp, q_[:], uview(O_HQ + h * 128, 128), reads=[S.U], writes=[q_])
                        pg.dma(pg.sp, z_[:], uview((O_HFF if fwd else O_HFB) + h * 128, 128), reads=[S.U], writes=[z_])
                        pg.dma(pg.sp, v_[:], uview(O_HI + h * 128, 128), reads=[S.U], writes=[v_])
                    if gla:
                        for g in range(2):
                            P(lambda e, g=g: e.transpose(out=tok_[0:17, 384 + g * C:384 + (g + 1) * C], in_=gl_[:, g, :],
                                                         identity=ident[0:C, 0:C]), reads=[gl_, ident], writes=[tok_])
                        V(lambda e: e.tensor_copy(out=glT_[:].rearrange("p g t -> p (g t)"), in_=tok_[0:17, 384:512]),
                          reads=[tok_], writes=[glT_])
                        for g in range(2):
                            P(lambda e, g=g: e.matmul(pre_[0:C, 260 + g * 64:260 + (g + 1) * 64], lhsT=glT_[:, g, :],
                                                      rhs=wgk[:, h * 64:(h + 1) * 64], start=True, stop=True),
                              reads=[glT_, wgk], writes=[pre_])
                        A(lambda e: e.activation(out=la_[:, :, 0:64], in_=pre_[0:C, 260:388].rearrange("p (g k) -> p g k", g=2),
                                                 func=AF.Exp, scale=-1.0), reads=[pre_], writes=[la_])
                        A(lambda e: e.activation(out=la_[:, :, 0:64], in_=la_[:, :, 0:64], func=AF.Ln, bias=1.0, scale=1.0),
                          reads=[la_], writes=[la_])
                    else:
                        A(lambda e: e.activation(out=la_[:], in_=z_[:], func=AF.Exp), reads=[z_], writes=[la_])
                        A(lambda e: e.activation(out=k_[:], in_=la_[:], func=AF.Ln, bias=1.0, scale=1.0), reads=[la_], writes=[k_])
                        A(lambda e: e.activation(out=k_[:], in_=k_[:], func=AF.Exp, scale=-1.0), reads=[k_], writes=[k_])
                        V(lambda e: e.tensor_tensor(out=k_[:], in0=k_[:], in1=omlr[:], op=ALU.mult), reads=[k_, omlr], writes=[k_])
                        A(lambda e: e.activation(out=la_[:], in_=k_[:], func=AF.Ln, bias=1.0, scale=-1.0), reads=[k_], writes=[la_])
                    G(lambda e: e.tensor_copy(out=vb_[:], in_=v_[:]), reads=[v_], writes=[vb_])
                    for g in range(2):
                        P(lambda e, g=g: e.matmul(pre_[0:dk, g * 66:(g + 1) * 66], lhsT=la_[:, g, 0:dk], rhs=tm[:],
                                                  start=True, stop=True), reads=[la_, tm], writes=[pre_])
                        P(lambda e, g=g: e.matmul(pre_[0:C, 132 + g * dk:132 + (g + 1) * dk], lhsT=tm[:, 0:C], rhs=la_[:, g, 0:dk],
                                                  start=True, stop=True), reads=[la_, tm], writes=[pre_])
                    for g in range(2):
                        P(lambda e, g=g: e.transpose(out=tok_[0:dk, g * C:(g + 1) * C], in_=k_[:, g, 0:dk],
                                                     identity=ident[0:C, 0:C]), reads=[k_, ident], writes=[tok_])
                        P(lambda e, g=g: e.transpose(out=tok_[0:dk, 128 + g * C:128 + (g + 1) * C], in_=q_[:, g, 0:dk],
                                                     identity=ident[0:C, 0:C]), reads=[q_, ident], writes=[tok_])
                    dtv = pre_[0:dk, 0:132].rearrange("p (g c) -> p g c", g=2)
                    A(lambda e: e.activation(out=ET_[0:dk], in_=dtv[:, :, 0:C], func=AF.Exp), reads=[pre_], writes=[ET_])
                    A(lambda e: e.activation(out=EiT_[0:dk], in_=dtv[:, :, 0:C], func=AF.Exp, scale=-1.0), reads=[pre_], writes=[EiT_])
                    A(lambda e: e.activation(out=gm_[0:dk, :, 0:1], in_=dtv[:, :, C:C + 1], func=AF.Exp), reads=[pre_], writes=[gm_])
                    V(lambda e: e.tensor_tensor(out=gm_[0:dk, :, 1:2], in0=gm_[0:dk, :, 0:1], in1=EiT_[0:dk, :, endc:endc + 1],
                                                op=ALU.mult), reads=[gm_, EiT_], writes=[gm_])
                    A(lambda e: e.activation(out=Etk_[:, :, 0:dk], in_=pre_[0:C, 132:132 + 2 * dk].rearrange("p (g k) -> p g k", g=2),
                                             func=AF.Exp), reads=[pre_], writes=[Etk_])
                    V(lambda e: e.tensor_tensor(out=KdT_[0:dk], in0=tok_[0:dk, 0:128].rearrange("p (g c) -> p g c", g=2),
                                                in1=ET_[0:dk], op=ALU.mult), reads=[tok_, ET_], writes=[KdT_])
                    qsc = 0.125 if gla else 1.0
                    V(lambda e: e.scalar_tensor_tensor(out=QdT_[0:dk], in0=tok_[0:dk, 128:256].rearrange("p (g c) -> p g c", g=2),
                                                       scalar=qsc, in1=EiT_[0:dk], op0=ALU.mult, op1=ALU.mult),
                      reads=[tok_, EiT_], writes=[QdT_])
                    V(lambda e: e.tensor_tensor(out=Kd_[:, :, 0:dk], in0=k_[:, :, 0:dk], in1=Etk_[:, :, 0:dk], op=ALU.mult),
                      reads=[k_, Etk_], writes=[Kd_])
                    sco = 256
                    for g in range(2):
                        P(lambda e, g=g: e.matmul(tok_[0:C, sco + g * C:sco + (g + 1) * C], lhsT=KdT_[0:dk, g, :],
                                                  rhs=QdT_[0:dk, g, :], start=True, stop=True),
                          reads=[KdT_, QdT_], writes=[tok_])
                    V(lambda e: e.copy_predicated(out=scT_[:], mask=mask[:].bitcast(U32),
                                                  data=tok_[0:C, sco:sco + 2 * C].rearrange("p (g c) -> p g c", g=2)),
                      reads=[tok_, mask], writes=[scT_])
                    return dict(j=j, q_=q_, v_=v_, k_=k_, la_=la_, vb_=vb_, ET_=ET_, EiT_=EiT_, gm_=gm_, Etk_=Etk_, KdT_=KdT_, QdT_=QdT_,
                                Kd_=Kd_, scT_=scT_, osb_=osb_, o_=obk[ip], pre_=pre_, tok_=tok_)

                def seq(cx):
                    j, vb_, EiT_, gm_, KdT_, QdT_, Kd_, scT_, osb_, o_ = (cx[k_] for k_ in ('j', 'vb_', 'EiT_', 'gm_', 'KdT_', 'QdT_', 'Kd_', 'scT_', 'osb_', 'o_'))
                    for g in ((0, 1) if fwd else (1, 0)):
                        ssc = Ssc[ndc[0] % 2]
                        dsp = dsb[ndc[0] % 2]
                        ndc[0] += 1
                        A(lambda e, g=g, ssc=ssc: e.activation(out=ssc[0:dk, :], in_=Sst[0:dk, :], func=AF.Identity,
                                                               scale=gm_[0:dk, g, 0:1]), reads=[Sst, gm_], writes=[ssc])
                        P(lambda e, g=g: e.matmul(o_[0:C, g * 128:(g + 1) * 128], lhsT=scT_[:, g, :], rhs=vb_[:, g, :],
                                                  start=True, stop=False), reads=[scT_, vb_], writes=[o_])
                        P(lambda e, g=g, ssc=ssc: e.matmul(o_[0:C, g * 128:(g + 1) * 128], lhsT=QdT_[0:dk, g, :], rhs=ssc[0:dk, :],
                                                           start=False, stop=True), reads=[QdT_, ssc], writes=[o_])
                        P(lambda e, g=g, dsp=dsp: e.matmul(dsp[0:dk, 0:128], lhsT=Kd_[:, g, 0:dk], rhs=vb_[:, g, :],
                                                           start=True, stop=True), reads=[Kd_, vb_], writes=[dsp])
                        V(lambda e, g=g: e.tensor_scalar(out=Stmp[0:dk, :], in0=Sst[0:dk, :], scalar1=gm_[0:dk, g, 1:2], scalar2=None,
                                                         op0=ALU.mult), reads=[Sst, gm_], writes=[Stmp])
                        V(lambda e, g=g, dsp=dsp: e.scalar_tensor_tensor(out=Sst[0:dk, :], in0=dsp[0:dk, 0:128],
                                                                         scalar=EiT_[0:dk, g, endc:endc + 1], in1=Stmp[0:dk, :],
                                                                         op0=ALU.mult, op1=ALU.add),
                          reads=[dsp, EiT_, Stmp], writes=[Sst])
                    A(lambda e: e.activation(out=osb_[:], in_=o_[0:C, 0:256].rearrange("p (g v) -> p g v", g=2), func=AF.Copy),
                      reads=[o_], writes=[osb_])
                    pg.dma(pg.pool, group_rows(OD[:, ocol:ocol + 128], gla, j), osb_[:], reads=[osb_], writes=[OD])

                order = fwd_groups if fwd else bwd_groups
                if getattr(S, 'dbgfn', None) is not None:
                    cx = prep(order[0])
                    seq(cx)
                    S.dbgfn(dict(k=cx['k_'], la=cx['la_'], ET=cx['ET_'], EiT=cx['EiT_'], gm=cx['gm_'], Etk=cx['Etk_'], KdT=cx['KdT_'], QdT=cx['QdT_'],
                                 Kd=cx['Kd_'], scT=cx['scT_'], osb=cx['osb_'], S=Sst, vb=cx['vb_'], q=cx['q_'], v=cx['v_'], tm=tm, pre=cx['pre_'], tok=cx['tok_']), pb)
                    pg.barrier()
                    return
                nxt = prep(order[0])
                for oi in range(len(order)):
                    cur = nxt
                    if oi + 1 < len(order):
                        nxt = prep(order[oi + 1])
                    seq(cur)
        pg.barrier()


def phase_c(S, li):
    pg = S.pg
    V, A, G, P = pg.V, pg.A, pg.G, pg.P
    ident = S.ident
    last = li == DEPTH - 1
    tiles = getattr(S, "ctiles", None) or list(range(2 if last else 0, NT))

    def x_src(tt):
        if li == 0:
            if tt < 2:
                return S.ctx_in, S.ctx_in[tt * 128:(tt + 1) * 128, :]
            return S.x_in, S.x_in[(tt - 2) * 128:(tt - 1) * 128, :]
        return S.X, S.X[tt * 128:(tt + 1) * 128, :]

    with ExitStack() as pc:
        pT = [pg.ps((128, 512), F32, f"c_pT{i}", pc) for i in range(2)]
        py = [pg.ps((128, 512), F32, f"c_py{i}", pc) for i in range(2)]
        pss = [pg.ps((128, 512), F32, f"c_ps{i}", pc) for i in range(4)]
        qbanks = pT + py
        wo = pg.sb((128, 8, D), BF16, "c_wo", pc)
        for hh in range(2):
            pg.dma(pg.pool, wo[:, :, hh * 512:(hh + 1) * 512],
                   S.w_out[li, :, hh * 512:(hh + 1) * 512].rearrange("(c p) n -> p c n", p=128), reads=[S.w_out], writes=[wo])
        wq = pg.sb((128, 8, 2048), BF16, "c_wq", pc)
        for hh in range(2):
            pg.dma(pg.pool, wq[:, :, hh * 1024:(hh + 1) * 1024],
                   S.pwq[li, :, hh * 1024:(hh + 1) * 1024].rearrange("(c p) n -> p c n", p=128), reads=[S.pwq], writes=[wq])
        ktmp = pg.sb((128, 2, 128), F32, "c_ktmp", pc)
        keysT = pg.sb((128, 2, 128), F32, "c_keysT", pc)
        for hf in range(2):
            pg.dma(pg.sp, ktmp[:, hf, :], S.psk[li, hf], reads=[S.psk], writes=[ktmp])
        for hf in range(2):
            P(lambda e, hf=hf: e.transpose(out=pT[0][:, hf * 128:(hf + 1) * 128], in_=ktmp[:, hf, :], identity=ident[:]),
              reads=[ktmp, ident], writes=[pT[0]])
        V(lambda e: e.tensor_copy(out=keysT[:].rearrange("p a b -> p (a b)"), in_=pT[0][:, 0:256]), reads=[pT[0]], writes=[keysT])
        bt = {n: pg.sb((128, D), F32, f"c_bt_{n}", pc) for n in ("gain", "g1", "gam1", "bet1", "sh2", "sc2", "g2", "gam2", "bet2")}
        for hh in range(4):
            pg.dma(pg.sp, bt["gain"][:, hh * 128:(hh + 1) * 128], S.hg_norm[li, :].partition_broadcast(128),
                   reads=[S.hg_norm], writes=[bt["gain"]])
            pg.dma(pg.sp, bt["gain"][:, 512 + hh * 128:512 + (hh + 1) * 128], S.gla_norm[li, :].partition_broadcast(128),
                   reads=[S.gla_norm], writes=[bt["gain"]])
        pg.dma(pg.sp, bt["gam1"][:], S.ln_gamma[li, 0, :].partition_broadcast(128), reads=[S.ln_gamma], writes=[bt["gam1"]])
        pg.dma(pg.sp, bt["bet1"][:], S.ln_beta[li, 0, :].partition_broadcast(128), reads=[S.ln_beta], writes=[bt["bet1"]])
        pg.dma(pg.sp, bt["gam2"][:], S.ln_gamma[li, 1, :].partition_broadcast(128), reads=[S.ln_gamma], writes=[bt["gam2"]])
        pg.dma(pg.sp, bt["bet2"][:], S.ln_beta[li, 1, :].partition_broadcast(128), reads=[S.ln_beta], writes=[bt["bet2"]])

        def load_mod(r):
            for n, k in (("g1", 2), ("sh2", 3), ("sc2", 4), ("g2", 5)):
                pg.dma(pg.sp, bt[n][:], S.MOD[li, r, k * D:(k + 1) * D].partition_broadcast(128), reads=[S.MOD], writes=[bt[n]])
            V(lambda e: e.tensor_scalar(out=bt["sc2"][:], in0=bt["sc2"][:], scalar1=1.0, scalar2=None, op0=ALU.add),
              reads=[bt["sc2"]], writes=[bt["sc2"]])

        iot = pg.sb((128, 8, 16, 16), BF16, "c_iot", pc)
        with ExitStack() as tmpst:
            ioti = pg.sb((128, 2048), I32, "c_ioti", tmpst)
            G(lambda e: e.iota(out=ioti[:], pattern=[[0, 128], [1, 16]], base=0, channel_multiplier=0), writes=[ioti])
            V(lambda e: e.tensor_copy(out=iot[:].rearrange("p a b c -> p (a b c)"), in_=ioti[:]), reads=[ioti], writes=[iot])
            pg.barrier()
        sbt = lambda n, shape=(128, D), dt=F32: pg.sb(shape, dt, "c_" + n, pc)
        of_t, x_t, x1, tb, acc = (sbt(n) for n in ("of", "x", "x1", "t", "acc"))
        ob_t = acc
        ug = tb
        junk = x1
        sgb = junk
        mixT = sbt("mixT", (128, 8, 128), BF16)
        tT = sbt("tT", (128, 8, 128), BF16)
        qT = sbt("qT", (128, 16, 128), F32)
        s_sb = sbt("s", (128, 16, 128), F32)
        cand = qT
        s2 = [sbt(f"s2{i}", (128, 256), F32) for i in range(2)]
        vv = sbt("vv", (128, 16, 16), F32)
        ii = sbt("ii", (128, 16, 16), U32)
        iif = sbt("iif", (128, 16, 16), F32)
        ts = sbt("ts", (128, 8, 16), F32)
        pos = sbt("pos", (128, 8, 16), U32)
        pa = sbt("pa", (128, 8, 16), U32)
        pbb = sbt("pb", (128, 8, 16), U32)
        paf = sbt("paf", (128, 8, 16), F32)
        pbf = sbt("pbf", (128, 8, 16), F32)
        i1g = sbt("i1g", (128, 8, 16), F32)
        i2g = sbt("i2g", (128, 8, 16), F32)
        idxf = sbt("idxf", (128, 128), F32)
        idxu = sbt("idxu", (128, 128), U32)
        ex = sbt("ex", (128, 8, 16), F32)
        zz = sbt("zz", (128, 8), F32)
        gw = sbt("gw", (128, 128), F32)
        actb = sbt("act", (128, 128), F32)
        g1b = sbt("gl1", (128, 128), F32)
        g2b = sbt("gl2", (128, 128), F32)
        wgt = sbt("wgt", (128, 128), F32)
        ssq = sbt("ssq", (128, 8), F32)
        lnb = ln_bufs(pg, pc, "c")
        GS = 4
        NG = 12
        gbuf = [sbt(f"gb{i}", (128, 2 * D), BF16) for i in range(NG)]
        dg = [sbt(f"dg{i}", (128, GS, 128), BF16) for i in range(2)]
        act2 = sbt("act2", (128, 2, 128), F32)
        junkb = sbt("junkb", (128, 512), BF16)
        ng = 0
        cur_r = None
        UV_tab = S.UVB[:, :]

        cand3 = qT[:].rearrange("p (h two) c -> p h (two c)", two=2)

        def top16(srcb, n, w, vals, idxs, k0, view=None):
            src3 = srcb if view is None else NSView(srcb, view)
            sc = s2[k0 % 2]
            V(lambda e: e.max(out=vals[:, n, 0:8], in_=src3[:, n, :]), reads=[srcb], writes=[vals])
            V(lambda e: e.max_index(out=idxs[:, n, 0:8], in_max=vals[:, n, 0:8], in_values=src3[:, n, :]),
              reads=[srcb, vals], writes=[idxs])
            V(lambda e: e.match_replace(out=sc[:, 0:w], in_to_replace=vals[:, n, 0:8], in_values=src3[:, n, :], imm_value=-1e30),
              reads=[srcb, vals], writes=[sc])
            V(lambda e: e.max(out=vals[:, n, 8:16], in_=sc[:, 0:w]), reads=[sc], writes=[vals])
            V(lambda e: e.max_index(out=idxs[:, n, 8:16], in_max=vals[:, n, 8:16], in_values=sc[:, 0:w]),
              reads=[sc, vals], writes=[idxs])

        for tt in tiles:
            r = 1 if tt < 2 else 0
            if r != cur_r:
                load_mod(r)
                cur_r = r
            rows = slice(tt * 128, (tt + 1) * 128)
            xsb, xsap = x_src(tt)
            pg.dma(pg.sp, of_t[:], S.OF[rows, :], reads=[S.OF], writes=[of_t])
            pg.dma(pg.sp, ob_t[:], S.OB[rows, :], reads=[S.OB], writes=[ob_t])
            pg.dma(pg.sp, x_t[:], xsap, reads=[xsb], writes=[x_t])
            pg.dma(pg.sp, ug[:, 0:512], S.U[rows, O_HG:O_HG + 512], reads=[S.U], writes=[ug])
            pg.dma(pg.sp, ug[:, 512:1024], S.U[rows, O_GG:O_GG + 512], reads=[S.U], writes=[ug])
            V(lambda e: e.tensor_tensor(out=of_t[:], in0=of_t[:], in1=ob_t[:], op=ALU.add), reads=[of_t, ob_t], writes=[of_t])
            A(lambda e: e.activation(out=junk[:], in_=of_t[:], func=AF.Square), reads=[of_t], writes=[junk])
            V(lambda e: e.tensor_reduce(out=ssq[:], in_=junk[:].rearrange("p (h d) -> p h d", h=8), axis=AX.X, op=ALU.add),
              reads=[junk], writes=[ssq])
            A(lambda e: e.activation(out=ssq[:], in_=ssq[:], func=AF.Sqrt, bias=EPS, scale=1.0 / 128.0), reads=[ssq], writes=[ssq])
            V(lambda e: e.reciprocal(out=ssq[:], in_=ssq[:]), reads=[ssq], writes=[ssq])
            V(lambda e: e.tensor_tensor(out=of_t[:].rearrange("p (h d) -> p h d", h=8), in0=of_t[:].rearrange("p (h d) -> p h d", h=8),
                                        in1=ssq[:].unsqueeze(2).to_broadcast([128, 8, 128]), op=ALU.mult),
              reads=[of_t, ssq], writes=[of_t])
            V(lambda e: e.tensor_tensor(out=of_t[:], in0=of_t[:], in1=bt["gain"][:], op=ALU.mult), reads=[of_t, bt["gain"]], writes=[of_t])
            A(lambda e: e.activation(out=sgb[:], in_=ug[:], func=AF.Sigmoid), reads=[ug], writes=[sgb])
            V(lambda e: e.tensor_tensor(out=ug[:], in0=ug[:], in1=sgb[:], op=ALU.mult), reads=[ug, sgb], writes=[ug])
            V(lambda e: e.tensor_tensor(out=of_t[:], in0=of_t[:], in1=ug[:], op=ALU.mult), reads=[of_t, ug], writes=[of_t])
            if getattr(S, "dbgc", None) is not None and tt == S.dbgc_tile:
                S.dbgc(dict(mix=of_t, ssq=ssq, gain=bt["gain"], ug=ug), pc)
            for hb in range(2):
                for c4 in range(4):
                    ch = hb * 4 + c4
                    P(lambda e, hb=hb, c4=c4, ch=ch: e.transpose(out=pT[hb][:, c4 * 128:(c4 + 1) * 128],
                                                                in_=of_t[:, ch * 128:(ch + 1) * 128], identity=ident[:]),
                      reads=[of_t, ident], writes=[pT[hb]])
                A(lambda e, hb=hb: e.activation(out=mixT[:, hb * 4:(hb + 1) * 4, :].rearrange("p a b -> p (a b)"), in_=pT[hb][:],
                                               func=AF.Copy), reads=[pT[hb]], writes=[mixT])
            for nb in range(2):
                for ch in range(8):
                    P(lambda e, nb=nb, ch=ch: e.matmul(py[nb][:], lhsT=mixT[:, ch, :], rhs=wo[:, ch, nb * 512:(nb + 1) * 512],
                                                      start=(ch == 0), stop=(ch == 7)), reads=[mixT, wo], writes=[py[nb]])
                V(lambda e, nb=nb: e.tensor_tensor(out=x1[:, nb * 512:(nb + 1) * 512], in0=py[nb][:],
                                                   in1=bt["g1"][:, nb * 512:(nb + 1) * 512], op=ALU.mult),
                  reads=[py[nb], bt["g1"]], writes=[x1])
            if getattr(S, "dbgc", None) is not None and tt == S.dbgc_tile:
                S.dbgc(dict(y1=x1, g1=bt["g1"], xt=x_t), pc)
            V(lambda e: e.scalar_tensor_tensor(out=x1[:], in0=x_t[:], scalar=ALPHA, in1=x1[:], op0=ALU.mult, op1=ALU.add),
              reads=[x_t, x1], writes=[x1])
            rstd, nmr = ln_stats(pg, x1, lnb)
            A(lambda e: e.activation(out=x1[:], in_=x1[:], func=AF.Identity, bias=nmr[:], scale=rstd[:]), reads=[x1, rstd, nmr], writes=[x1])
            V(lambda e: e.tensor_tensor(out=x1[:], in0=x1[:], in1=bt["gam1"][:], op=ALU.mult), reads=[x1, bt["gam1"]], writes=[x1])
            V(lambda e: e.tensor_tensor(out=x1[:], in0=x1[:], in1=bt["bet1"][:], op=ALU.add), reads=[x1, bt["bet1"]], writes=[x1])
            rstd, nmr = ln_stats(pg, x1, lnb)
            A(lambda e: e.activation(out=tb[:], in_=x1[:], func=AF.Identity, bias=nmr[:], scale=rstd[:]), reads=[x1, rstd, nmr], writes=[tb])
            V(lambda e: e.tensor_tensor(out=tb[:], in0=tb[:], in1=bt["sc2"][:], op=ALU.mult), reads=[tb, bt["sc2"]], writes=[tb])
            V(lambda e: e.tensor_tensor(out=tb[:], in0=tb[:], in1=bt["sh2"][:], op=ALU.add), reads=[tb, bt["sh2"]], writes=[tb])
            for hb in range(2):
                for c4 in range(4):
                    ch = hb * 4 + c4
                    P(lambda e, hb=hb, c4=c4, ch=ch: e.transpose(out=pT[hb][:, c4 * 128:(c4 + 1) * 128],
                                                                in_=tb[:, ch * 128:(ch + 1) * 128], identity=ident[:]),
                      reads=[tb, ident], writes=[pT[hb]])
                A(lambda e, hb=hb: e.activation(out=tT[:, hb * 4:(hb + 1) * 4, :].rearrange("p a b -> p (a b)"), in_=pT[hb][:],
                                               func=AF.Copy), reads=[pT[hb]], writes=[tT])
            for qb in range(4):
                bank = qbanks[qb]
                for p4 in range(4):
                    pq = qb * 4 + p4
                    for ch in range(8):
                        P(lambda e, bank=bank, p4=p4, pq=pq, ch=ch: e.matmul(bank[:, p4 * 128:(p4 + 1) * 128],
                                                                            lhsT=wq[:, ch, pq * 128:(pq + 1) * 128], rhs=tT[:, ch, :],
                                                                            start=(ch == 0), stop=(ch == 7)),
                          reads=[wq, tT], writes=[bank])
                eng = A if qb % 2 == 0 else V
                if qb % 2 == 0:
                    A(lambda e, bank=bank, qb=qb: e.activation(out=qT[:, qb * 4:(qb + 1) * 4, :].rearrange("p a b -> p (a b)"),
                                                               in_=bank[:], func=AF.Copy), reads=[bank], writes=[qT])
                else:
                    V(lambda e, bank=bank, qb=qb: e.tensor_copy(out=qT[:, qb * 4:(qb + 1) * 4, :].rearrange("p a b -> p (a b)"),
                                                                in_=bank[:]), reads=[bank], writes=[qT])
            for qb in range(4):
                bank = pss[qb]
                for p4 in range(4):
                    pq = qb * 4 + p4
                    P(lambda e, bank=bank, p4=p4, pq=pq: e.matmul(bank[:, p4 * 128:(p4 + 1) * 128], lhsT=qT[:, pq, :],
                                                                 rhs=keysT[:, pq % 2, :], start=True, stop=True),
                      reads=[qT, keysT], writes=[bank])
                if qb % 2 == 0:
                    A(lambda e, bank=bank, qb=qb: e.activation(out=s_sb[:, qb * 4:(qb + 1) * 4, :].rearrange("p a b -> p (a b)"),
                                                               in_=bank[:], func=AF.Copy), reads=[bank], writes=[s_sb])
                else:
                    V(lambda e, bank=bank, qb=qb: e.tensor_copy(out=s_sb[:, qb * 4:(qb + 1) * 4, :].rearrange("p a b -> p (a b)"),
                                                                in_=bank[:]), reads=[bank], writes=[s_sb])
            for pq in range(16):
                top16(s_sb, pq, 128, vv, ii, pq)
            for h in range(8):
                V(lambda e, h=h: e.tensor_tensor(out=cand3[:, h, :].rearrange("p (a b) -> p a b", a=16),
                                                 in0=vv[:, 2 * h, :].unsqueeze(2).to_broadcast([128, 16, 16]),
                                                 in1=vv[:, 2 * h + 1, :].unsqueeze(1).to_broadcast([128, 16, 16]), op=ALU.add),
                  reads=[vv], writes=[cand])
            for h in range(8):
                top16(cand, h, 256, ts, pos, h, view=cand3)
            V(lambda e: e.tensor_single_scalar(out=pa[:], in_=pos[:], scalar=4, op=ALU.logical_shift_right), reads=[pos], writes=[pa])
            V(lambda e: e.tensor_single_scalar(out=pbb[:], in_=pos[:], scalar=15, op=ALU.bitwise_and), reads=[pos], writes=[pbb])
            V(lambda e: e.tensor_copy(out=paf[:], in_=pa[:]), reads=[pa], writes=[paf])
            V(lambda e: e.tensor_copy(out=pbf[:], in_=pbb[:]), reads=[pbb], writes=[pbf])
            V(lambda e: e.tensor_copy(out=iif[:], in_=ii[:]), reads=[ii], writes=[iif])
            eq = s_sb[:].rearrange("p a (b c) -> p (a b c)", b=8)[:, 0:2048].rearrange("p (h k a) -> p h k a", h=8, k=16)
            prod = cand3.rearrange("p h (k a) -> p h k a", k=16)
            iif4 = iif[:].rearrange("p (h two) a -> p h two a", two=2)
            for half, (pxf, outg) in enumerate(((paf, i1g), (pbf, i2g))):
                V(lambda e, pxf=pxf: e.tensor_tensor(out=eq, in0=pxf[:].unsqueeze(3).to_broadcast([128, 8, 16, 16]), in1=iot[:],
                                                     op=ALU.is_equal), reads=[pxf, iot], writes=[s_sb])
                V(lambda e, half=half: e.tensor_tensor(out=prod, in0=eq,
                                                       in1=iif4[:, :, half, :].unsqueeze(2).to_broadcast([128, 8, 16, 16]), op=ALU.mult),
                  reads=[s_sb, iif], writes=[cand])
                V(lambda e, outg=outg: e.tensor_reduce(out=outg[:], in_=prod, axis=AX.X, op=ALU.add), reads=[cand], writes=[outg])
            V(lambda e: e.scalar_tensor_tensor(out=idxf[:].rearrange("p (h k) -> p h k", h=8), in0=i1g[:], scalar=128.0, in1=i2g[:],
                                               op0=ALU.mult, op1=ALU.add), reads=[i1g, i2g], writes=[idxf])
            if li > 0:
                V(lambda e: e.tensor_scalar(out=idxf[:], in0=idxf[:], scalar1=float(li * NEXP), scalar2=None, op0=ALU.add),
                  reads=[idxf], writes=[idxf])
            V(lambda e: e.tensor_copy(out=idxu[:], in_=idxf[:]), reads=[idxf], writes=[idxu])
            V(lambda e: e.tensor_tensor(out=ex[:], in0=ts[:], in1=ts[:, :, 0:1].to_broadcast([128, 8, 16]), op=ALU.subtract),
              reads=[ts], writes=[ex])
            A(lambda e: e.activation(out=ex[:], in_=ex[:], func=AF.Exp), reads=[ex], writes=[ex])
            V(lambda e: e.tensor_reduce(out=zz[:], in_=ex[:], axis=AX.X, op=ALU.add), reads=[ex], writes=[zz])
            V(lambda e: e.reciprocal(out=zz[:], in_=zz[:]), reads=[zz], writes=[zz])
            V(lambda e: e.tensor_tensor(out=gw[:].rearrange("p (h k) -> p h k", h=8), in0=ex[:],
                                        in1=zz[:].unsqueeze(2).to_broadcast([128, 8, 16]), op=ALU.mult), reads=[ex, zz], writes=[gw])
            for hb in range(2):
                A(lambda e, hb=hb: e.activation(out=pT[hb][:], in_=tb[:, hb * 512:(hb + 1) * 512], func=AF.Copy),
                  reads=[tb], writes=[pT[hb]])

            def finish_group(gi, bufs):
                sl = slice(gi * GS, (gi + 1) * GS)
                dgb = dg[gi % 2]
                V(lambda e: e.tensor_tensor(out=g2b[:, sl], in0=g2b[:, sl], in1=actb[:, sl], op=ALU.mult), reads=[g2b, actb], writes=[g2b])
                V(lambda e: e.tensor_tensor(out=wgt[:, sl], in0=g2b[:, sl], in1=gw[:, sl], op=ALU.mult), reads=[g2b, gw], writes=[wgt])
                V(lambda e: e.tensor_tensor(out=dgb[:], in0=wgt[:, sl].unsqueeze(2).to_broadcast([128, GS, 128]),
                                            in1=ident[:].unsqueeze(1).to_broadcast([128, GS, 128]), op=ALU.mult),
                  reads=[wgt, ident], writes=[dgb])
                for s_, gb in enumerate(bufs):
                    j = gi * GS + s_
                    for hb in range(2):
                        P(lambda e, gb=gb, s_=s_, hb=hb, j=j: e.matmul(py[hb][:], lhsT=dgb[:, s_, :],
                                                                      rhs=gb[:, D + hb * 512:D + (hb + 1) * 512],
                                                                      start=(j == 0), stop=(j == 127)),
                          reads=[gb, dgb], writes=[py[hb]])

            pend = None
            for gi in range(128 // GS):
                sl = slice(gi * GS, (gi + 1) * GS)
                bufs = []
                for s_ in range(GS):
                    j = gi * GS + s_
                    gb = gbuf[ng % NG]
                    ng += 1
                    bufs.append(gb)
                    pg.dma(pg.pool, gb[:], UV_tab, reads=[S.UVB, idxu], writes=[gb],
                           indirect=bass.IndirectOffsetOnAxis(ap=idxu[:, j:j + 1], axis=0))
                    for hb in range(2):
                        V(lambda e, gb=gb, j=j, hb=hb: e.scalar_tensor_tensor(out=junkb[:], in0=gb[:, hb * 512:(hb + 1) * 512],
                                                                             scalar=1.0, in1=pT[hb][:], op0=ALU.mult, op1=ALU.mult,
                                                                             accum_out=act2[:, hb, j:j + 1]),
                          reads=[gb, pT[hb]], writes=[junkb, act2])
                V(lambda e: e.tensor_tensor(out=actb[:, sl], in0=act2[:, 0, sl], in1=act2[:, 1, sl], op=ALU.add), reads=[act2], writes=[actb])
                V(lambda e: e.tensor_tensor(out=g1b[:, sl], in0=actb[:, sl], in1=actb[:, sl], op=ALU.mult), reads=[actb], writes=[g1b])
                V(lambda e: e.tensor_scalar(out=g1b[:, sl], in0=g1b[:, sl], scalar1=0.044715, scalar2=1.0, op0=ALU.mult, op1=ALU.add),
                  reads=[g1b], writes=[g1b])
                V(lambda e: e.tensor_tensor(out=g1b[:, sl], in0=g1b[:, sl], in1=actb[:, sl], op=ALU.mult), reads=[g1b, actb], writes=[g1b])
                A(lambda e: e.activation(out=g2b[:, sl], in_=g1b[:, sl], func=AF.Sigmoid, scale=1.5957691216057308), reads=[g1b], writes=[g2b])
                if pend is not None:
                    finish_group(*pend)
                pend = (gi, bufs)
            finish_group(*pend)
            for hb in range(2):
                V(lambda e, hb=hb: e.tensor_tensor(out=acc[:, hb * 512:(hb + 1) * 512], in0=py[hb][:],
                                                   in1=bt["g2"][:, hb * 512:(hb + 1) * 512], op=ALU.mult),
                  reads=[py[hb], bt["g2"]], writes=[acc])
            V(lambda e: e.scalar_tensor_tensor(out=acc[:], in0=x1[:], scalar=ALPHA, in1=acc[:], op0=ALU.mult, op1=ALU.add),
              reads=[x1, acc], writes=[acc])
            rstd, nmr = ln_stats(pg, acc, lnb)
            A(lambda e: e.activation(out=acc[:], in_=acc[:], func=AF.Identity, bias=nmr[:], scale=rstd[:]), reads=[acc, rstd, nmr], writes=[acc])
            V(lambda e: e.tensor_tensor(out=acc[:], in0=acc[:], in1=bt["gam2"][:], op=ALU.mult), reads=[acc, bt["gam2"]], writes=[acc])
            V(lambda e: e.tensor_tensor(out=acc[:], in0=acc[:], in1=bt["bet2"][:], op=ALU.add), reads=[acc, bt["bet2"]], writes=[acc])
            if last and S.mode == "full":
                pg.dma(pg.act, S.OUT[(tt - 2) * 128:(tt - 1) * 128, :], acc[:], reads=[acc], writes=[S.OUT])
            else:
                pg.dma(pg.act, S.X[rows, :], acc[:], reads=[acc], writes=[S.X])
            if getattr(S, "dbgc", None) is not None and tt == S.dbgc_tile:
                S.dbgc(dict(x1=x1, t=tb, s=None, vv=vv, ii=ii, ts=ts, pos=pos, idxf=idxf, gw=gw, act=actb, wgt=wgt), pc)
        pg.barrier()

def _run(nc, in_maps, ncores):
    return run_bass_kernel_spmd(nc, in_maps, core_ids=list(range(ncores)))


def make_in_maps(inputs, batches, big=True):
    f = lambda a: np.ascontiguousarray(np.asarray(a, dtype=np.float32))
    names = ["w_ada", "b_ada", "w_in", "w_gk2", "b_gk", "hg_lower_bounds", "hg_norm", "gla_norm",
             "w_out", "ln_gamma", "ln_beta", "peer_w_query", "peer_sub_keys"] + (["peer_u", "peer_v"] if big else [])
    shared = {k: f(inputs[k]) for k in names}
    maps = []
    for b in batches:
        m = dict(shared)
        m["x"] = f(inputs["x"][b])
        m["ctx"] = f(inputs["ctx"][b])
        m["c"] = f(inputs["c"][b:b + 1])
        m["c_ctx"] = f(np.asarray(inputs["c_ctx"]).reshape(1, D))
        maps.append(m)
    return maps


def kernel(**inputs):
    nc = build_nc("full")
    maps = make_in_maps(inputs, range(4))
    res = _run(nc, maps, 4)
    return np.stack([r["out"] for r in res.results], axis=0)
```
